# Optimizing a Trainium2 kernel written in Bass

```python
import math
import jax, jax.numpy as jnp
from jax import lax
import numpy as np

D_MODEL = 1024
BATCH = 4
SEQ = 4096
DEPTH = 2
DEC_BATCH = 8
DEC_SEQ = 2048
PAST_LEN = 128

HEAD_DIM = 64
DIFF_HEADS = 4
DIFF_WIDTH = DIFF_HEADS * 2 * HEAD_DIM
RWKV_HEADS = 8
RWKV_WIDTH = RWKV_HEADS * HEAD_DIM
DECAY_LORA = 64
AAA_LORA = 64
GATE_LORA = 160
RWKV_GN_EPS = 64e-5
DIL_PAIRS = ((128, 1), (512, 4), (2048, 16))
DIL_HEADS_PER_GROUP = 4
DIL_WIDTH = len(DIL_PAIRS) * DIL_HEADS_PER_GROUP * HEAD_DIM
DIL_BLOCK = 128
Q_BLOCK = 128
FFN_HIDDEN = -(-8 * D_MODEL // (3 * 256)) * 256
ROPE_THETA = 10000.0
NORM_EPS = 1e-6
SUBLN_EPS = 1e-5
NEG_INF = -1e30

kernel_name = "hybrid_diffattn_rwkv7_dilated_encoder"


def rms_norm(x, w, eps=NORM_EPS):
    xf = x.astype(jnp.float32)
    y = xf * lax.rsqrt(jnp.mean(xf * xf, axis=-1, keepdims=True) + eps)
    return (y * w.astype(jnp.float32)).astype(x.dtype)


def rope(x):
    S, d = x.shape[1], x.shape[-1]
    half = d // 2
    inv = ROPE_THETA ** (-jnp.arange(half, dtype=jnp.float32) / half)
    ang = jnp.arange(S, dtype=jnp.float32)[:, None] * inv[None, :]
    shape = (1, S) + (1,) * (x.ndim - 3) + (half,)
    cos = jnp.cos(ang).reshape(shape)
    sin = jnp.sin(ang).reshape(shape)
    xf = x.astype(jnp.float32)
    x1, x2 = xf[..., :half], xf[..., half:]
    return jnp.concatenate([x1 * cos - x2 * sin, x2 * cos + x1 * sin], axis=-1).astype(x.dtype)


def centred_shift(x):
    z = jnp.zeros_like(x[:, :1])
    prev = jnp.concatenate([z, x[:, :-1]], axis=1)
    nxt = jnp.concatenate([x[:, 1:], z], axis=1)
    return 0.5 * (prev + nxt)


def swiglu(x, wg, wu, wd):
    return (jax.nn.silu(x @ wg) * (x @ wu)) @ wd


def diff_attention(q, k, v, lam, lambda_init, subln_w):
    B, S, H, _, d = q.shape
    nq = S // Q_BLOCK
    qb = (q * (d ** -0.5)).reshape(B, nq, Q_BLOCK, H, 2, d).transpose(1, 0, 2, 3, 4, 5)
    vf = v.astype(jnp.float32)

    def block(qi):
        s = jnp.einsum('bqhcd,bkhcd->bhcqk', qi, k, preferred_element_type=jnp.float32)
        p = jax.nn.softmax(s, axis=-1)
        pd = p[:, :, 0] - lam * p[:, :, 1]
        return jnp.einsum('bhqk,bkhe->bqhe', pd, vf)

    o = lax.map(block, qb)
    o = o.transpose(1, 0, 2, 3, 4).reshape(B, S, H, 2 * d)
    o = rms_norm(o, subln_w, SUBLN_EPS) * (1.0 - lambda_init)
    return o.astype(q.dtype)


def rwkv_scan(r, w, k, v, kk, a, reverse):
    B, S, H, N = r.shape
    xs = tuple(t.transpose(1, 0, 2, 3) for t in (r, w, k, v, kk, a))

    def step(st, inp):
        r_t, w_t, k_t, v_t, kk_t, a_t = inp
        sa = jnp.einsum('bhvk,bhk->bhv', st, -kk_t)
        st = (st * w_t[:, :, None, :] + sa[..., None] * (kk_t * a_t)[:, :, None, :]
              + v_t[..., None] * k_t[:, :, None, :])
        y = jnp.einsum('bhvk,bhk->bhv', st, r_t)
        return st, y

    s0 = jnp.zeros((B, H, N, N), jnp.float32)
    _, ys = lax.scan(step, s0, xs, reverse=reverse)
    return ys.transpose(1, 0, 2, 3)


def banded_attention(q, k, v, radius):
    N, L, d = q.shape
    Q = min(DIL_BLOCK, L)
    nblk = -(-L // Q)
    Lp = nblk * Q
    qp = jnp.pad(q, ((0, 0), (0, Lp - L), (0, 0))).reshape(N, nblk, Q, d)
    kpad = ((0, 0), (radius, Lp - L + radius), (0, 0))
    kp = jnp.pad(k, kpad)
    vp = jnp.pad(v, kpad)
    W = Q + 2 * radius
    idx = jnp.arange(nblk)[:, None] * Q + jnp.arange(W)[None, :]
    kw = kp[:, idx]
    vw = vp[:, idx].astype(jnp.float32)
    s = jnp.einsum('nbqd,nbkd->nbqk', qp, kw, preferred_element_type=jnp.float32)
    qpos = jnp.arange(nblk)[:, None] * Q + jnp.arange(Q)[None, :]
    kpos = idx - radius
    rel = kpos[:, None, :] - qpos[:, :, None]
    valid = (jnp.abs(rel) <= radius) & (kpos[:, None, :] >= 0) & (kpos[:, None, :] < L)
    s = jnp.where(valid[None], s, NEG_INF)
    lse = jax.nn.logsumexp(s, axis=-1)
    p = jnp.exp(s - lse[..., None])
    o = jnp.einsum('nbqk,nbkd->nbqd', p, vw)
    return o.reshape(N, Lp, d)[:, :L], lse.reshape(N, Lp)[:, :L]


def dilated_group(q, k, v, window, dilation):
    B, S, Hg, d = q.shape
    L = S // dilation
    radius = window // (2 * dilation)

    def fold(t):
        return t.reshape(B, L, dilation, Hg, d).transpose(0, 2, 3, 1, 4).reshape(B * dilation * Hg, L, d)

    o, lse = banded_attention(fold(q), fold(k), fold(v), radius)
    o = o.reshape(B, dilation, Hg, L, d).transpose(0, 3, 1, 2, 4).reshape(B, S, Hg, d)
    lse = lse.reshape(B, dilation, Hg, L).transpose(0, 3, 1, 2).reshape(B, S, Hg)
    return o, lse


def setup_inputs(seed: int = 0) -> dict:
    key = jax.random.key(seed)
    keys = iter(jax.random.split(key, 80))
    f32 = jnp.float32

    def nrm(shape, scale):
        return jax.random.normal(next(keys), shape, f32) * scale

    def gain(n):
        return 1.0 + nrm((n,), 0.05)

    def unif(shape, lo, hi):
        return jax.random.uniform(next(keys), shape, f32, lo, hi)

    D = D_MODEL
    p = {}
    p['x_prompt'] = nrm((BATCH, SEQ, D), 1.0)
    p['x_sample'] = nrm((DEC_BATCH, DEC_SEQ, D), 1.0)
    p['mix_pre0'] = gain(D)
    p['mix_post0'] = gain(D)
    p['w_in0'] = nrm((D, 3 * DIFF_WIDTH + 3 * RWKV_WIDTH), D ** -0.5)
    p['lam_q1'] = nrm((HEAD_DIM,), 0.1)
    p['lam_k1'] = nrm((HEAD_DIM,), 0.1)
    p['lam_q2'] = nrm((HEAD_DIM,), 0.1)
    p['lam_k2'] = nrm((HEAD_DIM,), 0.1)
    p['subln_w'] = gain(2 * HEAD_DIM)
    p['mu_r'] = unif((RWKV_WIDTH,), 0.0, 1.0)
    p['mu_k'] = unif((RWKV_WIDTH,), 0.0, 1.0)
    p['mu_v'] = unif((RWKV_WIDTH,), 0.0, 1.0)
    p['mu_w'] = unif((D,), 0.0, 1.0)
    p['mu_a'] = unif((D,), 0.0, 1.0)
    p['mu_g'] = unif((D,), 0.0, 1.0)
    for dname in ('f', 'b'):
        p['w0_' + dname] = unif((RWKV_WIDTH,), -6.0, -1.0)
        p['w1_' + dname] = nrm((D, DECAY_LORA), D ** -0.5)
        p['w2_' + dname] = nrm((DECAY_LORA, RWKV_WIDTH), 0.5 * DECAY_LORA ** -0.5)
    for dname in ('f', 'b'):
        p['a0_' + dname] = nrm((RWKV_WIDTH,), 0.1)
        p['a1_' + dname] = nrm((D, AAA_LORA), D ** -0.5)
        p['a2_' + dname] = nrm((AAA_LORA, RWKV_WIDTH), AAA_LORA ** -0.5)
    p['g1'] = nrm((D, GATE_LORA), D ** -0.5)
    p['g2'] = nrm((GATE_LORA, RWKV_WIDTH), GATE_LORA ** -0.5)
    p['k_k'] = 0.85 + nrm((RWKV_WIDTH,), 0.05)
    p['k_a'] = 1.0 + nrm((RWKV_WIDTH,), 0.05)
    p['r_k'] = nrm((RWKV_HEADS, HEAD_DIM), 0.1)
    p['lnx_w'] = gain(RWKV_WIDTH)
    p['lnx_b'] = nrm((RWKV_WIDTH,), 0.02)
    p['w_out0'] = nrm((DIFF_WIDTH + RWKV_WIDTH, D), (DIFF_WIDTH + RWKV_WIDTH) ** -0.5)
    p['ffn_pre0'] = gain(D)
    p['ffn_post0'] = gain(D)
    p['ffn_gate0'] = nrm((D, FFN_HIDDEN), D ** -0.5)
    p['ffn_up0'] = nrm((D, FFN_HIDDEN), D ** -0.5)
    p['ffn_down0'] = nrm((FFN_HIDDEN, D), FFN_HIDDEN ** -0.5)
    p['mix_pre1'] = gain(D)
    p['mix_post1'] = gain(D)
    p['w_in1'] = nrm((D, 3 * DIL_WIDTH), D ** -0.5)
    p['w_out1'] = nrm((DIL_WIDTH, D), DIL_WIDTH ** -0.5)
    p['ffn_pre1'] = gain(D)
    p['ffn_post1'] = gain(D)
    p['ffn_gate1'] = nrm((D, FFN_HIDDEN), D ** -0.5)
    p['ffn_up1'] = nrm((D, FFN_HIDDEN), D ** -0.5)
    p['ffn_down1'] = nrm((FFN_HIDDEN, D), FFN_HIDDEN ** -0.5)
    return p


def reference(x_prompt, x_sample, mix_pre0, mix_post0, w_in0, lam_q1, lam_k1, lam_q2, lam_k2, subln_w,
              mu_r, mu_k, mu_v, mu_w, mu_a, mu_g, w0_f, w1_f, w2_f, w0_b, w1_b, w2_b,
              a0_f, a1_f, a2_f, a0_b, a1_b, a2_b, g1, g2, k_k, k_a, r_k, lnx_w, lnx_b, w_out0,
              ffn_pre0, ffn_post0, ffn_gate0, ffn_up0, ffn_down0,
              mix_pre1, mix_post1, w_in1, w_out1, ffn_pre1, ffn_post1, ffn_gate1, ffn_up1, ffn_down1):
    f32 = jnp.float32

    def rwkv_mixer(xn, r_p, k_p, v_p):
        B, S, _ = xn.shape
        H, N = RWKV_HEADS, HEAD_DIM
        xx = centred_shift(xn) - xn
        xw = xn + xx * mu_w
        xa = xn + xx * mu_a
        xg = xn + xx * mu_g

        def shift_mix(t, mu):
            return (t + (centred_shift(t) - t) * mu).astype(f32)

        r = shift_mix(r_p, mu_r)
        k = shift_mix(k_p, mu_k)
        v = shift_mix(v_p, mu_v)

        def decay(w0, w1, w2):
            wl = -jax.nn.softplus(-(w0 + jnp.tanh(xw @ w1) @ w2).astype(f32)) - 0.5
            return jnp.exp(-jnp.exp(wl))

        def icl_rate(a0, a1, a2):
            return jax.nn.sigmoid((a0 + (xa @ a1) @ a2).astype(f32))

        g = (jax.nn.sigmoid(xg @ g1) @ g2).astype(f32)

        def heads(t):
            return t.reshape(B, S, H, N)

        kk = heads(k * k_k)
        kk = kk / jnp.maximum(jnp.sqrt(jnp.sum(kk * kk, axis=-1, keepdims=True)), 1e-12)
        rh, vh = heads(r), heads(v)
        a_f = icl_rate(a0_f, a1_f, a2_f)
        a_b = icl_rate(a0_b, a1_b, a2_b)
        k_f = k * (1.0 + (a_f - 1.0) * k_a)
        k_b = k * (1.0 + (a_b - 1.0) * k_a)
        y = (rwkv_scan(rh, heads(decay(w0_f, w1_f, w2_f)), heads(k_f), vh, kk, heads(a_f), False)
             + rwkv_scan(rh, heads(decay(w0_b, w1_b, w2_b)), heads(k_b), vh, kk, heads(a_b), True))
        mean = jnp.mean(y, axis=-1, keepdims=True)
        var = jnp.mean(jnp.square(y - mean), axis=-1, keepdims=True)
        y = ((y - mean) * lax.rsqrt(var + RWKV_GN_EPS)).reshape(B, S, H * N) * lnx_w + lnx_b
        bonus = jnp.sum(rh * heads(0.5 * (k_f + k_b)) * r_k, axis=-1, keepdims=True) * vh
        return ((y + bonus.reshape(B, S, H * N)) * g).astype(xn.dtype)

    def even_mixer(xn, layer):
        B, S, _ = xn.shape
        proj = xn @ w_in0
        qa, ka, va, rb, kb, vb = jnp.split(proj, 6, axis=-1)
        qa = rope(qa.reshape(B, S, DIFF_HEADS, 2, HEAD_DIM))
        ka = rope(ka.reshape(B, S, DIFF_HEADS, 2, HEAD_DIM))
        va = va.reshape(B, S, DIFF_HEADS, 2 * HEAD_DIM)
        lambda_init = 0.8 - 0.6 * math.exp(-0.3 * layer)
        lam = (jnp.exp(jnp.sum(lam_q1.astype(f32) * lam_k1.astype(f32)))
               - jnp.exp(jnp.sum(lam_q2.astype(f32) * lam_k2.astype(f32))) + lambda_init)
        out_a = diff_attention(qa, ka, va, lam, lambda_init, subln_w).reshape(B, S, DIFF_WIDTH)
        out_b = rwkv_mixer(xn, rb, kb, vb)
        return jnp.concatenate([out_a.astype(xn.dtype), out_b], axis=-1) @ w_out0

    def odd_mixer(xn):
        B, S, _ = xn.shape
        G, Hg = len(DIL_PAIRS), DIL_HEADS_PER_GROUP
        q, k, v = jnp.split(xn @ w_in1, 3, axis=-1)
        q = rope(q.reshape(B, S, G * Hg, HEAD_DIM)) * (HEAD_DIM ** -0.5)
        k = rope(k.reshape(B, S, G * Hg, HEAD_DIM))
        v = v.reshape(B, S, G * Hg, HEAD_DIM)
        outs, lses = [], []
        for gi, (window, dilation) in enumerate(DIL_PAIRS):
            sl = slice(gi * Hg, (gi + 1) * Hg)
            o, lse = dilated_group(q[:, :, sl], k[:, :, sl], v[:, :, sl], window, dilation)
            outs.append(o)
            lses.append(lse)
        alpha = jax.nn.softmax(jnp.stack(lses, axis=0), axis=0)
        y = jnp.concatenate([outs[gi] * alpha[gi][..., None] for gi in range(G)], axis=2)
        return y.reshape(B, S, DIL_WIDTH).astype(xn.dtype) @ w_out1

    def run(x):
        for layer in range(DEPTH):
            if layer % 2 == 0:
                m = even_mixer(rms_norm(x, mix_pre0), layer)
                x = x + rms_norm(m, mix_post0)
                f = swiglu(rms_norm(x, ffn_pre0), ffn_gate0, ffn_up0, ffn_down0)
                x = x + rms_norm(f, ffn_post0)
            else:
                m = odd_mixer(rms_norm(x, mix_pre1))
                x = x + rms_norm(m, mix_post1)
                f = swiglu(rms_norm(x, ffn_pre1), ffn_gate1, ffn_up1, ffn_down1)
                x = x + rms_norm(f, ffn_post1)
        return x

    y_prompt = run(x_prompt)
    y_sample = run(x_sample)
    return (y_prompt, y_sample)
```

```python
import numpy as np
from contextlib import ExitStack
import concourse.bass as bass
import concourse.mybir as mybir

F32 = mybir.dt.float32
BF16 = mybir.dt.bfloat16
ALU = mybir.AluOpType
AF = mybir.ActivationFunctionType
AX = mybir.AxisListType


class Res:
    __slots__ = ("w", "r")

    def __init__(self):
        self.w = None
        self.r = {}


class View:
    __slots__ = ("tile", "ap", "key")

    def __init__(self, tile, ap, key):
        self.tile = tile
        self.ap = ap
        self.key = key


class _Keyed:
    def __init__(self, tile, key):
        self.tile = tile
        self.key = key

    def __getitem__(self, idx):
        return View(self.tile, self.tile.t[idx], self.key)


class Tile:
    def __init__(self, name, t):
        self.name = name
        self.t = t
        self.res = {None: Res()}

    def __getitem__(self, idx):
        return View(self, self.t[idx], None)

    def k(self, key):
        return _Keyed(self, key)

    def conflicts(self, key):
        if key is None:
            return list(self.res.values())
        if key not in self.res:
            self.res[key] = Res()
        return [self.res[None], self.res[key]]

    def get(self, key):
        if key not in self.res:
            self.res[key] = Res()
        return self.res[key]


NDMA = 56
SB_DEBUG = False
NSW = 32


class Builder:
    def __init__(self, nc):
        self.nc = nc
        self.eng = {"pe": nc.tensor, "dve": nc.vector, "act": nc.scalar, "pool": nc.gpsimd, "sp": nc.sync}
        self.sem = {e: nc.alloc_semaphore("sem_" + e) for e in ("pe", "dve", "act", "pool")}
        self.tick = {e: 0 for e in self.sem}
        self.dsem = [nc.alloc_semaphore("dsem%d" % i) for i in range(NDMA)]
        self.ssem = [nc.alloc_semaphore("ssem%d" % i) for i in range(NSW)]
        self.ndma = 0
        self.nsw = 0
        self.waited = {e: {} for e in self.eng}
        self.epoch = 0
        self.barA = nc.alloc_semaphore("barA")
        self.barB = nc.alloc_semaphore("barB")
        self.nwait = 0
        self.ninst = 0
        self.stack = None

    def phase(self):
        return _Phase(self)

    def sb(self, name, shape, dtype=F32):
        self.uid = getattr(self, "uid", 0) + 1
        name = "%s_u%d" % (name, self.uid)
        t = self.stack.enter_context(self.nc.sbuf_tensor(name, list(shape), dtype))
        self.sb_hi = max(getattr(self, "sb_hi", 0), self.nc.sbuf_base)
        if SB_DEBUG:
            print("SB", name, shape, "end", self.nc.sbuf_base)
        return Tile(name, t)

    def ps(self, name, shape, dtype=F32):
        self.uid = getattr(self, "uid", 0) + 1
        name = "%s_u%d" % (name, self.uid)
        t = self.stack.enter_context(self.nc.psum_tensor(name, list(shape), dtype))
        return Tile(name, t)

    def dram(self, name, shape, dtype, kind="Internal"):
        t = self.nc.dram_tensor(name, list(shape), dtype, kind=kind)
        return Tile(name, t.ap())

    def _wait(self, eng, tok):
        sem, val, owner = tok[0], tok[1], tok[2]
        if len(tok) > 3 and tok[3] < self.epoch:
            return
        if owner == eng and eng == "pe":
            return
        key = id(sem)
        w = self.waited[eng]
        if w.get(key, 0) >= val:
            return
        self.eng[eng].wait_ge(sem, val)
        w[key] = val
        self.nwait += 1

    def _deps(self, eng, reads, writes):
        for v in reads:
            for r in v.tile.conflicts(v.key):
                if r.w is not None:
                    self._wait(eng, r.w)
        for v in writes:
            for r in v.tile.conflicts(v.key):
                if r.w is not None:
                    self._wait(eng, r.w)
                for (sem, owner, ep), val in list(r.r.items()):
                    self._wait(eng, (sem, val, owner, ep))

    def _commit(self, tok, reads, writes):
        sem, val, owner = tok[0], tok[1], tok[2]
        for v in writes:
            if v.key is None:
                t = v.tile
                t.res = {None: t.res[None]}
            r = v.tile.get(v.key)
            r.w = tok
            r.r = {}
        for v in reads:
            r = v.tile.get(v.key)
            k = (sem, owner, tok[3])
            if r.r.get(k, 0) < val:
                r.r[k] = val

    def op(self, eng, fn, reads, writes):
        self._deps(eng, reads, writes)
        ins = fn()
        self.tick[eng] += 1
        ins.then_inc(self.sem[eng], 1)
        tok = (self.sem[eng], self.tick[eng], eng, self.epoch)
        self._commit(tok, reads, writes)
        self.ninst += 1
        return tok

    def _dslot(self, q):
        if q == "pool":
            i = self.nsw
            self.nsw += 1
            return self.ssem[i % NSW], i // NSW, "sdma%d" % (i % NSW)
        i = self.ndma
        self.ndma += 1
        return self.dsem[i % NDMA], i // NDMA, "dma%d" % (i % NDMA)

    def dma(self, q, out, in_, **kw):
        sem, gen, owner = self._dslot(q)
        if gen > 0:
            self._wait(q, (sem, 16 * gen, "dma"))
        self._deps(q, [in_], [out])
        ins = self.eng[q].dma_start(out=out.ap, in_=in_.ap, **kw)
        ins.then_inc(sem, 16)
        tok = (sem, 16 * (gen + 1), owner, self.epoch)
        self._commit(tok, [in_], [out])
        self.ninst += 1
        return tok

    def barrier(self):
        toks = [(self.sem[e], self.tick[e], e) for e in self.sem if self.tick[e] > 0]
        for (n, N, sems) in ((self.ndma, NDMA, self.dsem), (self.nsw, NSW, self.ssem)):
            for slot in range(min(n, N)):
                cnt = (n - 1 - slot) // N + 1
                toks.append((sems[slot], 16 * cnt, "dma"))
        for e in self.eng:
            for tok in toks:
                self._wait(e, tok)
        self.epoch += 1
        ep = self.epoch
        for e in self.eng:
            self.eng[e].sem_inc(self.barA, 1)
        for e in self.sem:
            self.eng[e].wait_ge(self.barA, 5 * ep)
            self.eng[e].sem_clear(self.sem[e])
            self.eng[e].sem_inc(self.barB, 1)
        for e in self.eng:
            self.eng[e].wait_ge(self.barB, 4 * ep)
        for e in self.sem:
            self.tick[e] = 0
        for e in self.eng:
            for s_ in self.sem.values():
                self.waited[e].pop(id(s_), None)

    def mm(self, out, lhsT, rhs, start=True, stop=True, **kw):
        return self.op("pe", lambda: self.nc.tensor.matmul(out.ap, lhsT.ap, rhs.ap, start=start, stop=stop, **kw),
                       [lhsT, rhs] + ([] if start else [out]), [out])

    def tr(self, out, in_, ident):
        return self.op("pe", lambda: self.nc.tensor.transpose(out.ap, in_.ap, ident.ap), [in_, ident], [out])

    def act(self, out, in_, func, bias=0.0, scale=1.0, accum=None):
        reads = [in_]
        kw = {}
        if isinstance(bias, View):
            reads.append(bias)
            kw["bias"] = bias.ap
        else:
            kw["bias"] = float(bias)
        if isinstance(scale, View):
            reads.append(scale)
            kw["scale"] = scale.ap
        else:
            kw["scale"] = float(scale)
        writes = [out]
        if accum is not None:
            writes.append(accum)
            kw["accum_out"] = accum.ap
        return self.op("act", lambda: self.nc.scalar.activation(out.ap, in_.ap, func, **kw), reads, writes)

    def tt(self, out, a, b, op, eng="dve", after=()):
        e = self.eng[eng]
        return self.op(eng, lambda: e.tensor_tensor(out.ap, a.ap, b.ap, op), [a, b] + list(after), [out])

    def ts(self, out, a, s1, op0, s2=None, op1=None, eng="dve", accum=None):
        e = self.eng[eng]
        reads = [a]
        x1 = s1.ap if isinstance(s1, View) else float(s1)
        if isinstance(s1, View):
            reads.append(s1)
        x2 = None
        if s2 is not None:
            x2 = s2.ap if isinstance(s2, View) else float(s2)
            if isinstance(s2, View):
                reads.append(s2)
        writes = [out]
        kw = {}
        if accum is not None:
            writes.append(accum)
            kw["accum_out"] = accum.ap
        if op1 is None:
            return self.op(eng, lambda: e.tensor_scalar(out.ap, a.ap, x1, None, op0, **kw), reads, writes)
        return self.op(eng, lambda: e.tensor_scalar(out.ap, a.ap, x1, x2, op0, op1, **kw), reads, writes)

    def stt(self, out, a, s, b, op0, op1, accum=None):
        reads = [a, b]
        x = s.ap if isinstance(s, View) else float(s)
        if isinstance(s, View):
            reads.append(s)
        writes = [out]
        kw = {}
        if accum is not None:
            writes.append(accum)
            kw["accum_out"] = accum.ap
        return self.op("dve", lambda: self.nc.vector.scalar_tensor_tensor(out.ap, a.ap, x, b.ap, op0, op1, **kw),
                       reads, writes)

    def copy(self, out, in_, eng="dve"):
        if eng == "act":
            return self.op("act", lambda: self.nc.scalar.copy(out.ap, in_.ap), [in_], [out])
        e = self.eng[eng]
        return self.op(eng, lambda: e.tensor_copy(out.ap, in_.ap), [in_], [out])

    def memset(self, out, val, eng="pool"):
        e = self.eng[eng]
        return self.op(eng, lambda: e.memset(out.ap, val), [], [out])

    def recip(self, out, in_):
        return self.op("dve", lambda: self.nc.vector.reciprocal(out.ap, in_.ap), [in_], [out])

    def reduce(self, out, in_, op=ALU.add, axis=AX.X):
        return self.op("dve", lambda: self.nc.vector.tensor_reduce(out.ap, in_.ap, axis, op), [in_], [out])


class _Phase:
    def __init__(self, b):
        self.b = b

    def __enter__(self):
        self.prev = self.b.stack
        self.st = ExitStack()
        self.st.__enter__()
        self.b.stack = self.st
        return self

    def __exit__(self, *a):
        self.b.barrier()
        self.b.stack = self.prev
        return self.st.__exit__(*a)
from concourse.bass_utils import run_bass_kernel_spmd
import math
T = 4096
D = 1024
FH = 2816
NHC = FH // 128
EPS = 1e-6


class K:
    def __init__(self, nc, ext_in=(), ext_out=()):
        self.nc = nc
        self.b = Builder(nc)
        self.ext_in = set(ext_in)
        self.ext_out = set(ext_out)
        self.d = {}

    def dram(self, name, shape, dtype, kind=None):
        if kind is None:
            kind = "ExternalInput" if name in self.ext_in else ("ExternalOutput" if name in self.ext_out else "Internal")
        t = self.b.dram(name, shape, dtype, kind=kind)
        self.d[name] = t
        return t


class _Cols:
    def __init__(self, tile, off):
        self.tile = tile
        self.off = off

    def __getitem__(self, idx):
        rows, cols = idx
        return View(self.tile, self.tile.t[rows, cols.start + self.off:cols.stop + self.off], self.off)


def V(tile, ap, key=None):
    return View(tile, ap, key)


def dma_nc(b, q, out, in_):
    return b.dma(q, out, in_, allow_slow_non_contiguous=True)


def prep_weights_ffn(k, L):
    b = k.b
    wg, wu, wd = k.d["ffn_gate%d" % L], k.d["ffn_up%d" % L], k.d["ffn_down%d" % L]
    wgu_b = k.dram("wgu_b%d" % L, [NHC, 128, 2, 8, 128], BF16)
    wd_b = k.dram("wd_b%d" % L, [NHC, 128, D], BF16)
    for hc in range(NHC):
        for j, w in enumerate((wg, wu)):
            src = V(w, w.t[:, hc * 128:(hc + 1) * 128].rearrange("(kc p) c -> p kc c", p=128))
            b.dma("pool", V(wgu_b, wgu_b.t[hc, :, j, :, :], hc), src)
    for hc in range(0, NHC, 2):
        src = V(wd, wd.t[hc * 128:(hc + 2) * 128, :].rearrange("(h p) c -> h p c", p=128))
        b.dma("pool", V(wd_b, wd_b.t[hc:hc + 2], hc), src)


def prep_weight_rows(k, name, nchunk, ncol):
    b = k.b
    w = k.d[name]
    wb = k.dram(name + "_b", [nchunk, 128, ncol], BF16)
    step = 2
    for c in range(0, nchunk, step):
        n = min(step, nchunk - c)
        src = V(w, w.t[c * 128:(c + n) * 128, :].rearrange("(h p) c -> h p c", p=128))
        b.dma("pool", V(wb, wb.t[c:c + n], c), src)
    return wb


def phase_post(k, L, C, mixT, wout_b, x_in, x_out, cst):
    b = k.b
    nc = k.nc
    wgu_b, wd_b = k.d["wgu_b%d" % L], k.d["wd_b%d" % L]
    x1d = k.dram("x1d%d" % L, [T, D], F32)
    with b.phase():
        ident = b.sb("ident", [128, 128], BF16)
        b.dma("pool", ident[:], cst["ident"])
        g_post = b.sb("g_post", [128, D])
        g_fpost = b.sb("g_fpost", [128, D])
        g_fpre = b.sb("g_fpre", [128, 8])
        b.dma("sp", g_post[:], V(k.d["mix_post%d" % L], k.d["mix_post%d" % L].t.partition_broadcast(128)))
        b.dma("sp", g_fpost[:], V(k.d["ffn_post%d" % L], k.d["ffn_post%d" % L].t.partition_broadcast(128)))
        dma_nc(b, "sp", g_fpre[:], V(k.d["ffn_pre%d" % L], k.d["ffn_pre%d" % L].t.rearrange("(kc p) -> p kc", p=128)))
        wout = b.sb("wout_sb%d" % L, [128, C, D], BF16)
        for c in range(C):
            b.dma("sp", wout[:, c, :], wout_b[c])
        wd = b.sb("wd", [128, NHC, D], BF16)
        for hc in range(NHC):
            b.dma("sp", wd.k(hc)[:, hc, :], wd_b.k(hc - hc % 2)[hc])
        hT = b.sb("hT", [128, NHC, 1024], BF16)
        xn2T = b.sb("xn2T", [128, 8, 1024], BF16)
        mixh = b.sb("mixh", [128, C, 512], BF16)
        wgu = [b.sb("wgu%d" % i, [128, 2, 8, 128], BF16) for i in range(2)]
        xin = [b.sb("xin%d" % i, [128, D]) for i in range(2)]
        x1r = [b.sb("x1r%d" % i, [128, D]) for i in range(2)]
        tmp = b.sb("tmp", [128, D])
        junk = b.sb("junk", [128, D], BF16)
        xs = b.sb("xs", [128, D], BF16)
        sg = [b.sb("sg%d" % i, [128, 512], BF16) for i in range(2)]
        st_t = b.sb("st", [128, 64])
        A = [b.ps("A%d" % i, [128, 1024]) for i in range(2)]
        G = [b.ps("G%d" % i, [128, 1024]) for i in range(2)]
        na = 0
        ng = 0
        nw = 0
        for stile in range(T // 1024):
            t0 = stile * 1024
            na0 = na

            def emit_outproj(s):
                if s % 4 == 0:
                    for c in range(C):
                        b.dma("sp", mixh[:, c, :], V(mixT, mixT.t[c * 128:(c + 1) * 128, t0 + (s // 4) * 512: t0 + (s // 4) * 512 + 512]))
                r0 = t0 + s * 128
                xi = xin[s % 2]
                b.dma("sp", xi[:], V(x_in, x_in.t[r0:r0 + 128, :]))
                acc = A[(na0 + s) % 2]
                for half in range(2):
                    for c in range(C):
                        b.mm(acc[:, half * 512:(half + 1) * 512], mixh[:, c, (s % 4) * 128:(s % 4) * 128 + 128],
                             wout[:, c, half * 512:(half + 1) * 512], start=(c == 0), stop=(c == C - 1))
            emit_outproj(0)
            for s in range(8):
                r0 = t0 + s * 128
                st = _Cols(st_t, (s % 2) * 16)
                xi = xin[s % 2]
                acc = A[(na0 + s) % 2]
                if s + 1 < 8 and (s + 1) % 4 != 0:
                    emit_outproj(s + 1)
                b.act(junk[:], acc[:], AF.Square, accum=st[:, 0:1])
                b.act(st[:, 1:2], st[:, 0:1], AF.Sqrt, bias=cst["eps"], scale=1.0 / D)
                b.recip(st[:, 2:3], st[:, 1:2])
                b.stt(tmp[:], acc[:], st[:, 2:3], g_post[:], ALU.mult, ALU.mult)
                b.tt(xi[:], tmp[:], xi[:], ALU.add)
                b.dma("sp", V(x1d, x1d.t[r0:r0 + 128, :], r0), xi[:])
                b.act(junk[:], xi[:], AF.Square, accum=st[:, 3:4])
                b.act(st[:, 4:5], st[:, 3:4], AF.Sqrt, bias=cst["eps"], scale=1.0 / D)
                b.recip(st[:, 5:6], st[:, 4:5])
                b.ts(xs[:], xi[:], st[:, 5:6], ALU.mult)
                gt = G[ng % 2]
                ng += 1
                gtb = gt.t[:, 0:512].bitcast(BF16)
                for kc in range(8):
                    b.tr(V(gt, gtb[:, kc * 128:(kc + 1) * 128]), xs[:, kc * 128:(kc + 1) * 128], ident[:])
                b.tt(xn2T[:, :, s * 128:(s + 1) * 128], V(gt, gtb.rearrange("p (kc t) -> p kc t", kc=8)),
                     V(g_fpre, g_fpre.t[:, :].unsqueeze(2).to_broadcast([128, 8, 128])), ALU.mult)
                if s + 1 < 8 and (s + 1) % 4 == 0:
                    emit_outproj(s + 1)
            na += 8
            for hc in range(NHC):
                w = wgu[nw % 2]
                nw += 1
                b.dma("sp", w[:], wgu_b.k(hc)[hc])
                for th in range(2):
                    gt = G[ng % 2]
                    ng += 1
                    for j in range(2):
                        for kc in range(8):
                            b.mm(gt[:, j * 512:(j + 1) * 512], w[:, j, kc, :], xn2T[:, kc, th * 512:(th + 1) * 512],
                                 start=(kc == 0), stop=(kc == 7))
                    sgt = sg[(ng) % 2]
                    b.act(sgt[:], gt[:, 0:512], AF.Silu)
                    b.tt(hT[:, hc, th * 512:(th + 1) * 512], gt[:, 512:1024], sgt[:], ALU.mult)
            for s in range(8):
                r0 = t0 + s * 128
                st = _Cols(st_t, 32 + (s % 2) * 16)
                xr = x1r[s % 2]
                b.dma("sp", xr[:], V(x1d, x1d.t[r0:r0 + 128, :], r0))
                acc = A[na % 2]
                na += 1
                for half in range(2):
                    for hc in range(NHC):
                        b.mm(acc[:, half * 512:(half + 1) * 512], hT[:, hc, s * 128:(s + 1) * 128],
                             wd.k(hc)[:, hc, half * 512:(half + 1) * 512], start=(hc == 0), stop=(hc == NHC - 1))
                b.act(junk[:], acc[:], AF.Square, accum=st[:, 6:7])
                b.act(st[:, 7:8], st[:, 6:7], AF.Sqrt, bias=cst["eps"], scale=1.0 / D)
                b.recip(st[:, 8:9], st[:, 7:8])
                b.stt(tmp[:], acc[:], st[:, 8:9], g_fpost[:], ALU.mult, ALU.mult)
                b.tt(xr[:], tmp[:], xr[:], ALU.add)
                b.dma("sp", V(x_out, x_out.t[r0:r0 + 128, :], r0), xr[:])


NEG = -30000.0


def norm_to_T(k, x_in, gain_name, xnT, colf, ident, tagp=""):
    for _ in norm_to_T_gen(k, x_in, gain_name, xnT, colf, ident):
        pass


def norm_to_T_gen(k, x_in, gain_name, xnT, colf, ident, NPS=4):
    b = k.b
    g = b.sb("gpre", [128, 8])
    dma_nc(b, "sp", g[:], V(k.d[gain_name], k.d[gain_name].t.rearrange("(kc p) -> p kc", p=128)))
    NB = 4
    xin = [b.sb("nx%d" % i, [128, D]) for i in range(NB)]
    xs = [b.sb("nxs%d" % i, [128, D], BF16) for i in range(NB)]
    junks = [b.sb("njunk%d" % i, [128, D], BF16) for i in range(2)]
    st_t = b.sb("nst", [128, 16 * NB])
    P = [b.ps("nP%d" % i, [128, 512]) for i in range(NPS)]
    for s in range(T // 128):
        r0 = s * 128
        st = _Cols(st_t, (s % NB) * 16)
        xi = xin[s % NB]
        junk = junks[s % 2]
        b.dma("sp", xi[:], V(x_in, x_in.t[r0:r0 + 128, :]))
        b.act(junk[:], xi[:], AF.Square, accum=st[:, 0:1])
        b.act(st[:, 1:2], st[:, 0:1], AF.Sqrt, bias=EPS, scale=1.0 / D)
        b.recip(st[:, 2:3], st[:, 1:2])
        b.ts(xs[s % NB][:], xi[:], st[:, 2:3], ALU.mult)
        pt = P[s % NPS]
        ptb = pt.t[:, 0:512].bitcast(BF16)
        for kc in range(8):
            b.tr(V(pt, ptb[:, kc * 128:(kc + 1) * 128]), xs[s % NB][:, kc * 128:(kc + 1) * 128], ident[:])
        c0 = colf(r0)
        b.tt(xnT.k(s // 4)[:, :, c0:c0 + 128], V(pt, ptb.rearrange("p (kc t) -> p kc t", kc=8)),
             V(g, g.t[:, :].unsqueeze(2).to_broadcast([128, 8, 128])), ALU.mult)
        yield


def phase_l1(k, x_in, mixT1, cst):
    b = k.b
    nc = k.nc
    win_b = k.d["w_in1_b"]
    PADR = 1024
    Vd = k.dram("Vd", [PADR + T + PADR, 12 * 65], BF16)
    Nd = [k.dram("Nd%d" % g, [T, 260], F32) for g in range(3)]
    DIL = (1, 4, 16)
    with b.phase():
        ident = b.sb("ident", [128, 128], BF16)
        b.dma("pool", ident[:], cst["ident"])
        perm = b.sb("perm", [128, 128], BF16)
        b.dma("pool", perm[:], cst["perm"])
        xnT = b.sb("xnT1", [128, 8, T], BF16)
        with b.phase():
            norm_to_T(k, x_in, "mix_pre1", xnT, lambda t: t, ident)
        cosT = b.sb("cosT", [128, T])
        sinT = b.sb("sinT", [128, T])
        b.dma("sp", cosT[:], cst["cos"])
        b.dma("sp", sinT[:], cst["sin"])
        with b.phase():
            wv = b.sb("wv", [128, 8, 768], BF16)
            for kc in range(8):
                b.dma("sp", wv[:, kc, :], V(win_b, win_b.t[kc, :, 1536:2304]))
            z = b.sb("zpad", [128, 12 * 65], BF16)
            b.memset(z[:], 0.0)
            for i in range(PADR // 128):
                b.dma("sp", V(Vd, Vd.t[i * 128:(i + 1) * 128, :], "p%d" % i), z[:])
                b.dma("sp", V(Vd, Vd.t[PADR + T + i * 128:PADR + T + (i + 1) * 128, :], "q%d" % i), z[:])
            va = [b.sb("va%d" % i, [128, 12, 65], BF16) for i in range(2)]
            for i in range(2):
                b.memset(va[i][:], 1.0)
            Pv = [b.ps("Pv%d" % i, [128, 1024]) for i in range(2)]
            for s in range(T // 128):
                p = Pv[s % 2]
                for (c0, c1) in ((0, 512), (512, 768)):
                    for kc in range(8):
                        b.mm(p[:, c0:c1], xnT[:, kc, s * 128:(s + 1) * 128], wv[:, kc, c0:c1], start=(kc == 0), stop=(kc == 7))
                b.copy(va[s % 2][:, :, 0:64], V(p, p.t[:, 0:768].rearrange("p (h e) -> p h e", e=64)), eng="act")
                b.dma("sp", V(Vd, Vd.t[PADR + s * 128:PADR + (s + 1) * 128, :], s),
                      V(va[s % 2], va[s % 2].t[:].rearrange("p h e -> p (h e)")))
        for g in range(3):
            dil = DIL[g]
            L = T // dil
            NQ = L // 128
            with b.phase():
                wqk = b.sb("wqk", [128, 8, 2, 256], BF16)
                for kc in range(8):
                    b.dma("sp", wqk[:, kc, 0, :], V(win_b, win_b.t[kc, :, g * 256:(g + 1) * 256]))
                    b.dma("sp", wqk[:, kc, 1, :], V(win_b, win_b.t[kc, :, 768 + g * 256:768 + (g + 1) * 256]))
                QT = b.sb("QT", [128, 2, dil, L], BF16)
                KT = b.sb("KT", [128, 2, dil, L + 128], BF16)
                b.memset(KT[:], 0.0)
                t1 = [b.sb("t1_%d" % i, [128, 512]) for i in range(2)]
                t2 = [b.sb("t2_%d" % i, [128, 512]) for i in range(2)]
                qbf = [b.sb("qbf_%d" % i, [128, 512], BF16) for i in range(2)]
                with b.phase():
                    PA = [b.ps("PA%d" % i, [128, 512]) for i in range(2)]
                    PB = [b.ps("PB%d" % i, [128, 512]) for i in range(2)]
                    n = 0
                    for j in range(T // 512):
                        for a in range(2):
                            for mm in range(2):
                                pa, pb = PA[n % 2], PB[n % 2]
                                for kc in range(8):
                                    b.mm(pa[:], wqk[:, kc, a, mm * 128:(mm + 1) * 128], xnT[:, kc, j * 512:(j + 1) * 512],
                                         start=(kc == 0), stop=(kc == 7))
                                b.copy(qbf[n % 2][:], pa[:], eng="act")
                                b.mm(pb[:], perm[:], qbf[n % 2][:])
                                b.tt(t1[n % 2][:], pa[:], cosT[:, j * 512:(j + 1) * 512], ALU.mult, after=[qbf[n % 2][:]])
                                b.tt(t2[n % 2][:], pb[:], sinT[:, j * 512:(j + 1) * 512], ALU.mult)
                                w = 512 // dil
                                if a == 0:
                                    dst = V(QT, QT.t[:, mm, :, j * w:(j + 1) * w])
                                else:
                                    dst = V(KT, KT.t[:, mm, :, 64 + j * w:64 + (j + 1) * w])
                                b.tt(dst, V(t1[n % 2], t1[n % 2].t[:].rearrange("p (jl r) -> p r jl", r=dil)),
                                     V(t2[n % 2], t2[n % 2].t[:].rearrange("p (jl r) -> p r jl", r=dil)), ALU.add, eng="pool")
                                n += 1
                with b.phase():
                    mk = b.sb("mk", [128, 2, 256], BF16)
                    mkx = b.sb("mkx", [128, 2, 256], BF16)
                    for i in range(2):
                        b.dma("pool", mk[:, i, :], cst["maskAB"])
                        b.dma("pool", mkx[:, i, :], cst["maskX"])
                    vt = [b.sb("vt%d" % i, [128, 4, 65], BF16) for i in range(3)]
                    PT = [b.sb("PT%d" % i, [128, 4, 256], BF16) for i in range(2)]
                    osb = [b.sb("osb%d" % i, [128, 260]) for i in range(2)]
                    S = [b.ps("S%d" % i, [128, 1024]) for i in range(2)]
                    ACC = [b.ps("ACC%d" % i, [128, 512]) for i in range(2)]
                    it = 0

                    def geom(kt):
                        q0 = max(kt - 1, 0) * 128
                        q1 = min(kt + 1, NQ) * 128
                        return q0, q1, q1 - q0, (0 if kt > 0 else 128)

                    def emit_scores(rho, kt, it_):
                        sp = S[it_ % 2]
                        q0, q1, nq, m0 = geom(kt)
                        msk = mkx if kt == NQ // 2 else mk
                        for hh in range(4):
                            mm, pb = hh // 2, (hh % 2) * 64
                            b.mm(sp[:, hh * 256:hh * 256 + nq], ident[:], msk[:, 0, m0:m0 + nq], start=True, stop=False,
                                 skip_group_check=True)
                            b.mm(sp[:, hh * 256:hh * 256 + nq], KT[pb:pb + 64, mm, rho, kt * 128:(kt + 1) * 128],
                                 QT[pb:pb + 64, mm, rho, q0:q1], start=False, stop=True, skip_group_check=True)
                    iters = [(rho, kt) for rho in range(dil) for kt in range(NQ + 1)]
                    emit_scores(*iters[0], 0)
                    for rho in range(dil):
                        for kt in range(NQ + 1):
                            v = vt[it % 3]
                            row0 = PADR + dil * (128 * kt - 64) + rho
                            b.dma("sp", v[:], V(Vd, Vd.t[row0:row0 + 127 * dil + 1:dil, g * 260:(g + 1) * 260].rearrange("p (h e) -> p h e", e=65)))
                            sp = S[it % 2]
                            pt = PT[it % 2]
                            q0, q1, nq, m0 = geom(kt)
                            if it + 1 < len(iters):
                                emit_scores(*iters[it + 1], it + 1)
                            b.act(pt[:, :, 0:nq], V(sp, sp.t[:].rearrange("p (h c) -> p h c", c=256)[:, :, 0:nq]), AF.Exp, scale=0.125)
                            if kt > 0:
                                acc = ACC[(kt - 1) % 2]
                                for hh in range(4):
                                    b.mm(acc[:, hh * 65:(hh + 1) * 65], pt[:, hh, 0:128], v[:, hh, :], start=False, stop=True,
                                         skip_group_check=True)
                                o = osb[(kt - 1) % 2]
                                b.copy(o[:], acc[:, 0:260])
                                tok0 = dil * 128 * (kt - 1) + rho
                                b.dma("sp", V(Nd[g], Nd[g].t[tok0:tok0 + 127 * dil + 1:dil, :], (rho, kt - 1)), o[:])
                            if kt < NQ:
                                acc = ACC[kt % 2]
                                c0 = nq - 128
                                for hh in range(4):
                                    b.mm(acc[:, hh * 65:(hh + 1) * 65], pt[:, hh, c0:c0 + 128], v[:, hh, :], start=(hh == 0), stop=False,
                                         skip_group_check=True)
                            it += 1
        with b.phase():
            nt = [b.sb("nt%d" % i, [128, 3, 4, 65]) for i in range(4)]
            zt = b.sb("zt", [128, 32])
            yb = [b.sb("yb%d" % i, [128, 768], BF16) for i in range(4)]
            yT = [b.sb("yT%d" % i, [128, 6, 512], BF16) for i in range(2)]
            PTt = [b.ps("PTt%d" % i, [128, 512]) for i in range(4)]
            for s in range(T // 128):
                n_ = nt[s % 4]
                for g in range(3):
                    b.dma("sp", n_[:, g, :, :], V(Nd[g], Nd[g].t[s * 128:(s + 1) * 128, :].rearrange("p (h e) -> p h e", e=65)))
                zc = _Cols(zt, (s % 4) * 8)
                b.tt(V(zt, zt.t[:, (s % 4) * 8:(s % 4) * 8 + 4], (s % 4) * 8), n_[:, 0, :, 64], n_[:, 1, :, 64], ALU.add)
                b.tt(V(zt, zt.t[:, (s % 4) * 8:(s % 4) * 8 + 4], (s % 4) * 8), zc[:, 0:4], n_[:, 2, :, 64], ALU.add)
                b.recip(zc[:, 4:8], zc[:, 0:4])
                rzb = zt.t[:, (s % 4) * 8 + 4:(s % 4) * 8 + 8].unsqueeze(1).unsqueeze(3).to_broadcast([128, 3, 4, 64])
                b.tt(V(yb[s % 4], yb[s % 4].t[:].rearrange("p (g h e) -> p g h e", g=3, h=4)), n_[:, :, :, 0:64],
                     V(zt, rzb, (s % 4) * 8), ALU.mult)
                pt = PTt[s % 4]
                ptb = pt.t[:, 0:512].bitcast(BF16)
                for c in range(6):
                    b.tr(V(pt, ptb[:, c * 128:(c + 1) * 128]), yb[s % 4][:, c * 128:(c + 1) * 128], ident[:])
                y_ = yT[(s // 4) % 2]
                b.copy(y_[:, :, (s % 4) * 128:(s % 4) * 128 + 128], V(pt, ptb[:, 0:768].rearrange("p (c t) -> p c t", c=6)), eng="act")
                if s % 4 == 3:
                    for c in range(6):
                        b.dma("sp", V(mixT1, mixT1.t[c * 128:(c + 1) * 128, (s - 3) * 128:(s + 1) * 128], (c, s)), y_[:, c, :])


def colf0(t):
    return t + 1 + (2 if t >= 2048 else 0)


XW = T + 4


def l0_norm(k, x_in, xnT, ident, cst, do_norm=True):
    b = k.b
    if do_norm:
        with b.phase():
            norm_to_T(k, x_in, "mix_pre0", xnT, colf0, ident)
    flag = b.sb("flag", [128, 4])
    b.dma("sp", flag[:], cst["flag"])
    b.memset(xnT[:, :, 0:1], 0.0)
    b.memset(xnT[:, :, XW - 1:XW], 0.0)
    b.ts(xnT[:, :, 2049:2050], xnT[:, :, 2051:2052], flag[:, 0:1], ALU.mult)
    b.ts(xnT[:, :, 2050:2051], xnT[:, :, 2048:2049], flag[:, 0:1], ALU.mult)
    return flag


def l0_qkv(k, xnT, cst, QTd, KTd, Vad, x_in=None, ident=None):
    b = k.b
    win_b = k.d["w_in0_b"]
    with b.phase():
        ngen = norm_to_T_gen(k, x_in, "mix_pre0", xnT, colf0, ident, NPS=2) if x_in is not None else None

        def norm_steps(n):
            if ngen is None:
                return
            for _ in range(n):
                try:
                    next(ngen)
                except StopIteration:
                    return
        cosT = b.sb("cosT", [128, T])
        sinT = b.sb("sinT", [128, T])
        b.dma("sp", cosT[:], cst["cos"])
        b.dma("sp", sinT[:], cst["sin"])
        wqk = b.sb("wqk0", [128, 8, 1024], BF16)
        wv = b.sb("wv0", [128, 8, 512], BF16)
        for kc in range(8):
            b.dma("sp", wqk[:, kc, :], V(win_b, win_b.t[kc, :, 0:1024]))
            b.dma("sp", wv[:, kc, :], V(win_b, win_b.t[kc, :, 1024:1536]))
        perm = b.sb("perm0", [128, 128], BF16)
        b.dma("pool", perm[:], cst["perm"])
        qbf = [b.sb("qbf0_%d" % i, [128, 512], BF16) for i in range(2)]
        t1 = [b.sb("t1_%d" % i, [128, 512]) for i in range(2)]
        t2 = [b.sb("t2_%d" % i, [128, 512]) for i in range(2)]
        qst = [b.sb("qst%d" % i, [128, 512], BF16) for i in range(3)]
        va = [b.sb("va0_%d" % i, [128, 4, 129], BF16) for i in range(2)]
        for i in range(2):
            b.memset(va[i][:], 1.0)
        PA = [b.ps("PA%d" % i, [128, 512]) for i in range(2)]
        PB = [b.ps("PB%d" % i, [128, 512]) for i in range(2)]
        PV = [b.ps("PV%d" % i, [128, 512]) for i in range(2)]
        nctr = [0]

        def qkv_tile(j):
            c0 = colf0(j * 512)
            for a in range(2):
                for m in range(4):
                    n = nctr[0]
                    nctr[0] += 1
                    pa, pb = PA[n % 2], PB[n % 2]
                    col = a * 512 + m * 128
                    for kc in range(8):
                        b.mm(pa[:], wqk[:, kc, col:col + 128], xnT.k(j)[:, kc, c0:c0 + 512], start=(kc == 0), stop=(kc == 7))
                    b.copy(qbf[n % 2][:], pa[:], eng="act")
                    b.mm(pb[:], perm[:], qbf[n % 2][:])
                    b.tt(t1[n % 2][:], pa[:], cosT[:, j * 512:(j + 1) * 512], ALU.mult, after=[qbf[n % 2][:]])
                    b.tt(t2[n % 2][:], pb[:], sinT[:, j * 512:(j + 1) * 512], ALU.mult)
                    q = qst[n % 3]
                    b.tt(q[:], t1[n % 2][:], t2[n % 2][:], ALU.add, eng="pool")
                    dst = QTd if a == 0 else KTd
                    b.dma("sp", V(dst, dst.t[m * 128:(m + 1) * 128, j * 512:(j + 1) * 512], (m, j)), q[:])
                    yield
            for s4 in range(4):
                s = j * 4 + s4
                p = PV[s % 2]
                for kc in range(8):
                    b.mm(p[:], xnT.k(j)[:, kc, c0 + s4 * 128:c0 + (s4 + 1) * 128], wv[:, kc, :], start=(kc == 0), stop=(kc == 7))
                b.copy(va[s % 2][:, :, 0:128], V(p, p.t[:].rearrange("p (h e) -> p h e", e=128)), eng="act")
                b.dma("sp", V(Vad, Vad.t[s * 128:(s + 1) * 128, :], s), V(va[s % 2], va[s % 2].t[:].rearrange("p h e -> p (h e)")))
                yield
        norm_steps(4)
        for j in range(T // 512):
            for i_, _ in enumerate(qkv_tile(j)):
                if i_ % 3 == 1:
                    norm_steps(1)
        norm_steps(T // 128)


def l0_diffattn(k, cst, QTd, KTd, Vad, mixT0, ident, flag, co_setup=None):
    b = k.b
    with b.phase():
        QT = b.sb("QT0", [128, 4, T], BF16)
        KT = b.sb("KT0", [128, 4, T], BF16)
        VA = b.sb("VA0", [128, 32, 4 * 129], BF16)
        for m in range(4):
            for hh in range(2):
                b.dma("sp", QT.k(m)[:, m, hh * 2048:(hh + 1) * 2048], V(QTd, QTd.t[m * 128:(m + 1) * 128, hh * 2048:(hh + 1) * 2048]))
                b.dma("sp", KT.k(m)[:, m, hh * 2048:(hh + 1) * 2048], V(KTd, KTd.t[m * 128:(m + 1) * 128, hh * 2048:(hh + 1) * 2048]))
        for s in range(32):
            b.dma("sp", VA.k(s)[:, s, :], V(Vad, Vad.t[s * 128:(s + 1) * 128, :]))
        lv = b.sb("lv", [128, 4, 64])
        for i, nm in enumerate(("lam_q1", "lam_k1", "lam_q2", "lam_k2")):
            b.dma("sp", lv[:, i, :], V(k.d[nm], k.d[nm].t.partition_broadcast(128)))
        ls = b.sb("ls", [128, 8])
        lj = b.sb("lj", [128, 64])
        b.tt(lj[:], lv[:, 0, :], lv[:, 1, :], ALU.mult)
        b.reduce(ls[:, 0:1], lj[:])
        b.tt(lj[:], lv[:, 2, :], lv[:, 3, :], ALU.mult)
        b.reduce(ls[:, 1:2], lj[:])
        b.act(ls[:, 2:4], ls[:, 0:2], AF.Exp)
        b.tt(ls[:, 4:5], ls[:, 2:3], ls[:, 3:4], ALU.subtract)
        b.ts(ls[:, 5:6], ls[:, 4:5], -1.0, ALU.mult, -0.2, ALU.add)
        sw = b.sb("sw", [128, 128])
        b.dma("sp", sw[:], V(k.d["subln_w"], k.d["subln_w"].t.partition_broadcast(128)))
        b.ts(sw[:], sw[:], 0.8, ALU.mult)
        PT = [b.sb("PT0_%d" % i, [128, 1024], BF16) for i in range(3)]
        o1 = [b.sb("o1_%d" % i, [128, 128]) for i in range(2)]
        ob = [b.sb("ob_%d" % i, [128, 128], BF16) for i in range(2)]
        oj = b.sb("oj", [128, 128])
        aT = [b.sb("aT%d" % i, [128, 512], BF16) for i in range(2)]
        st_t = b.sb("dst", [128, 64])
        S = [b.ps("S0_%d" % i, [128, 1024]) for i in range(2)]
        ACC = [b.ps("AC0_%d" % i, [128, 512]) for i in range(3)]
        TP = b.ps("TP0", [128, 512])
        it = 0
        nsub = 0
        cogens = co_setup(TP) if co_setup is not None else []

        def advance():
            for g_ in list(cogens):
                try:
                    next(g_)
                except StopIteration:
                    cogens.remove(g_)

        def emit_qk(h, qb, kt, it_):
            sp = S[it_ % 2]
            for c in range(2):
                b.mm(sp[:, c * 512:(c + 1) * 512], KT.k(h)[c * 64:(c + 1) * 64, h, kt * 128:(kt + 1) * 128],
                     QT.k(h)[c * 64:(c + 1) * 64, h, qb * 512:(qb + 1) * 512], start=True, stop=True)
        iters = [(h, qb, kt) for h in range(4) for qb in range(8) for kt in range(32)]
        emit_qk(*iters[0], 0)
        for h in range(4):
            for qb in range(8):
                for kt in range(32):
                    sp = S[it % 2]
                    pt = PT[it % 3]
                    if it + 1 < len(iters):
                        emit_qk(*iters[it + 1], it + 1)
                    cross = (kt < 16) != (qb < 4)
                    if cross:
                        b.act(pt[:], sp[:], AF.Exp, scale=0.125, bias=flag[:, 1:2])
                    else:
                        b.act(pt[:], sp[:], AF.Exp, scale=0.125)
                    for c in range(2):
                        for qs in range(4):
                            gi = c * 4 + qs
                            acc = ACC[gi // 3]
                            co = (gi % 3) * 129
                            b.mm(acc[:, co:co + 129], pt[:, c * 512 + qs * 128:c * 512 + (qs + 1) * 128], VA.k(kt)[:, kt, h * 129:(h + 1) * 129],
                                 start=(kt == 0 and gi % 3 == 0), stop=(kt == 31), skip_group_check=True)
                    it += 1
                    if it % CO_EVERY == 0:
                        advance()
                tp = TP
                tpb = tp.t[:, 0:256].bitcast(BF16)
                for qs in range(4):
                    st = _Cols(st_t, (nsub % 2) * 16)
                    a0 = ACC[qs // 3]
                    c0 = (qs % 3) * 129
                    a1 = ACC[(4 + qs) // 3]
                    c1 = ((4 + qs) % 3) * 129
                    b.recip(st[:, 0:1], a0[:, c0 + 128:c0 + 129])
                    b.recip(st[:, 1:2], a1[:, c1 + 128:c1 + 129])
                    b.tt(st[:, 2:3], st[:, 1:2], ls[:, 5:6], ALU.mult)
                    o = o1[nsub % 2]
                    b.ts(o[:], a0[:, c0:c0 + 128], st[:, 0:1], ALU.mult)
                    b.stt(o[:], a1[:, c1:c1 + 128], st[:, 2:3], o[:], ALU.mult, ALU.add)
                    b.act(oj[:], o[:], AF.Square, accum=st[:, 3:4])
                    b.act(st[:, 4:5], st[:, 3:4], AF.Sqrt, bias=1e-5, scale=1.0 / 128)
                    b.recip(st[:, 5:6], st[:, 4:5])
                    obf = ob[nsub % 2]
                    b.stt(obf[:], o[:], st[:, 5:6], sw[:], ALU.mult, ALU.mult)
                    b.tr(V(tp, tpb[:, qs * 128:(qs + 1) * 128]), obf[:], ident[:])
                    nsub += 1
                at = aT[(h * 8 + qb) % 2]
                b.copy(at[:], V(tp, tpb), eng="act")
                b.dma("sp", V(mixT0, mixT0.t[h * 128:(h + 1) * 128, qb * 512:(qb + 1) * 512], (h, qb)), at[:])
        while cogens:
            advance()


CDEC = 0.6065306597126334
NTL = 2
STAGGER = 0
CO_EVERY = 2


def l0_rwproj(k, xnT, cst, rwd):
    b = k.b
    d = k.d
    with b.phase():
        W1 = b.sb("W1", [128, 8, 1536], BF16)
        W2 = b.sb("W2", [128, 8, 1536], BF16)
        La = b.sb("La", [128, 8, 416], BF16)
        Lh = b.sb("Lh", [128, 8, 416], BF16)
        L2a = b.sb("L2a", [128, 512], BF16)
        L2b = b.sb("L2b", [128, 512], BF16)
        L2g = b.sb("L2g", [128, 512], BF16)
        L2g2 = b.sb("L2g2", [32, 512], BF16)
        b.dma("pool", L2a[0:64, :], d["w2_f"][:])
        b.dma("pool", L2a[64:128, :], d["w2_b"][:])
        b.dma("pool", L2b[0:64, :], d["a2_f"][:])
        b.dma("pool", L2b[64:128, :], d["a2_b"][:])
        b.dma("pool", L2g[:], V(d["g2"], d["g2"].t[0:128, :]))
        b.dma("pool", L2g2[:], V(d["g2"], d["g2"].t[128:160, :]))
        with b.phase():
            mub = b.sb("mub", [128, 1536])
            for i, nm in enumerate(("mu_r", "mu_k", "mu_v")):
                b.dma("sp", mub[:, i * 512:(i + 1) * 512], V(d[nm], d[nm].t.partition_broadcast(128)))
            omm = b.sb("omm", [128, 1536])
            hm = b.sb("hm", [128, 1536])
            b.ts(omm[:], mub[:], -1.0, ALU.mult, 1.0, ALU.add)
            b.ts(hm[:], mub[:], 0.5, ALU.mult)
            mus = b.sb("mus", [128, 3, 8])
            for i, nm in enumerate(("mu_w", "mu_a", "mu_g")):
                dma_nc(b, "sp", mus[:, i, :], V(d[nm], d[nm].t.rearrange("(kc p) -> p kc", p=128)))
            omm3 = b.sb("omm3", [128, 3, 8])
            hm3 = b.sb("hm3", [128, 3, 8])
            b.ts(omm3[:], mus[:], -1.0, ALU.mult, 1.0, ALU.add)
            b.ts(hm3[:], mus[:], 0.5, ALU.mult)
            wf = [b.sb("wf%d" % i, [128, 1536]) for i in range(2)]
            lf = [b.sb("lf%d" % i, [128, 416]) for i in range(2)]
            win = d["w_in0"]
            for kc in range(8):
                w = wf[kc % 2]
                b.dma("sp", w[:], V(win, win.t[kc * 128:(kc + 1) * 128, 1536:3072]))
                b.tt(W1[:, kc, :], w[:], omm[:], ALU.mult)
                b.tt(W2[:, kc, :], w[:], hm[:], ALU.mult, eng="pool")
                l = lf[kc % 2]
                for (nm, c0, cw) in (("w1_f", 0, 64), ("w1_b", 64, 64), ("a1_f", 128, 64), ("a1_b", 192, 64), ("g1", 256, 160)):
                    b.dma("sp", l[:, c0:c0 + cw], V(d[nm], d[nm].t[kc * 128:(kc + 1) * 128, :]))
                for gi, (c0, c1) in enumerate(((0, 128), (128, 256), (256, 416))):
                    b.ts(La[:, kc, c0:c1], l[:, c0:c1], omm3[:, gi, kc:kc + 1], ALU.mult)
                    b.ts(Lh[:, kc, c0:c1], l[:, c0:c1], hm3[:, gi, kc:kc + 1], ALU.mult)
        xsh = [b.sb("xsh%d" % i, [128, 8, 512], BF16) for i in range(2)]
        h1 = [[b.sb("h1_%d_%d" % (i, g), [128, 512], BF16) for g in range(4)] for i in range(2)]
        rw = [b.sb("rw%d" % i, [128, 8, 512]) for i in range(2)]
        PL = [b.ps("PL%d" % i, [128, 512]) for i in range(2)]
        PT_ = [b.ps("PTk%d" % i, [128, 512]) for i in range(4)]
        npl = 0
        npt = 0
        for j in range(T // 512):
            c0 = colf0(j * 512)
            xs = xsh[j % 2]
            b.tt(xs[:], xnT[:, :, c0 - 1:c0 + 511], xnT[:, :, c0 + 1:c0 + 513], ALU.add, eng="pool")
            hh = h1[j % 2]
            for gi, (r0, nr, fn) in enumerate(((0, 128, AF.Tanh), (128, 128, AF.Copy), (256, 128, AF.Sigmoid), (384, 32, AF.Sigmoid))):
                p = PL[npl % 2]
                npl += 1
                for kc in range(8):
                    b.mm(p[0:nr, :], La[:, kc, r0:r0 + nr], xnT[:, kc, c0:c0 + 512], start=(kc == 0), stop=False)
                for kc in range(8):
                    b.mm(p[0:nr, :], Lh[:, kc, r0:r0 + nr], xs[:, kc, :], start=False, stop=(kc == 7))
                if fn == AF.Copy:
                    b.copy(hh[gi][0:nr, :], p[0:nr, :], eng="act")
                else:
                    b.act(hh[gi][0:nr, :], p[0:nr, :], fn)
            for s4 in range(4):
                s = j * 4 + s4
                r = rw[s % 2]
                cs = c0 + s4 * 128
                ts_ = slice(s4 * 128, (s4 + 1) * 128)
                for q in range(8):
                    p = PT_[npt % 4]
                    npt += 1
                    if q < 3:
                        for kc in range(8):
                            b.mm(p[:], xnT[:, kc, cs:cs + 128], W1[:, kc, q * 512:(q + 1) * 512], start=(kc == 0), stop=False)
                        for kc in range(8):
                            b.mm(p[:], xs[:, kc, ts_], W2[:, kc, q * 512:(q + 1) * 512], start=False, stop=(kc == 7))
                    elif q == 3:
                        b.mm(p[:], hh[0][0:64, ts_], L2a[0:64, :])
                    elif q == 4:
                        b.mm(p[:], hh[0][64:128, ts_], L2a[64:128, :])
                    elif q == 5:
                        b.mm(p[:], hh[1][0:64, ts_], L2b[0:64, :])
                    elif q == 6:
                        b.mm(p[:], hh[1][64:128, ts_], L2b[64:128, :])
                    else:
                        b.mm(p[:], hh[2][:, ts_], L2g[:], start=True, stop=False)
                        b.mm(p[:], hh[3][0:32, ts_], L2g2[0:32, :], start=False, stop=True)
                    b.copy(r[:, q, :], p[:], eng=("act" if q % 2 == 0 else "dve"))
                b.dma("sp", V(rwd, rwd.t[s * 128:(s + 1) * 128, :, :], s), r[:])


def l0_rwkv(k, cst, rwd, mixT0, ident, Yd=None, flag=None, stage=9, do_post=True):
    b = k.b
    d = k.d
    if Yd is None:
        Yd = k.dram("Yd", [2, T, 512], F32)
    NT = T // 128
    with b.phase():
        def bc(nm):
            t = b.sb("bc_" + nm, [128, 512])
            src = d[nm].t
            if len(src.shape) == 2:
                src = src.rearrange("h n -> (h n)")
            b.dma("sp", t[:], V(d[nm], src.partition_broadcast(128)))
            return t
        w0 = [bc("w0_f"), bc("w0_b")]
        a0 = [bc("a0_f"), bc("a0_b")]
        kkb = bc("k_k")
        kab = bc("k_a")
        omka = b.sb("omka", [128, 512])
        b.ts(omka[:], kab[:], -1.0, ALU.mult, 1.0, ALU.add)
        tri = b.sb("tri", [128, 6, 128])
        b.dma("sp", tri[:], cst["tri"])
        irep = b.sb("irep", [128, 512])
        b.dma("sp", irep[:], cst["irep"])
        SU, IU, SL, IL = 0, 1, 2, 3
        M4 = []
        MQ = []
        for dr in range(2):
            s_, i_, sp_ = (SU, IU, SL) if dr == 0 else (SL, IL, SU)
            m4 = b.sb("M4_%d" % dr, [128, 4, 128], BF16)
            mq = b.sb("MQ_%d" % dr, [128, 4, 128], BF16)
            for j in range(4):
                b.copy(m4[:, j, :], tri[:, (s_ if j % 2 == 0 else i_), :], eng="pool")
                b.copy(mq[:, j, :], tri[:, sp_, :], eng="pool")
            M4.append(m4)
            MQ.append(mq)
        CS = [(IU, SU, SL), (IL, SL, SU)]
        GS = [[b.ps("Gp%d_%d" % (d_, i), [128, 512]) for i in range(3)] for d_ in range(2)]
        PYS = [b.ps("PY%d" % i, [128, 512]) for i in range(2)]
        gcnt = [0, 0]

        class DirState:
            pass
        DS = []
        for dr in range(2):
            s = DirState()
            s.rwb = [b.sb("rw%d_%d" % (i, dr), [128, 5, 512]) for i in range(2)]
            s.f = [b.sb("f%d_%d" % (i, dr), [128, 512]) for i in range(8)]
            s.st = b.sb("st_%d" % dr, [128, 32])
            s.h16 = {nm: b.sb("%s_%d" % (nm, dr), [128, 512], BF16) for nm in ("Rt", "Kt", "Bt", "Kp", "Kh", "Bh", "v16", "AV")}
            s.U = [b.sb("U%d_%d" % (c, dr), [128, 512], BF16) for c in range(2)]
            s.vz = [b.sb("vz%d_%d" % (c, dr), [128, 512], BF16) for c in range(2)]
            s.RTz = b.sb("RTz_%d" % dr, [128, 8, 128], BF16)
            for c in range(2):
                b.memset(s.U[c][:], 0.0)
                b.memset(s.vz[c][:], 0.0)
            b.memset(s.RTz[:], 0.0)
            s.Dg = [b.sb("Dg%d_%d" % (c, dr), [128, 512]) for c in range(2)]
            s.XT = b.sb("XT_%d" % dr, [128, 4, 4, 128], BF16)
            s.AM = b.sb("AM_%d" % dr, [128, 8, 4, 128], BF16)
            s.P = [b.sb("P%d_%d" % (i, dr), [128, 8, 128], BF16) for i in range(2)]
            s.PT = [b.sb("PT%d_%d" % (i, dr), [128, 8, 128], BF16) for i in range(2)]
            s.S = [b.sb("S%d_%d" % (i, dr), [128, 8, 128], BF16) for i in range(2)]
            s.WT = b.sb("WT_%d" % dr, [128, 8, 128], BF16)
            b.memset(s.WT[:], 0.0)
            s.H32 = [b.sb("H32_%d_%d" % (i, dr), [128, 4, 64]) for i in range(2)]
            s.H16 = [b.sb("H16_%d_%d" % (i, dr), [128, 4, 64], BF16) for i in range(2)]
            s.hi = 0
            s.Y = b.sb("Yt_%d" % dr, [128, 512])
            b.memset(s.H32[0][:], 0.0)
            b.memset(s.H16[0][:], 0.0)
            DS.append(s)

        def v3(view_tile, ap):
            return V(view_tile, ap.rearrange("p (h n) -> p h n", n=64))

        tcount = [0, 0]

        def load_rw(dr, ti, dst):
            rows = slice(ti * 128, (ti + 1) * 128)
            b.dma("sp", dst[:, 0:3, :], V(rwd, rwd.t[rows, 0:3, :]))
            b.dma("sp", dst[:, 3, :], V(rwd, rwd.t[rows, 3 + dr, :]))
            b.dma("sp", dst[:, 4, :], V(rwd, rwd.t[rows, 5 + dr, :]))

        def rw_tile(dr, ti):
            s = DS[dr]
            rw = s.rwb[tcount[dr] % 2]
            f = s.f
            h = s.h16
            st = s.st
            PY = PYS[dr]

            def gp():
                gcnt[dr] += 1
                return GS[dr][gcnt[dr] % 3]
            if tcount[dr] == 0:
                load_rw(dr, ti, rw)
            tn = ti + 1 if dr == 0 else ti - 1
            if 0 <= tn < NT:
                load_rw(dr, tn, s.rwb[(tcount[dr] + 1) % 2])
            tcount[dr] += 1
            yield
            r_, k_, v_ = rw[:, 0, :], rw[:, 1, :], rw[:, 2, :]
            b.tt(f[0][:], rw[:, 3, :], w0[dr][:], ALU.add)
            yield
            b.act(f[0][:], f[0][:], AF.Sigmoid)
            yield
            b.tt(f[1][:], rw[:, 4, :], a0[dr][:], ALU.add, eng="pool")
            yield
            b.act(f[1][:], f[1][:], AF.Sigmoid)
            yield
            b.tt(f[2][:], k_, kkb[:], ALU.mult)
            yield
            b.act(f[3][:], f[2][:], AF.Square)
            yield
            b.reduce(st[:, 0:8], v3(f[3], f[3].t[:]))
            yield
            b.act(st[:, 8:16], st[:, 0:8], AF.Sqrt)
            yield
            b.ts(st[:, 8:16], st[:, 8:16], 1e-12, ALU.max)
            yield
            b.recip(st[:, 16:24], st[:, 8:16])
            yield
            b.tt(v3(f[2], f[2].t[:]), v3(f[2], f[2].t[:]),
                 V(st, st.t[:, 16:24].unsqueeze(2).to_broadcast([128, 8, 64])), ALU.mult)
            yield
            b.tt(f[3][:], f[1][:], kab[:], ALU.mult, eng="pool")
            yield
            b.tt(f[3][:], f[3][:], omka[:], ALU.add, eng="pool")
            yield
            b.tt(f[3][:], f[3][:], k_, ALU.mult, eng="pool")
            yield
            b.stt(f[4][:], f[2][:], -1.0, f[1][:], ALU.mult, ALU.mult)
            yield
            b.copy(h["v16"][:], v_, eng="act")
            yield
            for c in range(2):
                b.copy(s.vz[c][c * 64:(c + 1) * 64, :], rw[c * 64:(c + 1) * 64, 2, :], eng="act")
                yield
            ci, ce, ca = CS[dr]
            pi = gp()
            b.mm(pi[:], tri[:, ci, :], f[0][:])
            yield
            b.act(f[5][:], pi[:], AF.Exp, scale=-CDEC)
            yield
            b.act(f[6][:], pi[:], AF.Exp, scale=CDEC)
            yield
            b.tt(h["Rt"][:], r_, f[5][:], ALU.mult)
            yield
            b.tt(h["Kt"][:], f[3][:], f[6][:], ALU.mult)
            yield
            b.tt(h["Bt"][:], f[4][:], f[6][:], ALU.mult)
            yield
            pe = gp()
            b.mm(pe[:], tri[:, ce, :], f[0][:])
            yield
            b.act(f[5][:], pe[:], AF.Exp, scale=-CDEC)
            yield
            b.tt(h["Kp"][:], f[2][:], f[5][:], ALU.mult)
            yield
            pa = gp()
            b.mm(pa[:], tri[:, ca, :], f[0][:])
            yield
            b.act(f[6][:], pa[:], AF.Exp, scale=-CDEC)
            yield
            b.tt(h["Kh"][:], f[3][:], f[6][:], ALU.mult, eng="pool")
            yield
            b.tt(h["Bh"][:], f[4][:], f[6][:], ALU.mult, eng="pool")
            yield
            p0 = gp()
            b.mm(p0[:], tri[:, 4, :], f[0][:])
            yield
            b.act(f[5][:], p0[:], AF.Exp, scale=-CDEC)
            yield
            b.tt(s.Dg[0][:], f[5][:], irep[:], ALU.mult, eng="pool")
            yield
            p1 = gp()
            b.mm(p1[:], tri[:, 5, :], f[0][:])
            yield
            b.act(f[7][:], p1[:], AF.Exp, scale=-CDEC)
            yield
            b.tt(s.Dg[1][:], f[7][:], irep[:], ALU.mult, eng="pool")
            yield
            for half in range(2):
                pt = gp()
                ptb = pt.t[:, 0:512].bitcast(BF16)
                for qi2 in range(2):
                    qi = half * 2 + qi2
                    src = h[("Kt", "Bt", "Kp", "Rt")[qi]]
                    for hp in range(4):
                        b.tr(V(pt, ptb[:, (qi2 * 4 + hp) * 128:(qi2 * 4 + hp + 1) * 128]), src[:, hp * 128:(hp + 1) * 128], ident[:])
                        yield
                for qi2 in range(2):
                    b.copy(V(s.XT, s.XT.t[:, :, half * 2 + qi2, :]),
                           V(pt, ptb[:, qi2 * 512:(qi2 + 1) * 512].rearrange("p (hp t) -> p hp t", hp=4)), eng=("act" if qi2 == 0 else "dve"))
                    yield
            b.copy(V(s.RTz, s.RTz.t[0:64, 0:8:2, :]), s.XT[0:64, :, 3, :], eng="act")
            yield
            b.copy(V(s.RTz, s.RTz.t[64:128, 1:8:2, :]), s.XT[64:128, :, 3, :], eng="act")
            yield
            if stage < 2:
                return
            pq = None
            for hd in range(8):
                hp, pb = hd // 2, (hd % 2) * 64
                p12 = gp()
                rhs = V(s.XT, s.XT.t[pb:pb + 64, hp, 2:4, :].rearrange("p a t -> p (a t)"))
                b.mm(p12[:, 0:256], s.XT[pb:pb + 64, hp, 0, :], rhs)
                yield
                b.mm(p12[:, 256:512], s.XT[pb:pb + 64, hp, 1, :], rhs)
                yield
                if stage >= 2.2:
                    b.tt(V(s.AM, s.AM.t[:, hd, :, :]), V(p12, p12.t[:].rearrange("p (a t) -> p a t", a=4)), M4[dr][:], ALU.mult)
                    yield
            for par in range(2):
                pq = gp()
                pb = par * 64
                for j in range(4):
                    hd = 2 * j + par
                    b.mm(pq[:, j * 128:(j + 1) * 128], s.XT[pb:pb + 64, j, 2, :], s.XT[pb:pb + 64, j, 1, :])
                    yield
                b.copy(s.f[7][:], pq[:], eng="act")
                yield
                b.tt(V(s.PT[0], s.PT[0].t[:, par:8:2, :]), V(s.f[7], s.f[7].t[:].rearrange("p (a t) -> p a t", a=4)),
                     MQ[dr][:], ALU.mult, eng="pool")
                yield
            if stage < 3:
                return
            b.copy(s.P[0][:], V(s.AM, s.AM.t[:, :, 2, :]), eng="act")
            yield
            b.tt(s.S[0][:], V(s.AM, s.AM.t[:, :, 2, :]), V(ident, ident.t[:].unsqueeze(1).to_broadcast([128, 8, 128])), ALU.add, eng="pool")
            yield
            cur = 0
            for lev in range(1, 6):
                nxt = 1 - cur
                for g4 in range(2):
                    hs = range(g4 * 4, g4 * 4 + 4)
                    if lev < 5:
                        p = gp()
                        for hd in hs:
                            b.mm(p[:, (hd % 4) * 128:(hd % 4 + 1) * 128], s.PT[cur][:, hd, :], s.P[cur][:, hd, :])
                            yield
                        b.copy(V(s.P[nxt], s.P[nxt].t[:, g4 * 4:g4 * 4 + 4, :]), V(p, p.t[:].rearrange("p (a t) -> p a t", a=4)), eng="act")
                        yield
                    p = gp()
                    for hd in hs:
                        b.mm(p[:, (hd % 4) * 128:(hd % 4 + 1) * 128], s.P[cur][:, hd, :], s.PT[cur][:, hd, :])
                        yield
                    b.copy(V(s.PT[nxt], s.PT[nxt].t[:, g4 * 4:g4 * 4 + 4, :]), V(p, p.t[:].rearrange("p (a t) -> p a t", a=4)),
                           eng=("act" if g4 == 0 else "dve"))
                    yield
                for g4 in range(2):
                    hs = range(g4 * 4, g4 * 4 + 4)
                    p = gp()
                    for hd in hs:
                        o = p[:, (hd % 4) * 128:(hd % 4 + 1) * 128]
                        b.mm(o, s.PT[nxt][:, hd, :], s.S[cur][:, hd, :])
                        yield
                    b.tt(V(s.S[nxt], s.S[nxt].t[:, g4 * 4:g4 * 4 + 4, :]), V(p, p.t[:].rearrange("p (a t) -> p a t", a=4)),
                         V(s.S[cur], s.S[cur].t[:, g4 * 4:g4 * 4 + 4, :]), ALU.add)
                    yield
                cur = nxt
            TT = s.S[cur]
            if stage < 4:
                return
            p = gp()
            for hd in range(8):
                b.mm(p[:, hd * 64:(hd + 1) * 64], s.AM[:, hd, 0, :], h["v16"][:, hd * 64:(hd + 1) * 64])
                yield
            b.copy(h["AV"][:], p[:], eng="act")
            yield
            p = gp()
            for hd in range(8):
                hp, pb = hd // 2, (hd % 2) * 64
                b.mm(p[pb:pb + 64, hp * 128:(hp + 1) * 128], h["Kp"][:, hd * 64:(hd + 1) * 64], TT[:, hd, :])
                yield
            b.copy(V(s.WT, s.WT.t[0:64, 0:8:2, :]), V(p, p.t[0:64, :].rearrange("p (a t) -> p a t", a=4)), eng="act")
            yield
            b.copy(V(s.WT, s.WT.t[64:128, 1:8:2, :]), V(p, p.t[64:128, :].rearrange("p (a t) -> p a t", a=4)), eng="act")
            yield
            if stage < 5:
                return
            if (dr == 0 and ti == NT // 2) or (dr == 1 and ti == NT // 2 - 1):
                b.ts(s.H32[s.hi][:], s.H32[s.hi][:], flag[:, 0:1], ALU.mult)
                yield
                b.ts(s.H16[s.hi][:], s.H16[s.hi][:], flag[:, 0:1], ALU.mult)
                yield
            for c in ((0, 1) if dr == 0 else (1, 0)):
                cb = c * 64
                cs = slice(cb, cb + 64)
                Ho32, Ho16 = s.H32[s.hi], s.H16[s.hi]
                Hn32, Hn16 = s.H32[1 - s.hi], s.H16[1 - s.hi]
                s.hi = 1 - s.hi
                PH = gp()
                for hd in range(8):
                    hp, pb = hd // 2, (hd % 2) * 64
                    hc = slice(hd * 64, (hd + 1) * 64)
                    o = PH[pb:pb + 64, hp * 64:(hp + 1) * 64]
                    b.mm(o, s.Dg[c][:, hc], Ho32[:, hp, :], start=(hd < 2), stop=False, skip_group_check=True)
                    yield
                    b.mm(o, h["Kh"][:, hc], s.vz[c][:, hc], start=False, stop=False, skip_group_check=True)
                    yield
                PU = gp()
                for hd in range(8):
                    hp = hd // 2
                    hc = slice(hd * 64, (hd + 1) * 64)
                    o = PU[cs, hc]
                    b.mm(o, TT[:, hd, cs], h["AV"][:, hc], start=True, stop=False, skip_group_check=True)
                    yield
                    b.mm(o, s.WT[:, hd, cs], Ho16[:, hp, :], start=False, stop=True, skip_group_check=True)
                    yield
                b.copy(s.U[c][cs, :], PU[cs, :], eng="act")
                yield
                for hd in range(8):
                    hp, pb = hd // 2, (hd % 2) * 64
                    hc = slice(hd * 64, (hd + 1) * 64)
                    o = PH[pb:pb + 64, hp * 64:(hp + 1) * 64]
                    b.mm(o, h["Bh"][:, hc], s.U[c][:, hc], start=False, stop=True, skip_group_check=True)
                    yield
                b.copy(V(Hn32, Hn32.t[:].rearrange("p a n -> p (a n)")), PH[:, 0:256], eng="act")
                yield
                b.copy(Hn16[:], Hn32[:], eng="pool")
                yield
                for hd in range(8):
                    hp = hd // 2
                    hc = slice(hd * 64, (hd + 1) * 64)
                    o = PY[cs, hc]
                    b.mm(o, s.RTz[:, hd, cs], Ho16[:, hp, :], start=True, stop=False, skip_group_check=True)
                    yield
                    b.mm(o, s.AM[:, hd, 1, cs], h["v16"][:, hc], start=False, stop=False, skip_group_check=True)
                    yield
                    b.mm(o, s.AM[:, hd, 3, cs], s.U[c][:, hc], start=False, stop=True, skip_group_check=True)
                    yield
            b.copy(s.Y[:], PY[:], eng="act")
            yield
            b.dma("sp", V(Yd, Yd.t[dr, ti * 128:(ti + 1) * 128, :], (dr, ti)), s.Y[:])
            yield

        nloop = NT if stage >= 9 else min(NT, NTL)

        def stream(dr):
            for i in range(nloop):
                yield from rw_tile(dr, i if dr == 0 else NT - 1 - i)
        alive = [stream(0), stream(1)]
        for _ in range(STAGGER):
            next(alive[0])
        while alive:
            for g_ in list(alive):
                try:
                    next(g_)
                except StopIteration:
                    alive.remove(g_)
    if stage < 9:
        return

    if not do_post:
        return
    with b.phase():
        gens = rwkv_post_setup(k, rwd, Yd, mixT0, ident, None, 4, "pool")
        while gens:
            for g_ in list(gens):
                try:
                    next(g_)
                except StopIteration:
                    gens.remove(g_)


def rwkv_post_setup(k, rwd, Yd, mixT0, ident, TP, NS, e2):
    b = k.b
    d = k.d
    NT = T // 128
    if True:
        def bc2(nm):
            t = b.sb("bc_" + nm, [128, 512])
            src = d[nm].t
            if len(src.shape) == 2:
                src = src.rearrange("h n -> (h n)")
            b.dma("sp", t[:], V(d[nm], src.partition_broadcast(128)))
            return t
        a0 = [bc2("a0_f"), bc2("a0_b")]
        kab = bc2("k_a")
        rkb = bc2("r_k")
        lw = bc2("lnx_w")
        lb = bc2("lnx_b")
        omka = b.sb("omka2", [128, 512])
        b.ts(omka[:], kab[:], -1.0, ALU.mult, 1.0, ALU.add)
        b.ts(kab[:], kab[:], 0.5, ALU.mult)
        rws = [b.sb("rwp%d" % i, [128, 8, 512]) for i in range(NS)]
        ys = [b.sb("yp%d" % i, [128, 2, 512]) for i in range(NS)]
        fs = [[b.sb("pf%d_%d" % (i, j), [128, 512]) for i in range(5)] for j in range(NS)]
        st_t = b.sb("pst", [128, 32 * NS])
        ob = [b.sb("pob%d" % i, [128, 512], BF16) for i in range(NS)]
        oT = [b.sb("poT%d" % i, [128, 4, 128], BF16) for i in range(NS)]
        PTt = [TP] * NS if TP is not None else [b.ps("PTp%d" % i, [128, 512]) for i in range(NS)]

        def v3(view_tile, ap):
            return V(view_tile, ap.rearrange("p (h n) -> p h n", n=64))

        def bc8(tile, c0, key):
            return V(tile, tile.t[:, c0:c0 + 8].unsqueeze(2).to_broadcast([128, 8, 64]), key)

        def post_tile(j, ti):
            rw = rws[j]
            yy = ys[j]
            f = fs[j]
            off = j * 32
            st = _Cols(st_t, off)
            b.dma("sp", rw[:, 0:3, :], V(rwd, rwd.t[ti * 128:(ti + 1) * 128, 0:3, :]))
            b.dma("sp", rw[:, 5:8, :], V(rwd, rwd.t[ti * 128:(ti + 1) * 128, 5:8, :]))
            for dr in range(2):
                b.dma("sp", yy[:, dr, :], V(Yd, Yd.t[dr, ti * 128:(ti + 1) * 128, :]))
            yield
            y = f[0]
            b.tt(y[:], yy[:, 0, :], yy[:, 1, :], ALU.add)
            yield
            b.reduce(st[:, 0:8], v3(y, y.t[:]))
            yield
            b.ts(st[:, 0:8], st[:, 0:8], 1.0 / 64, ALU.mult)
            yield
            b.tt(v3(y, y.t[:]), v3(y, y.t[:]), bc8(st_t, off, off), ALU.subtract)
            yield
            b.tt(f[1][:], y[:], y[:], ALU.mult)
            yield
            b.reduce(st[:, 8:16], v3(f[1], f[1].t[:]))
            yield
            b.act(st[:, 16:24], st[:, 8:16], AF.Sqrt, bias=64e-5, scale=1.0 / 64)
            yield
            b.recip(st[:, 24:32], st[:, 16:24])
            yield
            b.tt(v3(y, y.t[:]), v3(y, y.t[:]), bc8(st_t, off + 24, off), ALU.mult)
            yield
            b.tt(y[:], y[:], lw[:], ALU.mult, eng=e2)
            yield
            b.tt(y[:], y[:], lb[:], ALU.add, eng=e2)
            yield
            b.tt(f[2][:], rw[:, 5, :], a0[0][:], ALU.add, eng=e2)
            yield
            b.act(f[2][:], f[2][:], AF.Exp, scale=-1.0)
            yield
            b.ts(f[2][:], f[2][:], 1.0, ALU.add)
            yield
            b.recip(f[2][:], f[2][:])
            yield
            b.tt(f[3][:], rw[:, 6, :], a0[1][:], ALU.add, eng=e2)
            yield
            b.act(f[3][:], f[3][:], AF.Exp, scale=-1.0)
            yield
            b.ts(f[3][:], f[3][:], 1.0, ALU.add)
            yield
            b.recip(f[3][:], f[3][:])
            yield
            b.tt(f[2][:], f[2][:], f[3][:], ALU.add, eng=e2)
            yield
            b.tt(f[2][:], f[2][:], kab[:], ALU.mult, eng=e2)
            yield
            b.tt(f[2][:], f[2][:], omka[:], ALU.add, eng=e2)
            yield
            b.tt(f[2][:], f[2][:], rw[:, 1, :], ALU.mult)
            yield
            b.tt(f[2][:], f[2][:], rw[:, 0, :], ALU.mult)
            yield
            b.tt(f[2][:], f[2][:], rkb[:], ALU.mult)
            yield
            b.reduce(st[:, 8:16], v3(f[2], f[2].t[:]))
            yield
            b.tt(v3(f[4], f[4].t[:]), v3(rw, rw.t[:, 2, :]), bc8(st_t, off + 8, off), ALU.mult)
            yield
            b.tt(y[:], y[:], f[4][:], ALU.add)
            yield
            o = ob[j]
            b.tt(o[:], y[:], rw[:, 7, :], ALU.mult)
            yield
            pt = PTt[j]
            ptb = pt.t[:, 0:256].bitcast(BF16)
            for c in range(4):
                b.tr(V(pt, ptb[:, c * 128:(c + 1) * 128]), o[:, c * 128:(c + 1) * 128], ident[:])
            ot = oT[j]
            b.copy(ot[:], V(pt, ptb.rearrange("p (c t) -> p c t", c=4)), eng="act")
            yield
            for c in range(4):
                b.dma("sp", V(mixT0, mixT0.t[512 + c * 128:512 + (c + 1) * 128, ti * 128:(ti + 1) * 128], ("r", c, ti)), ot[:, c, :])
            yield

        def pstream(j):
            for ti in range(j, NT, NS):
                yield from post_tile(j, ti)
        return [pstream(j) for j in range(NS)]


def make_consts(is_prompt):
    c = {}
    p = np.arange(128)[:, None]
    f = np.arange(128)[None, :]
    c["c_ident"] = np.eye(128, dtype=np.float32)
    partner = (np.arange(128) // 64) * 64 + (np.arange(128) % 64 + 32) % 64
    perm = np.zeros((128, 128), np.float32)
    perm[partner, np.arange(128)] = 1.0
    c["c_perm"] = perm
    NEG = -30000.0
    IU = np.where(p <= f, 0.0, NEG).astype(np.float32)
    IL = np.where(p >= f, 0.0, NEG).astype(np.float32)
    c["c_maskAB"] = np.concatenate([IU, IL], axis=1)
    if is_prompt:
        c["c_maskX"] = c["c_maskAB"].copy()
    else:
        a = IU.copy(); a[64:, :] = NEG
        bb = IL.copy(); bb[:64, :] = NEG
        c["c_maskX"] = np.concatenate([a, bb], axis=1)
    T = 4096
    S = 4096 if is_prompt else 2048
    pos = (np.arange(T) % S).astype(np.float32)
    half = 32
    inv = (10000.0 ** (-np.arange(half, dtype=np.float32) / half)).astype(np.float32)
    ang = pos[None, :] * inv[:, None]
    cos = np.cos(ang).astype(np.float32)
    sin = np.sin(ang).astype(np.float32)
    c["c_cos"] = np.tile(cos, (4, 1))
    c["c_sin"] = np.concatenate([-sin, sin, -sin, sin], axis=0)
    flag = np.zeros((128, 4), np.float32)
    flag[:, 0] = 1.0 if is_prompt else 0.0
    flag[:, 1] = 0.0 if is_prompt else NEG
    c["c_flag"] = flag
    blk = (p // 64) == (f // 64)
    SU = (blk & (p < f)).astype(np.float32)
    IUb = (blk & (p <= f)).astype(np.float32)
    SL = (blk & (p > f)).astype(np.float32)
    ILb = (blk & (p >= f)).astype(np.float32)
    T0 = np.broadcast_to((p < 64), (128, 128)).astype(np.float32)
    T1 = np.broadcast_to((p >= 64), (128, 128)).astype(np.float32)
    c["c_tri"] = np.stack([SU, IUb, SL, ILb, T0, T1], axis=1).reshape(128, 6 * 128).astype(np.float32)
    kk = np.arange(512)[None, :] % 64
    hh = np.arange(512)[None, :] // 64
    c["c_irep"] = (((p % 64) == kk) & ((p // 64) == (hh % 2))).astype(np.float32)
    return c


INPUT_NAMES = None


def build_program(upto="all", skip=()):
    nc = bass.Bass("TRN2", target_bir_lowering=False)
    k = K(nc)
    b = k.b
    shapes = {
        "mix_pre0": [D], "mix_post0": [D], "w_in0": [D, 3072], "lam_q1": [64], "lam_k1": [64], "lam_q2": [64], "lam_k2": [64],
        "subln_w": [128], "mu_r": [512], "mu_k": [512], "mu_v": [512], "mu_w": [D], "mu_a": [D], "mu_g": [D],
        "w0_f": [512], "w1_f": [D, 64], "w2_f": [64, 512], "w0_b": [512], "w1_b": [D, 64], "w2_b": [64, 512],
        "a0_f": [512], "a1_f": [D, 64], "a2_f": [64, 512], "a0_b": [512], "a1_b": [D, 64], "a2_b": [64, 512],
        "g1": [D, 160], "g2": [160, 512], "k_k": [512], "k_a": [512], "r_k": [8, 64], "lnx_w": [512], "lnx_b": [512],
        "w_out0": [D, D], "ffn_pre0": [D], "ffn_post0": [D], "ffn_gate0": [D, FH], "ffn_up0": [D, FH], "ffn_down0": [FH, D],
        "mix_pre1": [D], "mix_post1": [D], "w_in1": [D, 2304], "w_out1": [768, D], "ffn_pre1": [D], "ffn_post1": [D],
        "ffn_gate1": [D, FH], "ffn_up1": [D, FH], "ffn_down1": [FH, D],
        "c_ident": [128, 128], "c_perm": [128, 128], "c_maskAB": [128, 256], "c_maskX": [128, 256], "c_cos": [128, T], "c_sin": [128, T],
        "c_flag": [128, 4], "c_tri": [128, 6, 128], "c_irep": [128, 512],
    }
    for n, sh in shapes.items():
        k.dram(n, sh, F32, kind="ExternalInput")
    x = k.dram("x", [T, D], F32, kind="ExternalInput")
    y = k.dram("y", [T, D], F32, kind="ExternalOutput")
    if upto != "all":
        k.ext_out = {upto}
    cst = {"ident": k.d["c_ident"][:], "perm": k.d["c_perm"][:], "flag": k.d["c_flag"][:], "cos": k.d["c_cos"][:], "sin": k.d["c_sin"][:],
           "maskAB": k.d["c_maskAB"][:], "maskX": k.d["c_maskX"][:], "tri": k.d["c_tri"][:], "irep": k.d["c_irep"][:], "eps": EPS}
    prep_weight_rows(k, "w_in0", 8, 3072)

    def late_casts():
        r = {}
        r["wout0"] = prep_weight_rows(k, "w_out0", 8, D)
        prep_weights_ffn(k, 0)
        prep_weight_rows(k, "w_in1", 8, 2304)
        r["wout1"] = prep_weight_rows(k, "w_out1", 6, D)
        prep_weights_ffn(k, 1)
        return r
    mixT0 = k.dram("mixT0", [1024, T], BF16)
    mixT1 = k.dram("mixT1", [768, T], BF16)
    xmid = k.dram("xmid", [T, D], F32)
    if upto != "all":
        y.t
    QTd = k.dram("QTd", [512, T], BF16)
    KTd = k.dram("KTd", [512, T], BF16)
    Vad = k.dram("Vad", [T, 4 * 129], BF16)
    rwd = k.dram("rwd", [T, 8, 512], F32)
    with b.phase():
        ident = b.sb("ident", [128, 128], BF16)
        b.dma("pool", ident[:], cst["ident"])
        flag = b.sb("flagm", [128, 4])
        b.dma("sp", flag[:], cst["flag"])
        with b.phase():
            xnT = b.sb("xnT0", [128, 8, XW], BF16)
            l0_qkv(k, xnT, cst, QTd, KTd, Vad, x_in=x, ident=ident)
            l0_norm(k, x, xnT, ident, cst, do_norm=False)
            if "rwproj" not in skip:
                l0_rwproj(k, xnT, cst, rwd)
        Yd = k.dram("Yd", [2, T, 512], F32)
        if "rwkv" not in skip:
            l0_rwkv(k, cst, rwd, mixT0, ident, Yd, flag, do_post=False)
        lc = late_casts()
        wout0_b, wout1_b = lc["wout0"], lc["wout1"]
        if "diff" not in skip:
            l0_diffattn(k, cst, QTd, KTd, Vad, mixT0, ident, flag,
                        co_setup=lambda TP: rwkv_post_setup(k, rwd, Yd, mixT0, ident, TP, 2, "dve"))
    if upto == "mixT0":
        return nc, list(shapes.keys())
    phase_post(k, 0, 8, mixT0, wout0_b, x, xmid, cst)
    if upto == "xmid":
        return nc, list(shapes.keys())
    phase_l1(k, xmid, mixT1, cst)
    if upto == "mixT1":
        return nc, list(shapes.keys())
    phase_post(k, 1, 6, mixT1, wout1_b, xmid, y, cst)
    return nc, list(shapes.keys())


def kernel(**inputs):
    n = 8
    xp = np.asarray(inputs["x_prompt"], dtype=np.float32)
    xs = np.asarray(inputs["x_sample"], dtype=np.float32)
    nc, names = build_program()
    cp = make_consts(True)
    cs = make_consts(False)
    in_maps = []
    for c in range(n):
        m = {}
        prompt = c < 4
        cc = cp if prompt else cs
        for nm in names:
            if nm.startswith("c_"):
                a = cc[nm]
                if nm == "c_tri":
                    a = a.reshape(128, 6, 128)
                m[nm] = np.ascontiguousarray(a, dtype=np.float32)
            else:
                m[nm] = np.ascontiguousarray(np.asarray(inputs[nm], dtype=np.float32))
        if prompt:
            m["x"] = np.ascontiguousarray(xp[c])
        else:
            j = 2 * (c - 4)
            m["x"] = np.ascontiguousarray(xs[j:j + 2].reshape(T, D))
        in_maps.append(m)
    res = run_bass_kernel_spmd(nc, in_maps, core_ids=list(range(n)))
    outs = [np.asarray(r["y"], dtype=np.float32) for r in res.results]
    y_prompt = np.stack(outs[0:4], axis=0)
    y_sample = np.concatenate([o.reshape(2, T // 2, D) for o in outs[4:8]], axis=0)
    return (y_prompt, y_sample)
```

```python
import numpy as np
from contextlib import ExitStack
import concourse.bass as bass
import concourse.mybir as mybir

F32 = mybir.dt.float32
BF16 = mybir.dt.bfloat16
ALU = mybir.AluOpType
AF = mybir.ActivationFunctionType
AX = mybir.AxisListType


class Res:
    __slots__ = ("w", "r")

    def __init__(self):
        self.w = None
        self.r = {}


class View:
    __slots__ = ("tile", "ap", "key")

    def __init__(self, tile, ap, key):
        self.tile = tile
        self.ap = ap
        self.key = key


class _Keyed:
    def __init__(self, tile, key):
        self.tile = tile
        self.key = key

    def __getitem__(self, idx):
        return View(self.tile, self.tile.t[idx], self.key)


class Tile:
    def __init__(self, name, t):
        self.name = name
        self.t = t
        self.res = {None: Res()}

    def __getitem__(self, idx):
        return View(self, self.t[idx], None)

    def k(self, key):
        return _Keyed(self, key)

    def conflicts(self, key):
        if key is None:
            return list(self.res.values())
        if key not in self.res:
            self.res[key] = Res()
        return [self.res[None], self.res[key]]

    def get(self, key):
        if key not in self.res:
            self.res[key] = Res()
        return self.res[key]


NDMA = 56
SB_DEBUG = False
NSW = 32


class Builder:
    def __init__(self, nc):
        self.nc = nc
        self.eng = {"pe": nc.tensor, "dve": nc.vector, "act": nc.scalar, "pool": nc.gpsimd, "sp": nc.sync}
        self.sem = {e: nc.alloc_semaphore("sem_" + e) for e in ("pe", "dve", "act", "pool")}
        self.tick = {e: 0 for e in self.sem}
        self.dsem = [nc.alloc_semaphore("dsem%d" % i) for i in range(NDMA)]
        self.ssem = [nc.alloc_semaphore("ssem%d" % i) for i in range(NSW)]
        self.ndma = 0
        self.nsw = 0
        self.waited = {e: {} for e in self.eng}
        self.epoch = 0
        self.barA = nc.alloc_semaphore("barA")
        self.barB = nc.alloc_semaphore("barB")
        self.nwait = 0
        self.ninst = 0
        self.stack = None

    def phase(self):
        return _Phase(self)

    def sb(self, name, shape, dtype=F32):
        self.uid = getattr(self, "uid", 0) + 1
        name = "%s_u%d" % (name, self.uid)
        t = self.stack.enter_context(self.nc.sbuf_tensor(name, list(shape), dtype))
        self.sb_hi = max(getattr(self, "sb_hi", 0), self.nc.sbuf_base)
        if SB_DEBUG:
            print("SB", name, shape, "end", self.nc.sbuf_base)
        return Tile(name, t)

    def ps(self, name, shape, dtype=F32):
        self.uid = getattr(self, "uid", 0) + 1
        name = "%s_u%d" % (name, self.uid)
        t = self.stack.enter_context(self.nc.psum_tensor(name, list(shape), dtype))
        return Tile(name, t)

    def dram(self, name, shape, dtype, kind="Internal"):
        t = self.nc.dram_tensor(name, list(shape), dtype, kind=kind)
        return Tile(name, t.ap())

    def _wait(self, eng, tok):
        sem, val, owner = tok[0], tok[1], tok[2]
        if len(tok) > 3 and tok[3] < self.epoch:
            return
        if owner == eng and eng == "pe":
            return
        key = id(sem)
        w = self.waited[eng]
        if w.get(key, 0) >= val:
            return
        self.eng[eng].wait_ge(sem, val)
        w[key] = val
        self.nwait += 1

    def _deps(self, eng, reads, writes):
        for v in reads:
            for r in v.tile.conflicts(v.key):
                if r.w is not None:
                    self._wait(eng, r.w)
        for v in writes:
            for r in v.tile.conflicts(v.key):
                if r.w is not None:
                    self._wait(eng, r.w)
                for (sem, owner, ep), val in list(r.r.items()):
                    self._wait(eng, (sem, val, owner, ep))

    def _commit(self, tok, reads, writes):
        sem, val, owner = tok[0], tok[1], tok[2]
        for v in writes:
            if v.key is None:
                t = v.tile
                t.res = {None: t.res[None]}
            r = v.tile.get(v.key)
            r.w = tok
            r.r = {}
        for v in reads:
            r = v.tile.get(v.key)
            k = (sem, owner, tok[3])
            if r.r.get(k, 0) < val:
                r.r[k] = val

    def op(self, eng, fn, reads, writes):
        self._deps(eng, reads, writes)
        ins = fn()
        self.tick[eng] += 1
        ins.then_inc(self.sem[eng], 1)
        tok = (self.sem[eng], self.tick[eng], eng, self.epoch)
        self._commit(tok, reads, writes)
        self.ninst += 1
        return tok

    def _dslot(self, q):
        if q == "pool":
            i = self.nsw
            self.nsw += 1
            return self.ssem[i % NSW], i // NSW, "sdma%d" % (i % NSW)
        i = self.ndma
        self.ndma += 1
        return self.dsem[i % NDMA], i // NDMA, "dma%d" % (i % NDMA)

    def dma(self, q, out, in_, **kw):
        sem, gen, owner = self._dslot(q)
        if gen > 0:
            self._wait(q, (sem, 16 * gen, "dma"))
        self._deps(q, [in_], [out])
        ins = self.eng[q].dma_start(out=out.ap, in_=in_.ap, **kw)
        ins.then_inc(sem, 16)
        tok = (sem, 16 * (gen + 1), owner, self.epoch)
        self._commit(tok, [in_], [out])
        self.ninst += 1
        return tok

    def barrier(self):
        toks = [(self.sem[e], self.tick[e], e) for e in self.sem if self.tick[e] > 0]
        for (n, N, sems) in ((self.ndma, NDMA, self.dsem), (self.nsw, NSW, self.ssem)):
            for slot in range(min(n, N)):
                cnt = (n - 1 - slot) // N + 1
                toks.append((sems[slot], 16 * cnt, "dma"))
        for e in self.eng:
            for tok in toks:
                self._wait(e, tok)
        self.epoch += 1
        ep = self.epoch
        for e in self.eng:
            self.eng[e].sem_inc(self.barA, 1)
        for e in self.sem:
            self.eng[e].wait_ge(self.barA, 5 * ep)
            self.eng[e].sem_clear(self.sem[e])
            self.eng[e].sem_inc(self.barB, 1)
        for e in self.eng:
            self.eng[e].wait_ge(self.barB, 4 * ep)
        for e in self.sem:
            self.tick[e] = 0
        for e in self.eng:
            for s_ in self.sem.values():
                self.waited[e].pop(id(s_), None)

    def mm(self, out, lhsT, rhs, start=True, stop=True, **kw):
        return self.op("pe", lambda: self.nc.tensor.matmul(out.ap, lhsT.ap, rhs.ap, start=start, stop=stop, **kw),
                       [lhsT, rhs] + ([] if start else [out]), [out])

    def tr(self, out, in_, ident):
        return self.op("pe", lambda: self.nc.tensor.transpose(out.ap, in_.ap, ident.ap), [in_, ident], [out])

    def act(self, out, in_, func, bias=0.0, scale=1.0, accum=None):
        reads = [in_]
        kw = {}
        if isinstance(bias, View):
            reads.append(bias)
            kw["bias"] = bias.ap
        else:
            kw["bias"] = float(bias)
        if isinstance(scale, View):
            reads.append(scale)
            kw["scale"] = scale.ap
        else:
            kw["scale"] = float(scale)
        writes = [out]
        if accum is not None:
            writes.append(accum)
            kw["accum_out"] = accum.ap
        return self.op("act", lambda: self.nc.scalar.activation(out.ap, in_.ap, func, **kw), reads, writes)

    def tt(self, out, a, b, op, eng="dve", after=()):
        e = self.eng[eng]
        return self.op(eng, lambda: e.tensor_tensor(out.ap, a.ap, b.ap, op), [a, b] + list(after), [out])

    def ts(self, out, a, s1, op0, s2=None, op1=None, eng="dve", accum=None):
        e = self.eng[eng]
        reads = [a]
        x1 = s1.ap if isinstance(s1, View) else float(s1)
        if isinstance(s1, View):
            reads.append(s1)
        x2 = None
        if s2 is not None:
            x2 = s2.ap if isinstance(s2, View) else float(s2)
            if isinstance(s2, View):
                reads.append(s2)
        writes = [out]
        kw = {}
        if accum is not None:
            writes.append(accum)
            kw["accum_out"] = accum.ap
        if op1 is None:
            return self.op(eng, lambda: e.tensor_scalar(out.ap, a.ap, x1, None, op0, **kw), reads, writes)
        return self.op(eng, lambda: e.tensor_scalar(out.ap, a.ap, x1, x2, op0, op1, **kw), reads, writes)

    def stt(self, out, a, s, b, op0, op1, accum=None):
        reads = [a, b]
        x = s.ap if isinstance(s, View) else float(s)
        if isinstance(s, View):
            reads.append(s)
        writes = [out]
        kw = {}
        if accum is not None:
            writes.append(accum)
            kw["accum_out"] = accum.ap
        return self.op("dve", lambda: self.nc.vector.scalar_tensor_tensor(out.ap, a.ap, x, b.ap, op0, op1, **kw),
                       reads, writes)

    def copy(self, out, in_, eng="dve"):
        if eng == "act":
            return self.op("act", lambda: self.nc.scalar.copy(out.ap, in_.ap), [in_], [out])
        e = self.eng[eng]
        return self.op(eng, lambda: e.tensor_copy(out.ap, in_.ap), [in_], [out])

    def memset(self, out, val, eng="pool"):
        e = self.eng[eng]
        return self.op(eng, lambda: e.memset(out.ap, val), [], [out])

    def recip(self, out, in_):
        return self.op("dve", lambda: self.nc.vector.reciprocal(out.ap, in_.ap), [in_], [out])

    def reduce(self, out, in_, op=ALU.add, axis=AX.X):
        return self.op("dve", lambda: self.nc.vector.tensor_reduce(out.ap, in_.ap, axis, op), [in_], [out])


class _Phase:
    def __init__(self, b):
        self.b = b

    def __enter__(self):
        self.prev = self.b.stack
        self.st = ExitStack()
        self.st.__enter__()
        self.b.stack = self.st
        return self

    def __exit__(self, *a):
        self.b.barrier()
        self.b.stack = self.prev
        return self.st.__exit__(*a)
from concourse.bass_utils import run_bass_kernel_spmd
import math
T = 4096
D = 1024
FH = 2816
NHC = FH // 128
EPS = 1e-6


class K:
    def __init__(self, nc, ext_in=(), ext_out=()):
        self.nc = nc
        self.b = Builder(nc)
        self.ext_in = set(ext_in)
        self.ext_out = set(ext_out)
        self.d = {}

    def dram(self, name, shape, dtype, kind=None):
        if kind is None:
            kind = "ExternalInput" if name in self.ext_in else ("ExternalOutput" if name in self.ext_out else "Internal")
        t = self.b.dram(name, shape, dtype, kind=kind)
        self.d[name] = t
        return t


class _Bank:
    def __init__(self, tile, i):
        self.tile = tile
        self.i = i

    def __getitem__(self, idx):
        rows, cols = idx
        return View(self.tile, self.tile.t[rows, self.i, cols], None)


class _Cols:
    def __init__(self, tile, off):
        self.tile = tile
        self.off = off

    def __getitem__(self, idx):
        rows, cols = idx
        return View(self.tile, self.tile.t[rows, cols.start + self.off:cols.stop + self.off], self.off)


def V(tile, ap, key=None):
    return View(tile, ap, key)


def dma_nc(b, q, out, in_):
    return b.dma(q, out, in_, allow_slow_non_contiguous=True)


def prep_weights_ffn(k, L):
    b = k.b
    wg, wu, wd = k.d["ffn_gate%d" % L], k.d["ffn_up%d" % L], k.d["ffn_down%d" % L]
    wgu_b = k.dram("wgu_b%d" % L, [NHC, 128, 2, 8, 128], BF16)
    wd_b = k.dram("wd_b%d" % L, [NHC, 128, D], BF16)
    for hc in range(NHC):
        for j, w in enumerate((wg, wu)):
            src = V(w, w.t[:, hc * 128:(hc + 1) * 128].rearrange("(kc p) c -> p kc c", p=128))
            b.dma("pool", V(wgu_b, wgu_b.t[hc, :, j, :, :], hc), src)
    for hc in range(0, NHC, 2):
        src = V(wd, wd.t[hc * 128:(hc + 2) * 128, :].rearrange("(h p) c -> h p c", p=128))
        b.dma("pool", V(wd_b, wd_b.t[hc:hc + 2], hc), src)


def prep_weight_rows(k, name, nchunk, ncol):
    b = k.b
    w = k.d[name]
    wb = k.dram(name + "_b", [nchunk, 128, ncol], BF16)
    step = 2
    for c in range(0, nchunk, step):
        n = min(step, nchunk - c)
        src = V(w, w.t[c * 128:(c + n) * 128, :].rearrange("(h p) c -> h p c", p=128))
        b.dma("pool", V(wb, wb.t[c:c + n], c), src)
    return wb


def phase_post(k, L, C, mixT, wout_b, x_in, x_out, cst):
    b = k.b
    nc = k.nc
    wgu_b, wd_b = k.d["wgu_b%d" % L], k.d["wd_b%d" % L]
    x1d = k.dram("x1d%d" % L, [T, D], F32)
    with b.phase():
        ident = b.sb("ident", [128, 128], BF16)
        b.dma("pool", ident[:], cst["ident"])
        g_post = b.sb("g_post", [128, D])
        g_fpost = b.sb("g_fpost", [128, D])
        g_fpre = b.sb("g_fpre", [128, 8])
        b.dma("sp", g_post[:], V(k.d["mix_post%d" % L], k.d["mix_post%d" % L].t.partition_broadcast(128)))
        b.dma("sp", g_fpost[:], V(k.d["ffn_post%d" % L], k.d["ffn_post%d" % L].t.partition_broadcast(128)))
        dma_nc(b, "sp", g_fpre[:], V(k.d["ffn_pre%d" % L], k.d["ffn_pre%d" % L].t.rearrange("(kc p) -> p kc", p=128)))
        wout = b.sb("wout_sb%d" % L, [128, C, D], BF16)
        for c in range(C):
            b.dma("sp", wout[:, c, :], wout_b[c])
        wd = b.sb("wd", [128, NHC, D], BF16)
        for hc in range(NHC):
            b.dma("sp", wd.k(hc)[:, hc, :], wd_b.k(hc - hc % 2)[hc])
        hT = b.sb("hT", [128, NHC, 1024], BF16)
        xn2T = b.sb("xn2T", [128, 8, 1024], BF16)
        mixh = b.sb("mixh", [128, C, 512], BF16)
        wgu = [b.sb("wgu%d" % i, [128, 2, 8, 128], BF16) for i in range(2)]
        xin = [b.sb("xin%d" % i, [128, D]) for i in range(2)]
        x1r = [b.sb("x1r%d" % i, [128, D]) for i in range(2)]
        tmp = b.sb("tmp", [128, D])
        junk = b.sb("junk", [128, D], BF16)
        xs = b.sb("xs", [128, D], BF16)
        sg = [b.sb("sg%d" % i, [128, 512], BF16) for i in range(2)]
        st_t = b.sb("st", [128, 64])
        A = [b.ps("A%d" % i, [128, 1024]) for i in range(2)]
        G = [b.ps("G%d" % i, [128, 1024]) for i in range(2)]
        na = 0
        ng = 0
        nw = 0
        for stile in range(T // 1024):
            t0 = stile * 1024
            na0 = na

            def emit_outproj(s):
                if s % 4 == 0:
                    for c in range(C):
                        b.dma("sp", mixh[:, c, :], V(mixT, mixT.t[c * 128:(c + 1) * 128, t0 + (s // 4) * 512: t0 + (s // 4) * 512 + 512]))
                r0 = t0 + s * 128
                xi = xin[s % 2]
                b.dma("sp", xi[:], V(x_in, x_in.t[r0:r0 + 128, :]))
                acc = A[(na0 + s) % 2]
                for half in range(2):
                    for c in range(C):
                        b.mm(acc[:, half * 512:(half + 1) * 512], mixh[:, c, (s % 4) * 128:(s % 4) * 128 + 128],
                             wout[:, c, half * 512:(half + 1) * 512], start=(c == 0), stop=(c == C - 1))
            emit_outproj(0)
            for s in range(8):
                r0 = t0 + s * 128
                st = _Cols(st_t, (s % 2) * 16)
                xi = xin[s % 2]
                acc = A[(na0 + s) % 2]
                if s + 1 < 8 and (s + 1) % 4 != 0:
                    emit_outproj(s + 1)
                b.act(junk[:], acc[:], AF.Square, accum=st[:, 0:1])
                b.act(st[:, 1:2], st[:, 0:1], AF.Sqrt, bias=cst["eps"], scale=1.0 / D)
                b.recip(st[:, 2:3], st[:, 1:2])
                b.stt(tmp[:], acc[:], st[:, 2:3], g_post[:], ALU.mult, ALU.mult)
                b.tt(xi[:], tmp[:], xi[:], ALU.add)
                b.dma("sp", V(x1d, x1d.t[r0:r0 + 128, :], r0), xi[:])
                b.act(junk[:], xi[:], AF.Square, accum=st[:, 3:4])
                b.act(st[:, 4:5], st[:, 3:4], AF.Sqrt, bias=cst["eps"], scale=1.0 / D)
                b.recip(st[:, 5:6], st[:, 4:5])
                b.ts(xs[:], xi[:], st[:, 5:6], ALU.mult)
                gt = G[ng % 2]
                ng += 1
                gtb = gt.t[:, 0:512].bitcast(BF16)
                for kc in range(8):
                    b.tr(V(gt, gtb[:, kc * 128:(kc + 1) * 128]), xs[:, kc * 128:(kc + 1) * 128], ident[:])
                b.tt(xn2T[:, :, s * 128:(s + 1) * 128], V(gt, gtb.rearrange("p (kc t) -> p kc t", kc=8)),
                     V(g_fpre, g_fpre.t[:, :].unsqueeze(2).to_broadcast([128, 8, 128])), ALU.mult)
                if s + 1 < 8 and (s + 1) % 4 == 0:
                    emit_outproj(s + 1)
            na += 8
            for hc in range(NHC):
                w = wgu[nw % 2]
                nw += 1
                b.dma("sp", w[:], wgu_b.k(hc)[hc])
                for th in range(2):
                    gt = G[ng % 2]
                    ng += 1
                    for j in range(2):
                        for kc in range(8):
                            b.mm(gt[:, j * 512:(j + 1) * 512], w[:, j, kc, :], xn2T[:, kc, th * 512:(th + 1) * 512],
                                 start=(kc == 0), stop=(kc == 7))
                    sgt = sg[(ng) % 2]
                    b.act(sgt[:], gt[:, 0:512], AF.Silu)
                    b.tt(hT[:, hc, th * 512:(th + 1) * 512], gt[:, 512:1024], sgt[:], ALU.mult)
            for s in range(8):
                r0 = t0 + s * 128
                st = _Cols(st_t, 32 + (s % 2) * 16)
                xr = x1r[s % 2]
                b.dma("sp", xr[:], V(x1d, x1d.t[r0:r0 + 128, :], r0))
                acc = A[na % 2]
                na += 1
                for half in range(2):
                    for hc in range(NHC):
                        b.mm(acc[:, half * 512:(half + 1) * 512], hT[:, hc, s * 128:(s + 1) * 128],
                             wd.k(hc)[:, hc, half * 512:(half + 1) * 512], start=(hc == 0), stop=(hc == NHC - 1))
                b.act(junk[:], acc[:], AF.Square, accum=st[:, 6:7])
                b.act(st[:, 7:8], st[:, 6:7], AF.Sqrt, bias=cst["eps"], scale=1.0 / D)
                b.recip(st[:, 8:9], st[:, 7:8])
                b.stt(tmp[:], acc[:], st[:, 8:9], g_fpost[:], ALU.mult, ALU.mult)
                b.tt(xr[:], tmp[:], xr[:], ALU.add)
                b.dma("sp", V(x_out, x_out.t[r0:r0 + 128, :], r0), xr[:])


NEG = -30000.0


def norm_to_T(k, x_in, gain_name, xnT, colf, ident, tagp=""):
    for _ in norm_to_T_gen(k, x_in, gain_name, xnT, colf, ident):
        pass


def norm_to_T_gen(k, x_in, gain_name, xnT, colf, ident, NPS=4):
    b = k.b
    g = b.sb("gpre", [128, 8])
    dma_nc(b, "sp", g[:], V(k.d[gain_name], k.d[gain_name].t.rearrange("(kc p) -> p kc", p=128)))
    NB = 4
    xin = [b.sb("nx%d" % i, [128, D]) for i in range(NB)]
    xs = [b.sb("nxs%d" % i, [128, D], BF16) for i in range(NB)]
    junks = [b.sb("njunk%d" % i, [128, D], BF16) for i in range(2)]
    st_t = b.sb("nst", [128, 16 * NB])
    P = [b.ps("nP%d" % i, [128, 512]) for i in range(NPS)]
    for s in range(T // 128):
        r0 = s * 128
        st = _Cols(st_t, (s % NB) * 16)
        xi = xin[s % NB]
        junk = junks[s % 2]
        b.dma("sp", xi[:], V(x_in, x_in.t[r0:r0 + 128, :]))
        b.act(junk[:], xi[:], AF.Square, accum=st[:, 0:1])
        b.act(st[:, 1:2], st[:, 0:1], AF.Sqrt, bias=EPS, scale=1.0 / D)
        b.recip(st[:, 2:3], st[:, 1:2])
        b.ts(xs[s % NB][:], xi[:], st[:, 2:3], ALU.mult)
        pt = P[s % NPS]
        ptb = pt.t[:, 0:512].bitcast(BF16)
        for kc in range(8):
            b.tr(V(pt, ptb[:, kc * 128:(kc + 1) * 128]), xs[s % NB][:, kc * 128:(kc + 1) * 128], ident[:])
        c0 = colf(r0)
        b.tt(xnT.k(s // 4)[:, :, c0:c0 + 128], V(pt, ptb.rearrange("p (kc t) -> p kc t", kc=8)),
             V(g, g.t[:, :].unsqueeze(2).to_broadcast([128, 8, 128])), ALU.mult)
        yield


def phase_l1(k, x_in, mixT1, cst):
    b = k.b
    nc = k.nc
    win_b = k.d["w_in1_b"]
    PADR = 1024
    Vd = k.dram("Vd", [PADR + T + PADR, 12 * 65], BF16)
    Nd = [k.dram("Nd%d" % g, [T, 260], F32) for g in range(3)]
    DIL = (1, 4, 16)
    with b.phase():
        ident = b.sb("ident", [128, 128], BF16)
        b.dma("pool", ident[:], cst["ident"])
        perm = b.sb("perm", [128, 128], BF16)
        b.dma("pool", perm[:], cst["perm"])
        xnT = b.sb("xnT1", [128, 8, T], BF16)
        with b.phase():
            norm_to_T(k, x_in, "mix_pre1", xnT, lambda t: t, ident)
        cosT = b.sb("cosT", [128, T])
        sinT = b.sb("sinT", [128, T])
        b.dma("sp", cosT[:], cst["cos"])
        b.dma("sp", sinT[:], cst["sin"])
        with b.phase():
            wv = b.sb("wv", [128, 8, 768], BF16)
            for kc in range(8):
                b.dma("sp", wv[:, kc, :], V(win_b, win_b.t[kc, :, 1536:2304]))
            z = b.sb("zpad", [128, 12 * 65], BF16)
            b.memset(z[:], 0.0)
            for i in range(PADR // 128):
                b.dma("sp", V(Vd, Vd.t[i * 128:(i + 1) * 128, :], "p%d" % i), z[:])
                b.dma("sp", V(Vd, Vd.t[PADR + T + i * 128:PADR + T + (i + 1) * 128, :], "q%d" % i), z[:])
            va = [b.sb("va%d" % i, [128, 12, 65], BF16) for i in range(2)]
            for i in range(2):
                b.memset(va[i][:], 1.0)
            Pv = [b.ps("Pv%d" % i, [128, 1024]) for i in range(2)]
            for s in range(T // 128):
                p = Pv[s % 2]
                for (c0, c1) in ((0, 512), (512, 768)):
                    for kc in range(8):
                        b.mm(p[:, c0:c1], xnT[:, kc, s * 128:(s + 1) * 128], wv[:, kc, c0:c1], start=(kc == 0), stop=(kc == 7))
                b.copy(va[s % 2][:, :, 0:64], V(p, p.t[:, 0:768].rearrange("p (h e) -> p h e", e=64)), eng="act")
                b.dma("sp", V(Vd, Vd.t[PADR + s * 128:PADR + (s + 1) * 128, :], s),
                      V(va[s % 2], va[s % 2].t[:].rearrange("p h e -> p (h e)")))
        for g in range(3):
            dil = DIL[g]
            L = T // dil
            NQ = L // 128
            with b.phase():
                wqk = b.sb("wqk", [128, 8, 2, 256], BF16)
                for kc in range(8):
                    b.dma("sp", wqk[:, kc, 0, :], V(win_b, win_b.t[kc, :, g * 256:(g + 1) * 256]))
                    b.dma("sp", wqk[:, kc, 1, :], V(win_b, win_b.t[kc, :, 768 + g * 256:768 + (g + 1) * 256]))
                QT = b.sb("QT", [128, 2, dil, L], BF16)
                KT = b.sb("KT", [128, 2, dil, L + 128], BF16)
                b.memset(KT[:], 0.0)
                t1 = [b.sb("t1_%d" % i, [128, 512]) for i in range(2)]
                t2 = [b.sb("t2_%d" % i, [128, 512]) for i in range(2)]
                qbf = [b.sb("qbf_%d" % i, [128, 512], BF16) for i in range(2)]
                with b.phase():
                    PA = [b.ps("PA%d" % i, [128, 512]) for i in range(2)]
                    PB = [b.ps("PB%d" % i, [128, 512]) for i in range(2)]
                    n = 0
                    for j in range(T // 512):
                        for a in range(2):
                            for mm in range(2):
                                pa, pb = PA[n % 2], PB[n % 2]
                                for kc in range(8):
                                    b.mm(pa[:], wqk[:, kc, a, mm * 128:(mm + 1) * 128], xnT[:, kc, j * 512:(j + 1) * 512],
                                         start=(kc == 0), stop=(kc == 7))
                                b.copy(qbf[n % 2][:], pa[:], eng="act")
                                b.mm(pb[:], perm[:], qbf[n % 2][:])
                                b.tt(t1[n % 2][:], pa[:], cosT[:, j * 512:(j + 1) * 512], ALU.mult, after=[qbf[n % 2][:]])
                                b.tt(t2[n % 2][:], pb[:], sinT[:, j * 512:(j + 1) * 512], ALU.mult)
                                w = 512 // dil
                                if a == 0:
                                    dst = V(QT, QT.t[:, mm, :, j * w:(j + 1) * w])
                                else:
                                    dst = V(KT, KT.t[:, mm, :, 64 + j * w:64 + (j + 1) * w])
                                b.tt(dst, V(t1[n % 2], t1[n % 2].t[:].rearrange("p (jl r) -> p r jl", r=dil)),
                                     V(t2[n % 2], t2[n % 2].t[:].rearrange("p (jl r) -> p r jl", r=dil)), ALU.add, eng="pool")
                                n += 1
                with b.phase():
                    mk = b.sb("mk", [128, 2, 256], BF16)
                    mkx = b.sb("mkx", [128, 2, 256], BF16)
                    for i in range(2):
                        b.dma("pool", mk[:, i, :], cst["maskAB"])
                        b.dma("pool", mkx[:, i, :], cst["maskX"])
                    vt = [b.sb("vt%d" % i, [128, 4, 65], BF16) for i in range(3)]
                    PT = [b.sb("PT%d" % i, [128, 4, 256], BF16) for i in range(2)]
                    osb = [b.sb("osb%d" % i, [128, 260]) for i in range(2)]
                    S = [b.ps("S%d" % i, [128, 1024]) for i in range(2)]
                    ACC = [b.ps("ACC%d" % i, [128, 512]) for i in range(2)]
                    it = 0

                    def geom(kt):
                        q0 = max(kt - 1, 0) * 128
                        q1 = min(kt + 1, NQ) * 128
                        return q0, q1, q1 - q0, (0 if kt > 0 else 128)

                    def emit_scores(rho, kt, it_):
                        sp = S[it_ % 2]
                        q0, q1, nq, m0 = geom(kt)
                        msk = mkx if kt == NQ // 2 else mk
                        for hh in range(4):
                            mm, pb = hh // 2, (hh % 2) * 64
                            b.mm(sp[:, hh * 256:hh * 256 + nq], ident[:], msk[:, 0, m0:m0 + nq], start=True, stop=False,
                                 skip_group_check=True)
                            b.mm(sp[:, hh * 256:hh * 256 + nq], KT[pb:pb + 64, mm, rho, kt * 128:(kt + 1) * 128],
                                 QT[pb:pb + 64, mm, rho, q0:q1], start=False, stop=True, skip_group_check=True)
                    iters = [(rho, kt) for rho in range(dil) for kt in range(NQ + 1)]
                    emit_scores(*iters[0], 0)
                    for rho in range(dil):
                        for kt in range(NQ + 1):
                            v = vt[it % 3]
                            row0 = PADR + dil * (128 * kt - 64) + rho
                            b.dma("sp", v[:], V(Vd, Vd.t[row0:row0 + 127 * dil + 1:dil, g * 260:(g + 1) * 260].rearrange("p (h e) -> p h e", e=65)))
                            sp = S[it % 2]
                            pt = PT[it % 2]
                            q0, q1, nq, m0 = geom(kt)
                            if it + 1 < len(iters):
                                emit_scores(*iters[it + 1], it + 1)
                            b.act(pt[:, :, 0:nq], V(sp, sp.t[:].rearrange("p (h c) -> p h c", c=256)[:, :, 0:nq]), AF.Exp, scale=0.125)
                            if kt > 0:
                                acc = ACC[(kt - 1) % 2]
                                for hh in range(4):
                                    b.mm(acc[:, hh * 65:(hh + 1) * 65], pt[:, hh, 0:128], v[:, hh, :], start=False, stop=True,
                                         skip_group_check=True)
                                o = osb[(kt - 1) % 2]
                                b.copy(o[:], acc[:, 0:260])
                                tok0 = dil * 128 * (kt - 1) + rho
                                b.dma("sp", V(Nd[g], Nd[g].t[tok0:tok0 + 127 * dil + 1:dil, :], (rho, kt - 1)), o[:])
                            if kt < NQ:
                                acc = ACC[kt % 2]
                                c0 = nq - 128
                                for hh in range(4):
                                    b.mm(acc[:, hh * 65:(hh + 1) * 65], pt[:, hh, c0:c0 + 128], v[:, hh, :], start=(hh == 0), stop=False,
                                         skip_group_check=True)
                            it += 1
        with b.phase():
            nt = [b.sb("nt%d" % i, [128, 3, 4, 65]) for i in range(4)]
            zt = b.sb("zt", [128, 32])
            yb = [b.sb("yb%d" % i, [128, 768], BF16) for i in range(4)]
            yT = [b.sb("yT%d" % i, [128, 6, 512], BF16) for i in range(2)]
            PTt = [b.ps("PTt%d" % i, [128, 512]) for i in range(4)]
            for s in range(T // 128):
                n_ = nt[s % 4]
                for g in range(3):
                    b.dma("sp", n_[:, g, :, :], V(Nd[g], Nd[g].t[s * 128:(s + 1) * 128, :].rearrange("p (h e) -> p h e", e=65)))
                zc = _Cols(zt, (s % 4) * 8)
                b.tt(V(zt, zt.t[:, (s % 4) * 8:(s % 4) * 8 + 4], (s % 4) * 8), n_[:, 0, :, 64], n_[:, 1, :, 64], ALU.add)
                b.tt(V(zt, zt.t[:, (s % 4) * 8:(s % 4) * 8 + 4], (s % 4) * 8), zc[:, 0:4], n_[:, 2, :, 64], ALU.add)
                b.recip(zc[:, 4:8], zc[:, 0:4])
                rzb = zt.t[:, (s % 4) * 8 + 4:(s % 4) * 8 + 8].unsqueeze(1).unsqueeze(3).to_broadcast([128, 3, 4, 64])
                b.tt(V(yb[s % 4], yb[s % 4].t[:].rearrange("p (g h e) -> p g h e", g=3, h=4)), n_[:, :, :, 0:64],
                     V(zt, rzb, (s % 4) * 8), ALU.mult)
                pt = PTt[s % 4]
                ptb = pt.t[:, 0:512].bitcast(BF16)
                for c in range(6):
                    b.tr(V(pt, ptb[:, c * 128:(c + 1) * 128]), yb[s % 4][:, c * 128:(c + 1) * 128], ident[:])
                y_ = yT[(s // 4) % 2]
                b.copy(y_[:, :, (s % 4) * 128:(s % 4) * 128 + 128], V(pt, ptb[:, 0:768].rearrange("p (c t) -> p c t", c=6)), eng="act")
                if s % 4 == 3:
                    for c in range(6):
                        b.dma("sp", V(mixT1, mixT1.t[c * 128:(c + 1) * 128, (s - 3) * 128:(s + 1) * 128], (c, s)), y_[:, c, :])


def colf0(t):
    return t + 1 + (2 if t >= 2048 else 0)


XW = T + 4


def l0_norm(k, x_in, xnT, ident, cst, do_norm=True):
    b = k.b
    if do_norm:
        with b.phase():
            norm_to_T(k, x_in, "mix_pre0", xnT, colf0, ident)
    flag = b.sb("flag", [128, 4])
    b.dma("sp", flag[:], cst["flag"])
    b.memset(xnT[:, :, 0:1], 0.0)
    b.memset(xnT[:, :, XW - 1:XW], 0.0)
    b.ts(xnT[:, :, 2049:2050], xnT[:, :, 2051:2052], flag[:, 0:1], ALU.mult)
    b.ts(xnT[:, :, 2050:2051], xnT[:, :, 2048:2049], flag[:, 0:1], ALU.mult)
    return flag


def l0_qkv(k, xnT, cst, QTd, KTd, Vad, x_in=None, ident=None):
    b = k.b
    win_b = k.d["w_in0_b"]
    with b.phase():
        ngen = norm_to_T_gen(k, x_in, "mix_pre0", xnT, colf0, ident, NPS=2) if x_in is not None else None

        def norm_steps(n):
            if ngen is None:
                return
            for _ in range(n):
                try:
                    next(ngen)
                except StopIteration:
                    return
        cosT = b.sb("cosT", [128, T])
        sinT = b.sb("sinT", [128, T])
        b.dma("sp", cosT[:], cst["cos"])
        b.dma("sp", sinT[:], cst["sin"])
        wqk = b.sb("wqk0", [128, 8, 1024], BF16)
        wv = b.sb("wv0", [128, 8, 512], BF16)
        for kc in range(8):
            b.dma("sp", wqk[:, kc, :], V(win_b, win_b.t[kc, :, 0:1024]))
            b.dma("sp", wv[:, kc, :], V(win_b, win_b.t[kc, :, 1024:1536]))
        perm = b.sb("perm0", [128, 128], BF16)
        b.dma("pool", perm[:], cst["perm"])
        qbf = [b.sb("qbf0_%d" % i, [128, 512], BF16) for i in range(2)]
        t1 = [b.sb("t1_%d" % i, [128, 512]) for i in range(2)]
        t2 = [b.sb("t2_%d" % i, [128, 512]) for i in range(2)]
        qst = [b.sb("qst%d" % i, [128, 512], BF16) for i in range(3)]
        va = [b.sb("va0_%d" % i, [128, 4, 129], BF16) for i in range(2)]
        for i in range(2):
            b.memset(va[i][:], 1.0)
        PA = [b.ps("PA%d" % i, [128, 512]) for i in range(2)]
        PB = [b.ps("PB%d" % i, [128, 512]) for i in range(2)]
        PV = [b.ps("PV%d" % i, [128, 512]) for i in range(2)]
        nctr = [0]

        def qkv_tile(j):
            c0 = colf0(j * 512)
            for a in range(2):
                for m in range(4):
                    n = nctr[0]
                    nctr[0] += 1
                    pa, pb = PA[n % 2], PB[n % 2]
                    col = a * 512 + m * 128
                    for kc in range(8):
                        b.mm(pa[:], wqk[:, kc, col:col + 128], xnT.k(j)[:, kc, c0:c0 + 512], start=(kc == 0), stop=(kc == 7))
                    b.copy(qbf[n % 2][:], pa[:], eng="act")
                    b.mm(pb[:], perm[:], qbf[n % 2][:])
                    b.tt(t1[n % 2][:], pa[:], cosT[:, j * 512:(j + 1) * 512], ALU.mult, after=[qbf[n % 2][:]])
                    b.tt(t2[n % 2][:], pb[:], sinT[:, j * 512:(j + 1) * 512], ALU.mult)
                    q = qst[n % 3]
                    b.tt(q[:], t1[n % 2][:], t2[n % 2][:], ALU.add, eng="pool")
                    dst = QTd if a == 0 else KTd
                    b.dma("sp", V(dst, dst.t[m * 128:(m + 1) * 128, j * 512:(j + 1) * 512], (m, j)), q[:])
                    yield
            for s4 in range(4):
                s = j * 4 + s4
                p = PV[s % 2]
                for kc in range(8):
                    b.mm(p[:], xnT.k(j)[:, kc, c0 + s4 * 128:c0 + (s4 + 1) * 128], wv[:, kc, :], start=(kc == 0), stop=(kc == 7))
                b.copy(va[s % 2][:, :, 0:128], V(p, p.t[:].rearrange("p (h e) -> p h e", e=128)), eng="act")
                b.dma("sp", V(Vad, Vad.t[s * 128:(s + 1) * 128, :], s), V(va[s % 2], va[s % 2].t[:].rearrange("p h e -> p (h e)")))
                yield
        norm_steps(4)
        for j in range(T // 512):
            for i_, _ in enumerate(qkv_tile(j)):
                if i_ % 3 == 1:
                    norm_steps(1)
        norm_steps(T // 128)


def l0_diffattn(k, cst, QTd, KTd, Vad, mixT0, ident, flag, co_setup=None):
    b = k.b
    with b.phase():
        QT = b.sb("QT0", [128, 4, T], BF16)
        KT = b.sb("KT0", [128, 4, T], BF16)
        VA = b.sb("VA0", [128, 32, 4 * 129], BF16)
        for m in range(4):
            for hh in range(2):
                b.dma("sp", QT.k(m)[:, m, hh * 2048:(hh + 1) * 2048], V(QTd, QTd.t[m * 128:(m + 1) * 128, hh * 2048:(hh + 1) * 2048]))
                b.dma("sp", KT.k(m)[:, m, hh * 2048:(hh + 1) * 2048], V(KTd, KTd.t[m * 128:(m + 1) * 128, hh * 2048:(hh + 1) * 2048]))
        for s in range(32):
            b.dma("sp", VA.k(s)[:, s, :], V(Vad, Vad.t[s * 128:(s + 1) * 128, :]))
        lv = b.sb("lv", [128, 4, 64])
        for i, nm in enumerate(("lam_q1", "lam_k1", "lam_q2", "lam_k2")):
            b.dma("sp", lv[:, i, :], V(k.d[nm], k.d[nm].t.partition_broadcast(128)))
        ls = b.sb("ls", [128, 8])
        lj = b.sb("lj", [128, 64])
        b.tt(lj[:], lv[:, 0, :], lv[:, 1, :], ALU.mult)
        b.reduce(ls[:, 0:1], lj[:])
        b.tt(lj[:], lv[:, 2, :], lv[:, 3, :], ALU.mult)
        b.reduce(ls[:, 1:2], lj[:])
        b.act(ls[:, 2:4], ls[:, 0:2], AF.Exp)
        b.tt(ls[:, 4:5], ls[:, 2:3], ls[:, 3:4], ALU.subtract)
        b.ts(ls[:, 5:6], ls[:, 4:5], -1.0, ALU.mult, -0.2, ALU.add)
        sw = b.sb("sw", [128, 128])
        b.dma("sp", sw[:], V(k.d["subln_w"], k.d["subln_w"].t.partition_broadcast(128)))
        b.ts(sw[:], sw[:], 0.8, ALU.mult)
        PT = [b.sb("PT0_%d" % i, [128, 1024], BF16) for i in range(3)]
        o1 = [b.sb("o1_%d" % i, [128, 128]) for i in range(2)]
        ob = [b.sb("ob_%d" % i, [128, 128], BF16) for i in range(2)]
        oj = b.sb("oj", [128, 128])
        aT = [b.sb("aT%d" % i, [128, 512], BF16) for i in range(2)]
        accs = [b.sb("accs%d" % i, [128, 3, 512]) for i in range(2)]
        st_t = b.sb("dst", [128, 64])
        S = [b.ps("S0_%d" % i, [128, 1024]) for i in range(2)]
        ACC = [b.ps("AC0_%d" % i, [128, 512]) for i in range(3)]
        TP = b.ps("TP0", [128, 512])
        it = 0
        nsub = 0
        cogens = co_setup(TP) if co_setup is not None else []

        def advance():
            for g_ in list(cogens):
                try:
                    next(g_)
                except StopIteration:
                    cogens.remove(g_)

        def emit_qk(h, qb, kt, it_):
            sp = S[it_ % 2]
            for c in range(2):
                b.mm(sp[:, c * 512:(c + 1) * 512], KT.k(h)[c * 64:(c + 1) * 64, h, kt * 128:(kt + 1) * 128],
                     QT.k(h)[c * 64:(c + 1) * 64, h, qb * 512:(qb + 1) * 512], start=True, stop=True)
        iters = [(h, qb, kt) for h in range(4) for qb in range(8) for kt in range(32)]
        emit_qk(*iters[0], 0)
        for h in range(4):
            for qb in range(8):
                for kt in range(32):
                    sp = S[it % 2]
                    pt = PT[it % 3]
                    if it + 1 < len(iters):
                        emit_qk(*iters[it + 1], it + 1)
                    cross = (kt < 16) != (qb < 4)
                    if cross:
                        b.act(pt[:], sp[:], AF.Exp, scale=0.125, bias=flag[:, 1:2])
                    else:
                        b.act(pt[:], sp[:], AF.Exp, scale=0.125)
                    for c in range(2):
                        for qs in range(4):
                            gi = c * 4 + qs
                            acc = ACC[gi // 3]
                            co = (gi % 3) * 129
                            b.mm(acc[:, co:co + 129], pt[:, c * 512 + qs * 128:c * 512 + (qs + 1) * 128], VA.k(kt)[:, kt, h * 129:(h + 1) * 129],
                                 start=(kt == 0 and gi % 3 == 0), stop=(kt == 31), skip_group_check=True)
                    it += 1
                    if it % CO_EVERY == 0:
                        advance()
                asb = accs[(h * 8 + qb) % 2]
                for i_ in range(3):
                    w_ = 387 if i_ < 2 else 258
                    b.copy(asb[:, i_, 0:w_], ACC[i_][:, 0:w_], eng="dve")
                tp = TP
                tpb = tp.t[:, 0:256].bitcast(BF16)
                for qs in range(4):
                    st = _Cols(st_t, (nsub % 2) * 16)
                    a0 = _Bank(asb, qs // 3)
                    c0 = (qs % 3) * 129
                    a1 = _Bank(asb, (4 + qs) // 3)
                    c1 = ((4 + qs) % 3) * 129
                    b.recip(st[:, 0:1], a0[:, c0 + 128:c0 + 129])
                    b.recip(st[:, 1:2], a1[:, c1 + 128:c1 + 129])
                    b.tt(st[:, 2:3], st[:, 1:2], ls[:, 5:6], ALU.mult)
                    o = o1[nsub % 2]
                    b.ts(o[:], a0[:, c0:c0 + 128], st[:, 0:1], ALU.mult)
                    b.stt(o[:], a1[:, c1:c1 + 128], st[:, 2:3], o[:], ALU.mult, ALU.add)
                    b.act(oj[:], o[:], AF.Square, accum=st[:, 3:4])
                    b.act(st[:, 4:5], st[:, 3:4], AF.Sqrt, bias=1e-5, scale=1.0 / 128)
                    b.recip(st[:, 5:6], st[:, 4:5])
                    obf = ob[nsub % 2]
                    b.stt(obf[:], o[:], st[:, 5:6], sw[:], ALU.mult, ALU.mult)
                    b.tr(V(tp, tpb[:, qs * 128:(qs + 1) * 128]), obf[:], ident[:])
                    nsub += 1
                at = aT[(h * 8 + qb) % 2]
                b.copy(at[:], V(tp, tpb), eng="act")
                b.dma("sp", V(mixT0, mixT0.t[h * 128:(h + 1) * 128, qb * 512:(qb + 1) * 512], (h, qb)), at[:])
        while cogens:
            advance()


CDEC = 0.6065306597126334
NTL = 2
STAGGER = 0
CO_EVERY = 2


def l0_rwproj(k, xnT, cst, rwd):
    b = k.b
    d = k.d
    with b.phase():
        W1 = b.sb("W1", [128, 8, 1536], BF16)
        W2 = b.sb("W2", [128, 8, 1536], BF16)
        La = b.sb("La", [128, 8, 416], BF16)
        Lh = b.sb("Lh", [128, 8, 416], BF16)
        L2a = b.sb("L2a", [128, 512], BF16)
        L2b = b.sb("L2b", [128, 512], BF16)
        L2g = b.sb("L2g", [128, 512], BF16)
        L2g2 = b.sb("L2g2", [32, 512], BF16)
        b.dma("pool", L2a[0:64, :], d["w2_f"][:])
        b.dma("pool", L2a[64:128, :], d["w2_b"][:])
        b.dma("pool", L2b[0:64, :], d["a2_f"][:])
        b.dma("pool", L2b[64:128, :], d["a2_b"][:])
        b.dma("pool", L2g[:], V(d["g2"], d["g2"].t[0:128, :]))
        b.dma("pool", L2g2[:], V(d["g2"], d["g2"].t[128:160, :]))
        with b.phase():
            mub = b.sb("mub", [128, 1536])
            for i, nm in enumerate(("mu_r", "mu_k", "mu_v")):
                b.dma("sp", mub[:, i * 512:(i + 1) * 512], V(d[nm], d[nm].t.partition_broadcast(128)))
            omm = b.sb("omm", [128, 1536])
            hm = b.sb("hm", [128, 1536])
            b.ts(omm[:], mub[:], -1.0, ALU.mult, 1.0, ALU.add)
            b.ts(hm[:], mub[:], 0.5, ALU.mult)
            mus = b.sb("mus", [128, 3, 8])
            for i, nm in enumerate(("mu_w", "mu_a", "mu_g")):
                dma_nc(b, "sp", mus[:, i, :], V(d[nm], d[nm].t.rearrange("(kc p) -> p kc", p=128)))
            omm3 = b.sb("omm3", [128, 3, 8])
            hm3 = b.sb("hm3", [128, 3, 8])
            b.ts(omm3[:], mus[:], -1.0, ALU.mult, 1.0, ALU.add)
            b.ts(hm3[:], mus[:], 0.5, ALU.mult)
            wf = [b.sb("wf%d" % i, [128, 1536]) for i in range(2)]
            lf = [b.sb("lf%d" % i, [128, 416]) for i in range(2)]
            win = d["w_in0"]
            for kc in range(8):
                w = wf[kc % 2]
                b.dma("sp", w[:], V(win, win.t[kc * 128:(kc + 1) * 128, 1536:3072]))
                b.tt(W1[:, kc, :], w[:], omm[:], ALU.mult)
                b.tt(W2[:, kc, :], w[:], hm[:], ALU.mult, eng="pool")
                l = lf[kc % 2]
                for (nm, c0, cw) in (("w1_f", 0, 64), ("w1_b", 64, 64), ("a1_f", 128, 64), ("a1_b", 192, 64), ("g1", 256, 160)):
                    b.dma("sp", l[:, c0:c0 + cw], V(d[nm], d[nm].t[kc * 128:(kc + 1) * 128, :]))
                for gi, (c0, c1) in enumerate(((0, 128), (128, 256), (256, 416))):
                    b.ts(La[:, kc, c0:c1], l[:, c0:c1], omm3[:, gi, kc:kc + 1], ALU.mult)
                    b.ts(Lh[:, kc, c0:c1], l[:, c0:c1], hm3[:, gi, kc:kc + 1], ALU.mult)
        xsh = [b.sb("xsh%d" % i, [128, 8, 512], BF16) for i in range(2)]
        h1 = [[b.sb("h1_%d_%d" % (i, g), [128, 512], BF16) for g in range(4)] for i in range(2)]
        rw = [b.sb("rw%d" % i, [128, 8, 512]) for i in range(2)]
        PL = [b.ps("PL%d" % i, [128, 512]) for i in range(2)]
        PT_ = [b.ps("PTk%d" % i, [128, 512]) for i in range(4)]
        npl = 0
        npt = 0
        for j in range(T // 512):
            c0 = colf0(j * 512)
            xs = xsh[j % 2]
            b.tt(xs[:], xnT[:, :, c0 - 1:c0 + 511], xnT[:, :, c0 + 1:c0 + 513], ALU.add, eng="pool")
            hh = h1[j % 2]
            for gi, (r0, nr, fn) in enumerate(((0, 128, AF.Tanh), (128, 128, AF.Copy), (256, 128, AF.Sigmoid), (384, 32, AF.Sigmoid))):
                p = PL[npl % 2]
                npl += 1
                for kc in range(8):
                    b.mm(p[0:nr, :], La[:, kc, r0:r0 + nr], xnT[:, kc, c0:c0 + 512], start=(kc == 0), stop=False)
                for kc in range(8):
                    b.mm(p[0:nr, :], Lh[:, kc, r0:r0 + nr], xs[:, kc, :], start=False, stop=(kc == 7))
                if fn == AF.Copy:
                    b.copy(hh[gi][0:nr, :], p[0:nr, :], eng="act")
                else:
                    b.act(hh[gi][0:nr, :], p[0:nr, :], fn)
            for s4 in range(4):
                s = j * 4 + s4
                r = rw[s % 2]
                cs = c0 + s4 * 128
                ts_ = slice(s4 * 128, (s4 + 1) * 128)
                for q in range(8):
                    p = PT_[npt % 4]
                    npt += 1
                    if q < 3:
                        for kc in range(8):
                            b.mm(p[:], xnT[:, kc, cs:cs + 128], W1[:, kc, q * 512:(q + 1) * 512], start=(kc == 0), stop=False)
                        for kc in range(8):
                            b.mm(p[:], xs[:, kc, ts_], W2[:, kc, q * 512:(q + 1) * 512], start=False, stop=(kc == 7))
                    elif q == 3:
                        b.mm(p[:], hh[0][0:64, ts_], L2a[0:64, :])
                    elif q == 4:
                        b.mm(p[:], hh[0][64:128, ts_], L2a[64:128, :])
                    elif q == 5:
                        b.mm(p[:], hh[1][0:64, ts_], L2b[0:64, :])
                    elif q == 6:
                        b.mm(p[:], hh[1][64:128, ts_], L2b[64:128, :])
                    else:
                        b.mm(p[:], hh[2][:, ts_], L2g[:], start=True, stop=False)
                        b.mm(p[:], hh[3][0:32, ts_], L2g2[0:32, :], start=False, stop=True)
                    b.copy(r[:, q, :], p[:], eng=("act" if q % 2 == 0 else "dve"))
                b.dma("sp", V(rwd, rwd.t[s * 128:(s + 1) * 128, :, :], s), r[:])


def l0_rwkv(k, cst, rwd, mixT0, ident, Yd=None, flag=None, stage=9, do_post=True):
    b = k.b
    d = k.d
    if Yd is None:
        Yd = k.dram("Yd", [2, T, 512], F32)
    NT = T // 128
    with b.phase():
        def bc(nm):
            t = b.sb("bc_" + nm, [128, 512])
            src = d[nm].t
            if len(src.shape) == 2:
                src = src.rearrange("h n -> (h n)")
            b.dma("sp", t[:], V(d[nm], src.partition_broadcast(128)))
            return t
        w0 = [bc("w0_f"), bc("w0_b")]
        a0 = [bc("a0_f"), bc("a0_b")]
        kkb = bc("k_k")
        kab = bc("k_a")
        omka = b.sb("omka", [128, 512])
        b.ts(omka[:], kab[:], -1.0, ALU.mult, 1.0, ALU.add)
        tri = b.sb("tri", [128, 6, 128])
        b.dma("sp", tri[:], cst["tri"])
        irep = b.sb("irep", [128, 512])
        b.dma("sp", irep[:], cst["irep"])
        SU, IU, SL, IL = 0, 1, 2, 3
        M4 = []
        MQ = []
        for dr in range(2):
            s_, i_, sp_ = (SU, IU, SL) if dr == 0 else (SL, IL, SU)
            m4 = b.sb("M4_%d" % dr, [128, 4, 128], BF16)
            mq = b.sb("MQ_%d" % dr, [128, 4, 128], BF16)
            for j in range(4):
                b.copy(m4[:, j, :], tri[:, (s_ if j % 2 == 0 else i_), :], eng="pool")
                b.copy(mq[:, j, :], tri[:, sp_, :], eng="pool")
            M4.append(m4)
            MQ.append(mq)
        CS = [(IU, SU, SL), (IL, SL, SU)]
        GS = [[b.ps("Gp%d_%d" % (d_, i), [128, 512]) for i in range(3)] for d_ in range(2)]
        PYS = [b.ps("PY%d" % i, [128, 512]) for i in range(2)]
        gcnt = [0, 0]

        class DirState:
            pass
        DS = []
        for dr in range(2):
            s = DirState()
            s.rwb = [b.sb("rw%d_%d" % (i, dr), [128, 5, 512]) for i in range(2)]
            s.f = [b.sb("f%d_%d" % (i, dr), [128, 512]) for i in range(8)]
            s.st = b.sb("st_%d" % dr, [128, 32])
            s.h16 = {nm: b.sb("%s_%d" % (nm, dr), [128, 512], BF16) for nm in ("Rt", "Kt", "Bt", "Kp", "Kh", "Bh", "v16", "AV")}
            s.U = [b.sb("U%d_%d" % (c, dr), [128, 512], BF16) for c in range(2)]
            s.vz = [b.sb("vz%d_%d" % (c, dr), [128, 512], BF16) for c in range(2)]
            s.RTz = b.sb("RTz_%d" % dr, [128, 8, 128], BF16)
            for c in range(2):
                b.memset(s.U[c][:], 0.0)
                b.memset(s.vz[c][:], 0.0)
            b.memset(s.RTz[:], 0.0)
            s.Dg = [b.sb("Dg%d_%d" % (c, dr), [128, 512]) for c in range(2)]
            s.XT = b.sb("XT_%d" % dr, [128, 4, 4, 128], BF16)
            s.AM = b.sb("AM_%d" % dr, [128, 8, 4, 128], BF16)
            s.P = [b.sb("P%d_%d" % (i, dr), [128, 8, 128], BF16) for i in range(2)]
            s.PT = [b.sb("PT%d_%d" % (i, dr), [128, 8, 128], BF16) for i in range(2)]
            s.S = [b.sb("S%d_%d" % (i, dr), [128, 8, 128], BF16) for i in range(2)]
            s.WT = b.sb("WT_%d" % dr, [128, 8, 128], BF16)
            b.memset(s.WT[:], 0.0)
            s.H32 = [b.sb("H32_%d_%d" % (i, dr), [128, 4, 64]) for i in range(2)]
            s.H16 = [b.sb("H16_%d_%d" % (i, dr), [128, 4, 64], BF16) for i in range(2)]
            s.hi = 0
            s.Y = b.sb("Yt_%d" % dr, [128, 512])
            b.memset(s.H32[0][:], 0.0)
            b.memset(s.H16[0][:], 0.0)
            DS.append(s)

        def v3(view_tile, ap):
            return V(view_tile, ap.rearrange("p (h n) -> p h n", n=64))

        tcount = [0, 0]

        def load_rw(dr, ti, dst):
            rows = slice(ti * 128, (ti + 1) * 128)
            b.dma("sp", dst[:, 0:3, :], V(rwd, rwd.t[rows, 0:3, :]))
            b.dma("sp", dst[:, 3, :], V(rwd, rwd.t[rows, 3 + dr, :]))
            b.dma("sp", dst[:, 4, :], V(rwd, rwd.t[rows, 5 + dr, :]))

        def rw_tile(dr, ti):
            s = DS[dr]
            rw = s.rwb[tcount[dr] % 2]
            f = s.f
            h = s.h16
            st = s.st
            PY = PYS[dr]

            def gp():
                gcnt[dr] += 1
                return GS[dr][gcnt[dr] % 3]
            if tcount[dr] == 0:
                load_rw(dr, ti, rw)
            tn = ti + 1 if dr == 0 else ti - 1
            if 0 <= tn < NT:
                load_rw(dr, tn, s.rwb[(tcount[dr] + 1) % 2])
            tcount[dr] += 1
            yield
            r_, k_, v_ = rw[:, 0, :], rw[:, 1, :], rw[:, 2, :]
            b.tt(f[0][:], rw[:, 3, :], w0[dr][:], ALU.add)
            yield
            b.act(f[0][:], f[0][:], AF.Sigmoid)
            yield
            b.tt(f[1][:], rw[:, 4, :], a0[dr][:], ALU.add, eng="pool")
            yield
            b.act(f[1][:], f[1][:], AF.Sigmoid)
            yield
            b.tt(f[2][:], k_, kkb[:], ALU.mult)
            yield
            b.act(f[3][:], f[2][:], AF.Square)
            yield
            b.reduce(st[:, 0:8], v3(f[3], f[3].t[:]))
            yield
            b.act(st[:, 8:16], st[:, 0:8], AF.Sqrt)
            yield
            b.ts(st[:, 8:16], st[:, 8:16], 1e-12, ALU.max)
            yield
            b.recip(st[:, 16:24], st[:, 8:16])
            yield
            b.tt(v3(f[2], f[2].t[:]), v3(f[2], f[2].t[:]),
                 V(st, st.t[:, 16:24].unsqueeze(2).to_broadcast([128, 8, 64])), ALU.mult)
            yield
            b.tt(f[3][:], f[1][:], kab[:], ALU.mult, eng="pool")
            yield
            b.tt(f[3][:], f[3][:], omka[:], ALU.add, eng="pool")
            yield
            b.tt(f[3][:], f[3][:], k_, ALU.mult, eng="pool")
            yield
            b.stt(f[4][:], f[2][:], -1.0, f[1][:], ALU.mult, ALU.mult)
            yield
            b.copy(h["v16"][:], v_, eng="act")
            yield
            for c in range(2):
                b.copy(s.vz[c][c * 64:(c + 1) * 64, :], rw[c * 64:(c + 1) * 64, 2, :], eng="act")
                yield
            ci, ce, ca = CS[dr]
            pi = gp()
            b.mm(pi[:], tri[:, ci, :], f[0][:])
            yield
            b.act(f[5][:], pi[:], AF.Exp, scale=-CDEC)
            yield
            b.act(f[6][:], pi[:], AF.Exp, scale=CDEC)
            yield
            b.tt(h["Rt"][:], r_, f[5][:], ALU.mult)
            yield
            b.tt(h["Kt"][:], f[3][:], f[6][:], ALU.mult)
            yield
            b.tt(h["Bt"][:], f[4][:], f[6][:], ALU.mult)
            yield
            pe = gp()
            b.mm(pe[:], tri[:, ce, :], f[0][:])
            yield
            b.act(f[5][:], pe[:], AF.Exp, scale=-CDEC)
            yield
            b.tt(h["Kp"][:], f[2][:], f[5][:], ALU.mult)
            yield
            pa = gp()
            b.mm(pa[:], tri[:, ca, :], f[0][:])
            yield
            b.act(f[6][:], pa[:], AF.Exp, scale=-CDEC)
            yield
            b.tt(h["Kh"][:], f[3][:], f[6][:], ALU.mult, eng="pool")
            yield
            b.tt(h["Bh"][:], f[4][:], f[6][:], ALU.mult, eng="pool")
            yield
            p0 = gp()
            b.mm(p0[:], tri[:, 4, :], f[0][:])
            yield
            b.act(f[5][:], p0[:], AF.Exp, scale=-CDEC)
            yield
            b.tt(s.Dg[0][:], f[5][:], irep[:], ALU.mult, eng="pool")
            yield
            p1 = gp()
            b.mm(p1[:], tri[:, 5, :], f[0][:])
            yield
            b.act(f[7][:], p1[:], AF.Exp, scale=-CDEC)
            yield
            b.tt(s.Dg[1][:], f[7][:], irep[:], ALU.mult, eng="pool")
            yield
            for half in range(2):
                pt = gp()
                ptb = pt.t[:, 0:512].bitcast(BF16)
                for qi2 in range(2):
                    qi = half * 2 + qi2
                    src = h[("Kt", "Bt", "Kp", "Rt")[qi]]
                    for hp in range(4):
                        b.tr(V(pt, ptb[:, (qi2 * 4 + hp) * 128:(qi2 * 4 + hp + 1) * 128]), src[:, hp * 128:(hp + 1) * 128], ident[:])
                        yield
                for qi2 in range(2):
                    b.copy(V(s.XT, s.XT.t[:, :, half * 2 + qi2, :]),
                           V(pt, ptb[:, qi2 * 512:(qi2 + 1) * 512].rearrange("p (hp t) -> p hp t", hp=4)), eng=("act" if qi2 == 0 else "dve"))
                    yield
            b.copy(V(s.RTz, s.RTz.t[0:64, 0:8:2, :]), s.XT[0:64, :, 3, :], eng="act")
            yield
            b.copy(V(s.RTz, s.RTz.t[64:128, 1:8:2, :]), s.XT[64:128, :, 3, :], eng="act")
            yield
            if stage < 2:
                return
            pq = None
            for hd in range(8):
                hp, pb = hd // 2, (hd % 2) * 64
                p12 = gp()
                rhs = V(s.XT, s.XT.t[pb:pb + 64, hp, 2:4, :].rearrange("p a t -> p (a t)"))
                b.mm(p12[:, 0:256], s.XT[pb:pb + 64, hp, 0, :], rhs)
                yield
                b.mm(p12[:, 256:512], s.XT[pb:pb + 64, hp, 1, :], rhs)
                yield
                if stage >= 2.2:
                    b.tt(V(s.AM, s.AM.t[:, hd, :, :]), V(p12, p12.t[:].rearrange("p (a t) -> p a t", a=4)), M4[dr][:], ALU.mult)
                    yield
            for par in range(2):
                pq = gp()
                pb = par * 64
                for j in range(4):
                    hd = 2 * j + par
                    b.mm(pq[:, j * 128:(j + 1) * 128], s.XT[pb:pb + 64, j, 2, :], s.XT[pb:pb + 64, j, 1, :])
                    yield
                b.copy(s.f[7][:], pq[:], eng="act")
                yield
                b.tt(V(s.PT[0], s.PT[0].t[:, par:8:2, :]), V(s.f[7], s.f[7].t[:].rearrange("p (a t) -> p a t", a=4)),
                     MQ[dr][:], ALU.mult, eng="pool")
                yield
            if stage < 3:
                return
            b.copy(s.P[0][:], V(s.AM, s.AM.t[:, :, 2, :]), eng="act")
            yield
            b.tt(s.S[0][:], V(s.AM, s.AM.t[:, :, 2, :]), V(ident, ident.t[:].unsqueeze(1).to_broadcast([128, 8, 128])), ALU.add, eng="pool")
            yield
            cur = 0
            for lev in range(1, 6):
                nxt = 1 - cur
                for g4 in range(2):
                    hs = range(g4 * 4, g4 * 4 + 4)
                    if lev < 5:
                        p = gp()
                        for hd in hs:
                            b.mm(p[:, (hd % 4) * 128:(hd % 4 + 1) * 128], s.PT[cur][:, hd, :], s.P[cur][:, hd, :])
                            yield
                        b.copy(V(s.P[nxt], s.P[nxt].t[:, g4 * 4:g4 * 4 + 4, :]), V(p, p.t[:].rearrange("p (a t) -> p a t", a=4)), eng="act")
                        yield
                    p = gp()
                    for hd in hs:
                        b.mm(p[:, (hd % 4) * 128:(hd % 4 + 1) * 128], s.P[cur][:, hd, :], s.PT[cur][:, hd, :])
                        yield
                    b.copy(V(s.PT[nxt], s.PT[nxt].t[:, g4 * 4:g4 * 4 + 4, :]), V(p, p.t[:].rearrange("p (a t) -> p a t", a=4)),
                           eng=("act" if g4 == 0 else "dve"))
                    yield
                for g4 in range(2):
                    hs = range(g4 * 4, g4 * 4 + 4)
                    p = gp()
                    for hd in hs:
                        o = p[:, (hd % 4) * 128:(hd % 4 + 1) * 128]
                        b.mm(o, s.PT[nxt][:, hd, :], s.S[cur][:, hd, :])
                        yield
                    b.tt(V(s.S[nxt], s.S[nxt].t[:, g4 * 4:g4 * 4 + 4, :]), V(p, p.t[:].rearrange("p (a t) -> p a t", a=4)),
                         V(s.S[cur], s.S[cur].t[:, g4 * 4:g4 * 4 + 4, :]), ALU.add)
                    yield
                cur = nxt
            TT = s.S[cur]
            if stage < 4:
                return
            p = gp()
            for hd in range(8):
                b.mm(p[:, hd * 64:(hd + 1) * 64], s.AM[:, hd, 0, :], h["v16"][:, hd * 64:(hd + 1) * 64])
                yield
            b.copy(h["AV"][:], p[:], eng="act")
            yield
            p = gp()
            for hd in range(8):
                hp, pb = hd // 2, (hd % 2) * 64
                b.mm(p[pb:pb + 64, hp * 128:(hp + 1) * 128], h["Kp"][:, hd * 64:(hd + 1) * 64], TT[:, hd, :])
                yield
            b.copy(V(s.WT, s.WT.t[0:64, 0:8:2, :]), V(p, p.t[0:64, :].rearrange("p (a t) -> p a t", a=4)), eng="act")
            yield
            b.copy(V(s.WT, s.WT.t[64:128, 1:8:2, :]), V(p, p.t[64:128, :].rearrange("p (a t) -> p a t", a=4)), eng="act")
            yield
            if stage < 5:
                return
            if (dr == 0 and ti == NT // 2) or (dr == 1 and ti == NT // 2 - 1):
                b.ts(s.H32[s.hi][:], s.H32[s.hi][:], flag[:, 0:1], ALU.mult)
                yield
                b.ts(s.H16[s.hi][:], s.H16[s.hi][:], flag[:, 0:1], ALU.mult)
                yield
            for c in ((0, 1) if dr == 0 else (1, 0)):
                cb = c * 64
                cs = slice(cb, cb + 64)
                Ho32, Ho16 = s.H32[s.hi], s.H16[s.hi]
                Hn32, Hn16 = s.H32[1 - s.hi], s.H16[1 - s.hi]
                s.hi = 1 - s.hi
                PH = gp()
                for hd in range(8):
                    hp, pb = hd // 2, (hd % 2) * 64
                    hc = slice(hd * 64, (hd + 1) * 64)
                    o = PH[pb:pb + 64, hp * 64:(hp + 1) * 64]
                    b.mm(o, s.Dg[c][:, hc], Ho32[:, hp, :], start=(hd < 2), stop=False, skip_group_check=True)
                    yield
                    b.mm(o, h["Kh"][:, hc], s.vz[c][:, hc], start=False, stop=False, skip_group_check=True)
                    yield
                PU = gp()
                for hd in range(8):
                    hp = hd // 2
                    hc = slice(hd * 64, (hd + 1) * 64)
                    o = PU[cs, hc]
                    b.mm(o, TT[:, hd, cs], h["AV"][:, hc], start=True, stop=False, skip_group_check=True)
                    yield
                    b.mm(o, s.WT[:, hd, cs], Ho16[:, hp, :], start=False, stop=True, skip_group_check=True)
                    yield
                b.copy(s.U[c][cs, :], PU[cs, :], eng="act")
                yield
                for hd in range(8):
                    hp, pb = hd // 2, (hd % 2) * 64
                    hc = slice(hd * 64, (hd + 1) * 64)
                    o = PH[pb:pb + 64, hp * 64:(hp + 1) * 64]
                    b.mm(o, h["Bh"][:, hc], s.U[c][:, hc], start=False, stop=True, skip_group_check=True)
                    yield
                b.copy(V(Hn32, Hn32.t[:].rearrange("p a n -> p (a n)")), PH[:, 0:256], eng="act")
                yield
                b.copy(Hn16[:], Hn32[:], eng="pool")
                yield
                for hd in range(8):
                    hp = hd // 2
                    hc = slice(hd * 64, (hd + 1) * 64)
                    o = PY[cs, hc]
                    b.mm(o, s.RTz[:, hd, cs], Ho16[:, hp, :], start=True, stop=False, skip_group_check=True)
                    yield
                    b.mm(o, s.AM[:, hd, 1, cs], h["v16"][:, hc], start=False, stop=False, skip_group_check=True)
                    yield
                    b.mm(o, s.AM[:, hd, 3, cs], s.U[c][:, hc], start=False, stop=True, skip_group_check=True)
                    yield
            b.copy(s.Y[:], PY[:], eng="act")
            yield
            b.dma("sp", V(Yd, Yd.t[dr, ti * 128:(ti + 1) * 128, :], (dr, ti)), s.Y[:])
            yield

        nloop = NT if stage >= 9 else min(NT, NTL)

        def stream(dr):
            for i in range(nloop):
                yield from rw_tile(dr, i if dr == 0 else NT - 1 - i)
        alive = [stream(0), stream(1)]
        for _ in range(STAGGER):
            next(alive[0])
        while alive:
            for g_ in list(alive):
                try:
                    next(g_)
                except StopIteration:
                    alive.remove(g_)
    if stage < 9:
        return

    if not do_post:
        return
    with b.phase():
        gens = rwkv_post_setup(k, rwd, Yd, mixT0, ident, None, 4, "pool")
        while gens:
            for g_ in list(gens):
                try:
                    next(g_)
                except StopIteration:
                    gens.remove(g_)


def rwkv_post_setup(k, rwd, Yd, mixT0, ident, TP, NS, e2):
    b = k.b
    d = k.d
    NT = T // 128
    if True:
        def bc2(nm):
            t = b.sb("bc_" + nm, [128, 512])
            src = d[nm].t
            if len(src.shape) == 2:
                src = src.rearrange("h n -> (h n)")
            b.dma("sp", t[:], V(d[nm], src.partition_broadcast(128)))
            return t
        a0 = [bc2("a0_f"), bc2("a0_b")]
        kab = bc2("k_a")
        rkb = bc2("r_k")
        lw = bc2("lnx_w")
        lb = bc2("lnx_b")
        omka = b.sb("omka2", [128, 512])
        b.ts(omka[:], kab[:], -1.0, ALU.mult, 1.0, ALU.add)
        b.ts(kab[:], kab[:], 0.5, ALU.mult)
        rws = [b.sb("rwp%d" % i, [128, 8, 512]) for i in range(NS)]
        ys = [b.sb("yp%d" % i, [128, 2, 512]) for i in range(NS)]
        fs = [[b.sb("pf%d_%d" % (i, j), [128, 512]) for i in range(5)] for j in range(NS)]
        st_t = b.sb("pst", [128, 32 * NS])
        ob = [b.sb("pob%d" % i, [128, 512], BF16) for i in range(NS)]
        oT = [b.sb("poT%d" % i, [128, 4, 128], BF16) for i in range(NS)]
        PTt = [TP] * NS if TP is not None else [b.ps("PTp%d" % i, [128, 512]) for i in range(NS)]

        def v3(view_tile, ap):
            return V(view_tile, ap.rearrange("p (h n) -> p h n", n=64))

        def bc8(tile, c0, key):
            return V(tile, tile.t[:, c0:c0 + 8].unsqueeze(2).to_broadcast([128, 8, 64]), key)

        def post_tile(j, ti):
            rw = rws[j]
            yy = ys[j]
            f = fs[j]
            off = j * 32
            st = _Cols(st_t, off)
            b.dma("sp", rw[:, 0:3, :], V(rwd, rwd.t[ti * 128:(ti + 1) * 128, 0:3, :]))
            b.dma("sp", rw[:, 5:8, :], V(rwd, rwd.t[ti * 128:(ti + 1) * 128, 5:8, :]))
            for dr in range(2):
                b.dma("sp", yy[:, dr, :], V(Yd, Yd.t[dr, ti * 128:(ti + 1) * 128, :]))
            yield
            y = f[0]
            b.tt(y[:], yy[:, 0, :], yy[:, 1, :], ALU.add)
            yield
            b.reduce(st[:, 0:8], v3(y, y.t[:]))
            yield
            b.ts(st[:, 0:8], st[:, 0:8], 1.0 / 64, ALU.mult)
            yield
            b.tt(v3(y, y.t[:]), v3(y, y.t[:]), bc8(st_t, off, off), ALU.subtract)
            yield
            b.tt(f[1][:], y[:], y[:], ALU.mult)
            yield
            b.reduce(st[:, 8:16], v3(f[1], f[1].t[:]))
            yield
            b.act(st[:, 16:24], st[:, 8:16], AF.Sqrt, bias=64e-5, scale=1.0 / 64)
            yield
            b.recip(st[:, 24:32], st[:, 16:24])
            yield
            b.tt(v3(y, y.t[:]), v3(y, y.t[:]), bc8(st_t, off + 24, off), ALU.mult)
            yield
            b.tt(y[:], y[:], lw[:], ALU.mult, eng=e2)
            yield
            b.tt(y[:], y[:], lb[:], ALU.add, eng=e2)
            yield
            b.tt(f[2][:], rw[:, 5, :], a0[0][:], ALU.add, eng=e2)
            yield
            b.act(f[2][:], f[2][:], AF.Exp, scale=-1.0)
            yield
            b.ts(f[2][:], f[2][:], 1.0, ALU.add)
            yield
            b.recip(f[2][:], f[2][:])
            yield
            b.tt(f[3][:], rw[:, 6, :], a0[1][:], ALU.add, eng=e2)
            yield
            b.act(f[3][:], f[3][:], AF.Exp, scale=-1.0)
            yield
            b.ts(f[3][:], f[3][:], 1.0, ALU.add)
            yield
            b.recip(f[3][:], f[3][:])
            yield
            b.tt(f[2][:], f[2][:], f[3][:], ALU.add, eng=e2)
            yield
            b.tt(f[2][:], f[2][:], kab[:], ALU.mult, eng=e2)
            yield
            b.tt(f[2][:], f[2][:], omka[:], ALU.add, eng=e2)
            yield
            b.tt(f[2][:], f[2][:], rw[:, 1, :], ALU.mult)
            yield
            b.tt(f[2][:], f[2][:], rw[:, 0, :], ALU.mult)
            yield
            b.tt(f[2][:], f[2][:], rkb[:], ALU.mult)
            yield
            b.reduce(st[:, 8:16], v3(f[2], f[2].t[:]))
            yield
            b.tt(v3(f[4], f[4].t[:]), v3(rw, rw.t[:, 2, :]), bc8(st_t, off + 8, off), ALU.mult)
            yield
            b.tt(y[:], y[:], f[4][:], ALU.add)
            yield
            o = ob[j]
            b.tt(o[:], y[:], rw[:, 7, :], ALU.mult)
            yield
            pt = PTt[j]
            ptb = pt.t[:, 0:256].bitcast(BF16)
            for c in range(4):
                b.tr(V(pt, ptb[:, c * 128:(c + 1) * 128]), o[:, c * 128:(c + 1) * 128], ident[:])
            ot = oT[j]
            b.copy(ot[:], V(pt, ptb.rearrange("p (c t) -> p c t", c=4)), eng="act")
            yield
            for c in range(4):
                b.dma("sp", V(mixT0, mixT0.t[512 + c * 128:512 + (c + 1) * 128, ti * 128:(ti + 1) * 128], ("r", c, ti)), ot[:, c, :])
            yield

        def pstream(j):
            for ti in range(j, NT, NS):
                yield from post_tile(j, ti)
        return [pstream(j) for j in range(NS)]


def make_consts(is_prompt):
    c = {}
    p = np.arange(128)[:, None]
    f = np.arange(128)[None, :]
    c["c_ident"] = np.eye(128, dtype=np.float32)
    partner = (np.arange(128) // 64) * 64 + (np.arange(128) % 64 + 32) % 64
    perm = np.zeros((128, 128), np.float32)
    perm[partner, np.arange(128)] = 1.0
    c["c_perm"] = perm
    NEG = -30000.0
    IU = np.where(p <= f, 0.0, NEG).astype(np.float32)
    IL = np.where(p >= f, 0.0, NEG).astype(np.float32)
    c["c_maskAB"] = np.concatenate([IU, IL], axis=1)
    if is_prompt:
        c["c_maskX"] = c["c_maskAB"].copy()
    else:
        a = IU.copy(); a[64:, :] = NEG
        bb = IL.copy(); bb[:64, :] = NEG
        c["c_maskX"] = np.concatenate([a, bb], axis=1)
    T = 4096
    S = 4096 if is_prompt else 2048
    pos = (np.arange(T) % S).astype(np.float32)
    half = 32
    inv = (10000.0 ** (-np.arange(half, dtype=np.float32) / half)).astype(np.float32)
    ang = pos[None, :] * inv[:, None]
    cos = np.cos(ang).astype(np.float32)
    sin = np.sin(ang).astype(np.float32)
    c["c_cos"] = np.tile(cos, (4, 1))
    c["c_sin"] = np.concatenate([-sin, sin, -sin, sin], axis=0)
    flag = np.zeros((128, 4), np.float32)
    flag[:, 0] = 1.0 if is_prompt else 0.0
    flag[:, 1] = 0.0 if is_prompt else NEG
    c["c_flag"] = flag
    blk = (p // 64) == (f // 64)
    SU = (blk & (p < f)).astype(np.float32)
    IUb = (blk & (p <= f)).astype(np.float32)
    SL = (blk & (p > f)).astype(np.float32)
    ILb = (blk & (p >= f)).astype(np.float32)
    T0 = np.broadcast_to((p < 64), (128, 128)).astype(np.float32)
    T1 = np.broadcast_to((p >= 64), (128, 128)).astype(np.float32)
    c["c_tri"] = np.stack([SU, IUb, SL, ILb, T0, T1], axis=1).reshape(128, 6 * 128).astype(np.float32)
    kk = np.arange(512)[None, :] % 64
    hh = np.arange(512)[None, :] // 64
    c["c_irep"] = (((p % 64) == kk) & ((p // 64) == (hh % 2))).astype(np.float32)
    return c


INPUT_NAMES = None


def build_program(upto="all", skip=()):
    nc = bass.Bass("TRN2", target_bir_lowering=False)
    k = K(nc)
    b = k.b
    shapes = {
        "mix_pre0": [D], "mix_post0": [D], "w_in0": [D, 3072], "lam_q1": [64], "lam_k1": [64], "lam_q2": [64], "lam_k2": [64],
        "subln_w": [128], "mu_r": [512], "mu_k": [512], "mu_v": [512], "mu_w": [D], "mu_a": [D], "mu_g": [D],
        "w0_f": [512], "w1_f": [D, 64], "w2_f": [64, 512], "w0_b": [512], "w1_b": [D, 64], "w2_b": [64, 512],
        "a0_f": [512], "a1_f": [D, 64], "a2_f": [64, 512], "a0_b": [512], "a1_b": [D, 64], "a2_b": [64, 512],
        "g1": [D, 160], "g2": [160, 512], "k_k": [512], "k_a": [512], "r_k": [8, 64], "lnx_w": [512], "lnx_b": [512],
        "w_out0": [D, D], "ffn_pre0": [D], "ffn_post0": [D], "ffn_gate0": [D, FH], "ffn_up0": [D, FH], "ffn_down0": [FH, D],
        "mix_pre1": [D], "mix_post1": [D], "w_in1": [D, 2304], "w_out1": [768, D], "ffn_pre1": [D], "ffn_post1": [D],
        "ffn_gate1": [D, FH], "ffn_up1": [D, FH], "ffn_down1": [FH, D],
        "c_ident": [128, 128], "c_perm": [128, 128], "c_maskAB": [128, 256], "c_maskX": [128, 256], "c_cos": [128, T], "c_sin": [128, T],
        "c_flag": [128, 4], "c_tri": [128, 6, 128], "c_irep": [128, 512],
    }
    for n, sh in shapes.items():
        k.dram(n, sh, F32, kind="ExternalInput")
    x = k.dram("x", [T, D], F32, kind="ExternalInput")
    y = k.dram("y", [T, D], F32, kind="ExternalOutput")
    if upto != "all":
        k.ext_out = {upto}
    cst = {"ident": k.d["c_ident"][:], "perm": k.d["c_perm"][:], "flag": k.d["c_flag"][:], "cos": k.d["c_cos"][:], "sin": k.d["c_sin"][:],
           "maskAB": k.d["c_maskAB"][:], "maskX": k.d["c_maskX"][:], "tri": k.d["c_tri"][:], "irep": k.d["c_irep"][:], "eps": EPS}
    prep_weight_rows(k, "w_in0", 8, 3072)

    def late_casts():
        r = {}
        r["wout0"] = prep_weight_rows(k, "w_out0", 8, D)
        prep_weights_ffn(k, 0)
        prep_weight_rows(k, "w_in1", 8, 2304)
        r["wout1"] = prep_weight_rows(k, "w_out1", 6, D)
        prep_weights_ffn(k, 1)
        return r
    mixT0 = k.dram("mixT0", [1024, T], BF16)
    mixT1 = k.dram("mixT1", [768, T], BF16)
    xmid = k.dram("xmid", [T, D], F32)
    if upto != "all":
        y.t
    QTd = k.dram("QTd", [512, T], BF16)
    KTd = k.dram("KTd", [512, T], BF16)
    Vad = k.dram("Vad", [T, 4 * 129], BF16)
    rwd = k.dram("rwd", [T, 8, 512], F32)
    with b.phase():
        ident = b.sb("ident", [128, 128], BF16)
        b.dma("pool", ident[:], cst["ident"])
        flag = b.sb("flagm", [128, 4])
        b.dma("sp", flag[:], cst["flag"])
        with b.phase():
            xnT = b.sb("xnT0", [128, 8, XW], BF16)
            l0_norm(k, x, xnT, ident, cst)
            if "qkv" not in skip:
                l0_qkv(k, xnT, cst, QTd, KTd, Vad)
            if "rwproj" not in skip:
                l0_rwproj(k, xnT, cst, rwd)
        Yd = k.dram("Yd", [2, T, 512], F32)
        if "rwkv" not in skip:
            l0_rwkv(k, cst, rwd, mixT0, ident, Yd, flag, do_post=False)
        lc = late_casts()
        wout0_b, wout1_b = lc["wout0"], lc["wout1"]
        if "diff" not in skip:
            l0_diffattn(k, cst, QTd, KTd, Vad, mixT0, ident, flag,
                        co_setup=lambda TP: rwkv_post_setup(k, rwd, Yd, mixT0, ident, TP, 2, "dve"))
    if upto == "mixT0":
        return nc, list(shapes.keys())
    phase_post(k, 0, 8, mixT0, wout0_b, x, xmid, cst)
    if upto == "xmid":
        return nc, list(shapes.keys())
    phase_l1(k, xmid, mixT1, cst)
    if upto == "mixT1":
        return nc, list(shapes.keys())
    phase_post(k, 1, 6, mixT1, wout1_b, xmid, y, cst)
    return nc, list(shapes.keys())


def kernel(**inputs):
    n = 8
    xp = np.asarray(inputs["x_prompt"], dtype=np.float32)
    xs = np.asarray(inputs["x_sample"], dtype=np.float32)
    nc, names = build_program()
    cp = make_consts(True)
    cs = make_consts(False)
    in_maps = []
    for c in range(n):
        m = {}
        prompt = c < 4
        cc = cp if prompt else cs
        for nm in names:
            if nm.startswith("c_"):
                a = cc[nm]
                if nm == "c_tri":
                    a = a.reshape(128, 6, 128)
                m[nm] = np.ascontiguousarray(a, dtype=np.float32)
            else:
                m[nm] = np.ascontiguousarray(np.asarray(inputs[nm], dtype=np.float32))
        if prompt:
            m["x"] = np.ascontiguousarray(xp[c])
        else:
            j = 2 * (c - 4)
            m["x"] = np.ascontiguousarray(xs[j:j + 2].reshape(T, D))
        in_maps.append(m)
    res = run_bass_kernel_spmd(nc, in_maps, core_ids=list(range(n)))
    outs = [np.asarray(r["y"], dtype=np.float32) for r in res.results]
    y_prompt = np.stack(outs[0:4], axis=0)
    y_sample = np.concatenate([o.reshape(2, T // 2, D) for o in outs[4:8]], axis=0)
    return (y_prompt, y_sample)
```

```python
import numpy as np
from contextlib import ExitStack
import concourse.bass as bass
import concourse.mybir as mybir

F32 = mybir.dt.float32
BF16 = mybir.dt.bfloat16
ALU = mybir.AluOpType
AF = mybir.ActivationFunctionType
AX = mybir.AxisListType


class Res:
    __slots__ = ("w", "r")

    def __init__(self):
        self.w = None
        self.r = {}


class View:
    __slots__ = ("tile", "ap", "key")

    def __init__(self, tile, ap, key):
        self.tile = tile
        self.ap = ap
        self.key = key


class _Keyed:
    def __init__(self, tile, key):
        self.tile = tile
        self.key = key

    def __getitem__(self, idx):
        return View(self.tile, self.tile.t[idx], self.key)


class Tile:
    def __init__(self, name, t):
        self.name = name
        self.t = t
        self.res = {None: Res()}

    def __getitem__(self, idx):
        return View(self, self.t[idx], None)

    def k(self, key):
        return _Keyed(self, key)

    def conflicts(self, key):
        if key is None:
            return list(self.res.values())
        if key not in self.res:
            self.res[key] = Res()
        return [self.res[None], self.res[key]]

    def get(self, key):
        if key not in self.res:
            self.res[key] = Res()
        return self.res[key]


NDMA = 56
SB_DEBUG = False
NSW = 32


class Builder:
    def __init__(self, nc):
        self.nc = nc
        self.eng = {"pe": nc.tensor, "dve": nc.vector, "act": nc.scalar, "pool": nc.gpsimd, "sp": nc.sync}
        self.sem = {e: nc.alloc_semaphore("sem_" + e) for e in ("pe", "dve", "act", "pool")}
        self.tick = {e: 0 for e in self.sem}
        self.dsem = [nc.alloc_semaphore("dsem%d" % i) for i in range(NDMA)]
        self.ssem = [nc.alloc_semaphore("ssem%d" % i) for i in range(NSW)]
        self.ndma = 0
        self.nsw = 0
        self.waited = {e: {} for e in self.eng}
        self.epoch = 0
        self.barA = nc.alloc_semaphore("barA")
        self.barB = nc.alloc_semaphore("barB")
        self.nwait = 0
        self.ninst = 0
        self.stack = None

    def phase(self):
        return _Phase(self)

    def sb(self, name, shape, dtype=F32):
        self.uid = getattr(self, "uid", 0) + 1
        name = "%s_u%d" % (name, self.uid)
        t = self.stack.enter_context(self.nc.sbuf_tensor(name, list(shape), dtype))
        self.sb_hi = max(getattr(self, "sb_hi", 0), self.nc.sbuf_base)
        if SB_DEBUG:
            print("SB", name, shape, "end", self.nc.sbuf_base)
        return Tile(name, t)

    def ps(self, name, shape, dtype=F32):
        self.uid = getattr(self, "uid", 0) + 1
        name = "%s_u%d" % (name, self.uid)
        t = self.stack.enter_context(self.nc.psum_tensor(name, list(shape), dtype))
        return Tile(name, t)

    def dram(self, name, shape, dtype, kind="Internal"):
        t = self.nc.dram_tensor(name, list(shape), dtype, kind=kind)
        return Tile(name, t.ap())

    def _wait(self, eng, tok):
        sem, val, owner = tok[0], tok[1], tok[2]
        if len(tok) > 3 and tok[3] < self.epoch:
            return
        if owner == eng and eng == "pe":
            return
        key = id(sem)
        w = self.waited[eng]
        if w.get(key, 0) >= val:
            return
        self.eng[eng].wait_ge(sem, val)
        w[key] = val
        self.nwait += 1

    def _deps(self, eng, reads, writes):
        for v in reads:
            for r in v.tile.conflicts(v.key):
                if r.w is not None:
                    self._wait(eng, r.w)
        for v in writes:
            for r in v.tile.conflicts(v.key):
                if r.w is not None:
                    self._wait(eng, r.w)
                for (sem, owner, ep), val in list(r.r.items()):
                    self._wait(eng, (sem, val, owner, ep))

    def _commit(self, tok, reads, writes):
        sem, val, owner = tok[0], tok[1], tok[2]
        for v in writes:
            if v.key is None:
                t = v.tile
                t.res = {None: t.res[None]}
            r = v.tile.get(v.key)
            r.w = tok
            r.r = {}
        for v in reads:
            r = v.tile.get(v.key)
            k = (sem, owner, tok[3])
            if r.r.get(k, 0) < val:
                r.r[k] = val

    def op(self, eng, fn, reads, writes):
        self._deps(eng, reads, writes)
        ins = fn()
        self.tick[eng] += 1
        ins.then_inc(self.sem[eng], 1)
        tok = (self.sem[eng], self.tick[eng], eng, self.epoch)
        self._commit(tok, reads, writes)
        self.ninst += 1
        return tok

    def _dslot(self, q):
        if q == "pool":
            i = self.nsw
            self.nsw += 1
            return self.ssem[i % NSW], i // NSW, "sdma%d" % (i % NSW)
        i = self.ndma
        self.ndma += 1
        return self.dsem[i % NDMA], i // NDMA, "dma%d" % (i % NDMA)

    def dma(self, q, out, in_, **kw):
        sem, gen, owner = self._dslot(q)
        if gen > 0:
            self._wait(q, (sem, 16 * gen, "dma"))
        self._deps(q, [in_], [out])
        ins = self.eng[q].dma_start(out=out.ap, in_=in_.ap, **kw)
        ins.then_inc(sem, 16)
        tok = (sem, 16 * (gen + 1), owner, self.epoch)
        self._commit(tok, [in_], [out])
        self.ninst += 1
        return tok

    def barrier(self):
        toks = [(self.sem[e], self.tick[e], e) for e in self.sem if self.tick[e] > 0]
        for (n, N, sems) in ((self.ndma, NDMA, self.dsem), (self.nsw, NSW, self.ssem)):
            for slot in range(min(n, N)):
                cnt = (n - 1 - slot) // N + 1
                toks.append((sems[slot], 16 * cnt, "dma"))
        for e in self.eng:
            for tok in toks:
                self._wait(e, tok)
        self.epoch += 1
        ep = self.epoch
        for e in self.eng:
            self.eng[e].sem_inc(self.barA, 1)
        for e in self.sem:
            self.eng[e].wait_ge(self.barA, 5 * ep)
            self.eng[e].sem_clear(self.sem[e])
            self.eng[e].sem_inc(self.barB, 1)
        for e in self.eng:
            self.eng[e].wait_ge(self.barB, 4 * ep)
        for e in self.sem:
            self.tick[e] = 0
        for e in self.eng:
            for s_ in self.sem.values():
                self.waited[e].pop(id(s_), None)

    def mm(self, out, lhsT, rhs, start=True, stop=True, **kw):
        return self.op("pe", lambda: self.nc.tensor.matmul(out.ap, lhsT.ap, rhs.ap, start=start, stop=stop, **kw),
                       [lhsT, rhs] + ([] if start else [out]), [out])

    def tr(self, out, in_, ident):
        return self.op("pe", lambda: self.nc.tensor.transpose(out.ap, in_.ap, ident.ap), [in_, ident], [out])

    def act(self, out, in_, func, bias=0.0, scale=1.0, accum=None):
        reads = [in_]
        kw = {}
        if isinstance(bias, View):
            reads.append(bias)
            kw["bias"] = bias.ap
        else:
            kw["bias"] = float(bias)
        if isinstance(scale, View):
            reads.append(scale)
            kw["scale"] = scale.ap
        else:
            kw["scale"] = float(scale)
        writes = [out]
        if accum is not None:
            writes.append(accum)
            kw["accum_out"] = accum.ap
        return self.op("act", lambda: self.nc.scalar.activation(out.ap, in_.ap, func, **kw), reads, writes)

    def tt(self, out, a, b, op, eng="dve", after=()):
        e = self.eng[eng]
        return self.op(eng, lambda: e.tensor_tensor(out.ap, a.ap, b.ap, op), [a, b] + list(after), [out])

    def ts(self, out, a, s1, op0, s2=None, op1=None, eng="dve", accum=None):
        e = self.eng[eng]
        reads = [a]
        x1 = s1.ap if isinstance(s1, View) else float(s1)
        if isinstance(s1, View):
            reads.append(s1)
        x2 = None
        if s2 is not None:
            x2 = s2.ap if isinstance(s2, View) else float(s2)
            if isinstance(s2, View):
                reads.append(s2)
        writes = [out]
        kw = {}
        if accum is not None:
            writes.append(accum)
            kw["accum_out"] = accum.ap
        if op1 is None:
            return self.op(eng, lambda: e.tensor_scalar(out.ap, a.ap, x1, None, op0, **kw), reads, writes)
        return self.op(eng, lambda: e.tensor_scalar(out.ap, a.ap, x1, x2, op0, op1, **kw), reads, writes)

    def stt(self, out, a, s, b, op0, op1, accum=None):
        reads = [a, b]
        x = s.ap if isinstance(s, View) else float(s)
        if isinstance(s, View):
            reads.append(s)
        writes = [out]
        kw = {}
        if accum is not None:
            writes.append(accum)
            kw["accum_out"] = accum.ap
        return self.op("dve", lambda: self.nc.vector.scalar_tensor_tensor(out.ap, a.ap, x, b.ap, op0, op1, **kw),
                       reads, writes)

    def copy(self, out, in_, eng="dve"):
        if eng == "act":
            return self.op("act", lambda: self.nc.scalar.copy(out.ap, in_.ap), [in_], [out])
        e = self.eng[eng]
        return self.op(eng, lambda: e.tensor_copy(out.ap, in_.ap), [in_], [out])

    def memset(self, out, val, eng="pool"):
        e = self.eng[eng]
        return self.op(eng, lambda: e.memset(out.ap, val), [], [out])

    def recip(self, out, in_):
        return self.op("dve", lambda: self.nc.vector.reciprocal(out.ap, in_.ap), [in_], [out])

    def reduce(self, out, in_, op=ALU.add, axis=AX.X):
        return self.op("dve", lambda: self.nc.vector.tensor_reduce(out.ap, in_.ap, axis, op), [in_], [out])


class _Phase:
    def __init__(self, b):
        self.b = b

    def __enter__(self):
        self.prev = self.b.stack
        self.st = ExitStack()
        self.st.__enter__()
        self.b.stack = self.st
        return self

    def __exit__(self, *a):
        self.b.barrier()
        self.b.stack = self.prev
        return self.st.__exit__(*a)
from concourse.bass_utils import run_bass_kernel_spmd
import math
T = 4096
D = 1024
FH = 2816
NHC = FH // 128
EPS = 1e-6


class K:
    def __init__(self, nc, ext_in=(), ext_out=()):
        self.nc = nc
        self.b = Builder(nc)
        self.ext_in = set(ext_in)
        self.ext_out = set(ext_out)
        self.d = {}

    def dram(self, name, shape, dtype, kind=None):
        if kind is None:
            kind = "ExternalInput" if name in self.ext_in else ("ExternalOutput" if name in self.ext_out else "Internal")
        t = self.b.dram(name, shape, dtype, kind=kind)
        self.d[name] = t
        return t


class _Bank:
    def __init__(self, tile, i):
        self.tile = tile
        self.i = i

    def __getitem__(self, idx):
        rows, cols = idx
        return View(self.tile, self.tile.t[rows, self.i, cols], None)


class _Cols:
    def __init__(self, tile, off):
        self.tile = tile
        self.off = off

    def __getitem__(self, idx):
        rows, cols = idx
        return View(self.tile, self.tile.t[rows, cols.start + self.off:cols.stop + self.off], self.off)


def V(tile, ap, key=None):
    return View(tile, ap, key)


def dma_nc(b, q, out, in_):
    return b.dma(q, out, in_, allow_slow_non_contiguous=True)


def prep_weights_ffn(k, L):
    b = k.b
    wg, wu, wd = k.d["ffn_gate%d" % L], k.d["ffn_up%d" % L], k.d["ffn_down%d" % L]
    wgu_b = k.dram("wgu_b%d" % L, [NHC, 128, 2, 8, 128], BF16)
    wd_b = k.dram("wd_b%d" % L, [NHC, 128, D], BF16)
    for hc in range(NHC):
        for j, w in enumerate((wg, wu)):
            src = V(w, w.t[:, hc * 128:(hc + 1) * 128].rearrange("(kc p) c -> p kc c", p=128))
            b.dma("pool", V(wgu_b, wgu_b.t[hc, :, j, :, :], hc), src)
    for hc in range(0, NHC, 2):
        src = V(wd, wd.t[hc * 128:(hc + 2) * 128, :].rearrange("(h p) c -> h p c", p=128))
        b.dma("pool", V(wd_b, wd_b.t[hc:hc + 2], hc), src)


def prep_weight_rows(k, name, nchunk, ncol):
    b = k.b
    w = k.d[name]
    wb = k.dram(name + "_b", [nchunk, 128, ncol], BF16)
    step = 2
    for c in range(0, nchunk, step):
        n = min(step, nchunk - c)
        src = V(w, w.t[c * 128:(c + n) * 128, :].rearrange("(h p) c -> h p c", p=128))
        b.dma("pool", V(wb, wb.t[c:c + n], c), src)
    return wb


def phase_post(k, L, C, mixT, wout_b, x_in, x_out, cst):
    b = k.b
    nc = k.nc
    wgu_b, wd_b = k.d["wgu_b%d" % L], k.d["wd_b%d" % L]
    x1d = k.dram("x1d%d" % L, [T, D], F32)
    with b.phase():
        ident = b.sb("ident", [128, 128], BF16)
        b.dma("pool", ident[:], cst["ident"])
        g_post = b.sb("g_post", [128, D])
        g_fpost = b.sb("g_fpost", [128, D])
        g_fpre = b.sb("g_fpre", [128, 8])
        b.dma("sp", g_post[:], V(k.d["mix_post%d" % L], k.d["mix_post%d" % L].t.partition_broadcast(128)))
        b.dma("sp", g_fpost[:], V(k.d["ffn_post%d" % L], k.d["ffn_post%d" % L].t.partition_broadcast(128)))
        dma_nc(b, "sp", g_fpre[:], V(k.d["ffn_pre%d" % L], k.d["ffn_pre%d" % L].t.rearrange("(kc p) -> p kc", p=128)))
        wout = b.sb("wout_sb%d" % L, [128, C, D], BF16)
        for c in range(C):
            b.dma("sp", wout[:, c, :], wout_b[c])
        wd = b.sb("wd", [128, NHC, D], BF16)
        for hc in range(NHC):
            b.dma("sp", wd.k(hc)[:, hc, :], wd_b.k(hc - hc % 2)[hc])
        hT = b.sb("hT", [128, NHC, 1024], BF16)
        xn2T = b.sb("xn2T", [128, 8, 1024], BF16)
        mixh = b.sb("mixh", [128, C, 512], BF16)
        wgu = [b.sb("wgu%d" % i, [128, 2, 8, 128], BF16) for i in range(2)]
        xin = [b.sb("xin%d" % i, [128, D]) for i in range(2)]
        x1r = [b.sb("x1r%d" % i, [128, D]) for i in range(2)]
        tmp = b.sb("tmp", [128, D])
        junk = b.sb("junk", [128, D], BF16)
        xs = b.sb("xs", [128, D], BF16)
        sg = [b.sb("sg%d" % i, [128, 512], BF16) for i in range(2)]
        st_t = b.sb("st", [128, 64])
        A = [b.ps("A%d" % i, [128, 1024]) for i in range(2)]
        G = [b.ps("G%d" % i, [128, 1024]) for i in range(2)]
        na = 0
        ng = 0
        nw = 0
        for stile in range(T // 1024):
            t0 = stile * 1024
            na0 = na

            def emit_outproj(s):
                if s % 4 == 0:
                    for c in range(C):
                        b.dma("sp", mixh[:, c, :], V(mixT, mixT.t[c * 128:(c + 1) * 128, t0 + (s // 4) * 512: t0 + (s // 4) * 512 + 512]))
                r0 = t0 + s * 128
                xi = xin[s % 2]
                b.dma("sp", xi[:], V(x_in, x_in.t[r0:r0 + 128, :]))
                acc = A[(na0 + s) % 2]
                for half in range(2):
                    for c in range(C):
                        b.mm(acc[:, half * 512:(half + 1) * 512], mixh[:, c, (s % 4) * 128:(s % 4) * 128 + 128],
                             wout[:, c, half * 512:(half + 1) * 512], start=(c == 0), stop=(c == C - 1))
            emit_outproj(0)
            for s in range(8):
                r0 = t0 + s * 128
                st = _Cols(st_t, (s % 2) * 16)
                xi = xin[s % 2]
                acc = A[(na0 + s) % 2]
                if s + 1 < 8 and (s + 1) % 4 != 0:
                    emit_outproj(s + 1)
                b.act(junk[:], acc[:], AF.Square, accum=st[:, 0:1])
                b.act(st[:, 1:2], st[:, 0:1], AF.Sqrt, bias=cst["eps"], scale=1.0 / D)
                b.recip(st[:, 2:3], st[:, 1:2])
                b.stt(tmp[:], acc[:], st[:, 2:3], g_post[:], ALU.mult, ALU.mult)
                b.tt(xi[:], tmp[:], xi[:], ALU.add)
                b.dma("sp", V(x1d, x1d.t[r0:r0 + 128, :], r0), xi[:])
                b.act(junk[:], xi[:], AF.Square, accum=st[:, 3:4])
                b.act(st[:, 4:5], st[:, 3:4], AF.Sqrt, bias=cst["eps"], scale=1.0 / D)
                b.recip(st[:, 5:6], st[:, 4:5])
                b.ts(xs[:], xi[:], st[:, 5:6], ALU.mult)
                gt = G[ng % 2]
                ng += 1
                gtb = gt.t[:, 0:512].bitcast(BF16)
                for kc in range(8):
                    b.tr(V(gt, gtb[:, kc * 128:(kc + 1) * 128]), xs[:, kc * 128:(kc + 1) * 128], ident[:])
                b.tt(xn2T[:, :, s * 128:(s + 1) * 128], V(gt, gtb.rearrange("p (kc t) -> p kc t", kc=8)),
                     V(g_fpre, g_fpre.t[:, :].unsqueeze(2).to_broadcast([128, 8, 128])), ALU.mult)
                if s + 1 < 8 and (s + 1) % 4 == 0:
                    emit_outproj(s + 1)
            na += 8
            for hc in range(NHC):
                w = wgu[nw % 2]
                nw += 1
                b.dma("sp", w[:], wgu_b.k(hc)[hc])
                for th in range(2):
                    gt = G[ng % 2]
                    ng += 1
                    for j in range(2):
                        for kc in range(8):
                            b.mm(gt[:, j * 512:(j + 1) * 512], w[:, j, kc, :], xn2T[:, kc, th * 512:(th + 1) * 512],
                                 start=(kc == 0), stop=(kc == 7))
                    sgt = sg[(ng) % 2]
                    b.act(sgt[:], gt[:, 0:512], AF.Silu)
                    b.tt(hT[:, hc, th * 512:(th + 1) * 512], gt[:, 512:1024], sgt[:], ALU.mult)
            for s in range(8):
                r0 = t0 + s * 128
                st = _Cols(st_t, 32 + (s % 2) * 16)
                xr = x1r[s % 2]
                b.dma("sp", xr[:], V(x1d, x1d.t[r0:r0 + 128, :], r0))
                acc = A[na % 2]
                na += 1
                for half in range(2):
                    for hc in range(NHC):
                        b.mm(acc[:, half * 512:(half + 1) * 512], hT[:, hc, s * 128:(s + 1) * 128],
                             wd.k(hc)[:, hc, half * 512:(half + 1) * 512], start=(hc == 0), stop=(hc == NHC - 1))
                b.act(junk[:], acc[:], AF.Square, accum=st[:, 6:7])
                b.act(st[:, 7:8], st[:, 6:7], AF.Sqrt, bias=cst["eps"], scale=1.0 / D)
                b.recip(st[:, 8:9], st[:, 7:8])
                b.stt(tmp[:], acc[:], st[:, 8:9], g_fpost[:], ALU.mult, ALU.mult)
                b.tt(xr[:], tmp[:], xr[:], ALU.add)
                b.dma("sp", V(x_out, x_out.t[r0:r0 + 128, :], r0), xr[:])


NEG = -30000.0


def norm_to_T(k, x_in, gain_name, xnT, colf, ident, tagp=""):
    for _ in norm_to_T_gen(k, x_in, gain_name, xnT, colf, ident):
        pass


def norm_to_T_gen(k, x_in, gain_name, xnT, colf, ident, NPS=4):
    b = k.b
    g = b.sb("gpre", [128, 8])
    dma_nc(b, "sp", g[:], V(k.d[gain_name], k.d[gain_name].t.rearrange("(kc p) -> p kc", p=128)))
    NB = 4
    xin = [b.sb("nx%d" % i, [128, D]) for i in range(NB)]
    xs = [b.sb("nxs%d" % i, [128, D], BF16) for i in range(NB)]
    junks = [b.sb("njunk%d" % i, [128, D], BF16) for i in range(2)]
    st_t = b.sb("nst", [128, 16 * NB])
    P = [b.ps("nP%d" % i, [128, 512]) for i in range(NPS)]
    for s in range(T // 128):
        r0 = s * 128
        st = _Cols(st_t, (s % NB) * 16)
        xi = xin[s % NB]
        junk = junks[s % 2]
        b.dma("sp", xi[:], V(x_in, x_in.t[r0:r0 + 128, :]))
        b.act(junk[:], xi[:], AF.Square, accum=st[:, 0:1])
        b.act(st[:, 1:2], st[:, 0:1], AF.Sqrt, bias=EPS, scale=1.0 / D)
        b.recip(st[:, 2:3], st[:, 1:2])
        b.ts(xs[s % NB][:], xi[:], st[:, 2:3], ALU.mult)
        pt = P[s % NPS]
        ptb = pt.t[:, 0:512].bitcast(BF16)
        for kc in range(8):
            b.tr(V(pt, ptb[:, kc * 128:(kc + 1) * 128]), xs[s % NB][:, kc * 128:(kc + 1) * 128], ident[:])
        c0 = colf(r0)
        b.tt(xnT.k(s // 4)[:, :, c0:c0 + 128], V(pt, ptb.rearrange("p (kc t) -> p kc t", kc=8)),
             V(g, g.t[:, :].unsqueeze(2).to_broadcast([128, 8, 128])), ALU.mult)
        yield


def phase_l1(k, x_in, mixT1, cst):
    b = k.b
    nc = k.nc
    win_b = k.d["w_in1_b"]
    PADR = 1024
    Vd = k.dram("Vd", [PADR + T + PADR, 12 * 65], BF16)
    Nd = [k.dram("Nd%d" % g, [T, 260], F32) for g in range(3)]
    DIL = (1, 4, 16)
    with b.phase():
        ident = b.sb("ident", [128, 128], BF16)
        b.dma("pool", ident[:], cst["ident"])
        perm = b.sb("perm", [128, 128], BF16)
        b.dma("pool", perm[:], cst["perm"])
        xnT = b.sb("xnT1", [128, 8, T], BF16)
        with b.phase():
            norm_to_T(k, x_in, "mix_pre1", xnT, lambda t: t, ident)
        cosT = b.sb("cosT", [128, T])
        sinT = b.sb("sinT", [128, T])
        b.dma("sp", cosT[:], cst["cos"])
        b.dma("sp", sinT[:], cst["sin"])
        with b.phase():
            wv = b.sb("wv", [128, 8, 768], BF16)
            for kc in range(8):
                b.dma("sp", wv[:, kc, :], V(win_b, win_b.t[kc, :, 1536:2304]))
            z = b.sb("zpad", [128, 12 * 65], BF16)
            b.memset(z[:], 0.0)
            for i in range(PADR // 128):
                b.dma("sp", V(Vd, Vd.t[i * 128:(i + 1) * 128, :], "p%d" % i), z[:])
                b.dma("sp", V(Vd, Vd.t[PADR + T + i * 128:PADR + T + (i + 1) * 128, :], "q%d" % i), z[:])
            va = [b.sb("va%d" % i, [128, 12, 65], BF16) for i in range(2)]
            for i in range(2):
                b.memset(va[i][:], 1.0)
            Pv = [b.ps("Pv%d" % i, [128, 1024]) for i in range(2)]
            for s in range(T // 128):
                p = Pv[s % 2]
                for (c0, c1) in ((0, 512), (512, 768)):
                    for kc in range(8):
                        b.mm(p[:, c0:c1], xnT[:, kc, s * 128:(s + 1) * 128], wv[:, kc, c0:c1], start=(kc == 0), stop=(kc == 7))
                b.copy(va[s % 2][:, :, 0:64], V(p, p.t[:, 0:768].rearrange("p (h e) -> p h e", e=64)), eng="act")
                b.dma("sp", V(Vd, Vd.t[PADR + s * 128:PADR + (s + 1) * 128, :], s),
                      V(va[s % 2], va[s % 2].t[:].rearrange("p h e -> p (h e)")))
        for g in range(3):
            dil = DIL[g]
            L = T // dil
            NQ = L // 128
            with b.phase():
                wqk = b.sb("wqk", [128, 8, 2, 256], BF16)
                for kc in range(8):
                    b.dma("sp", wqk[:, kc, 0, :], V(win_b, win_b.t[kc, :, g * 256:(g + 1) * 256]))
                    b.dma("sp", wqk[:, kc, 1, :], V(win_b, win_b.t[kc, :, 768 + g * 256:768 + (g + 1) * 256]))
                QT = b.sb("QT", [128, 2, dil, L], BF16)
                KT = b.sb("KT", [128, 2, dil, L + 128], BF16)
                b.memset(KT[:], 0.0)
                t1 = [b.sb("t1_%d" % i, [128, 512]) for i in range(2)]
                t2 = [b.sb("t2_%d" % i, [128, 512]) for i in range(2)]
                qbf = [b.sb("qbf_%d" % i, [128, 512], BF16) for i in range(2)]
                with b.phase():
                    PA = [b.ps("PA%d" % i, [128, 512]) for i in range(2)]
                    PB = [b.ps("PB%d" % i, [128, 512]) for i in range(2)]
                    n = 0
                    for j in range(T // 512):
                        for a in range(2):
                            for mm in range(2):
                                pa, pb = PA[n % 2], PB[n % 2]
                                for kc in range(8):
                                    b.mm(pa[:], wqk[:, kc, a, mm * 128:(mm + 1) * 128], xnT[:, kc, j * 512:(j + 1) * 512],
                                         start=(kc == 0), stop=(kc == 7))
                                b.copy(qbf[n % 2][:], pa[:], eng="act")
                                b.mm(pb[:], perm[:], qbf[n % 2][:])
                                b.tt(t1[n % 2][:], pa[:], cosT[:, j * 512:(j + 1) * 512], ALU.mult, after=[qbf[n % 2][:]])
                                b.tt(t2[n % 2][:], pb[:], sinT[:, j * 512:(j + 1) * 512], ALU.mult)
                                w = 512 // dil
                                if a == 0:
                                    dst = V(QT, QT.t[:, mm, :, j * w:(j + 1) * w])
                                else:
                                    dst = V(KT, KT.t[:, mm, :, 64 + j * w:64 + (j + 1) * w])
                                b.tt(dst, V(t1[n % 2], t1[n % 2].t[:].rearrange("p (jl r) -> p r jl", r=dil)),
                                     V(t2[n % 2], t2[n % 2].t[:].rearrange("p (jl r) -> p r jl", r=dil)), ALU.add, eng="pool")
                                n += 1
                with b.phase():
                    mk = b.sb("mk", [128, 2, 256], BF16)
                    mkx = b.sb("mkx", [128, 2, 256], BF16)
                    for i in range(2):
                        b.dma("pool", mk[:, i, :], cst["maskAB"])
                        b.dma("pool", mkx[:, i, :], cst["maskX"])
                    vt = [b.sb("vt%d" % i, [128, 4, 65], BF16) for i in range(3)]
                    PT = [b.sb("PT%d" % i, [128, 4, 256], BF16) for i in range(2)]
                    osb = [b.sb("osb%d" % i, [128, 260]) for i in range(2)]
                    S = [b.ps("S%d" % i, [128, 1024]) for i in range(2)]
                    ACC = [b.ps("ACC%d" % i, [128, 512]) for i in range(2)]
                    it = 0

                    def geom(kt):
                        q0 = max(kt - 1, 0) * 128
                        q1 = min(kt + 1, NQ) * 128
                        return q0, q1, q1 - q0, (0 if kt > 0 else 128)

                    def emit_scores(rho, kt, it_):
                        sp = S[it_ % 2]
                        q0, q1, nq, m0 = geom(kt)
                        msk = mkx if kt == NQ // 2 else mk
                        for hh in range(4):
                            mm, pb = hh // 2, (hh % 2) * 64
                            b.mm(sp[:, hh * 256:hh * 256 + nq], ident[:], msk[:, 0, m0:m0 + nq], start=True, stop=False,
                                 skip_group_check=True)
                            b.mm(sp[:, hh * 256:hh * 256 + nq], KT[pb:pb + 64, mm, rho, kt * 128:(kt + 1) * 128],
                                 QT[pb:pb + 64, mm, rho, q0:q1], start=False, stop=True, skip_group_check=True)
                    iters = [(rho, kt) for rho in range(dil) for kt in range(NQ + 1)]
                    emit_scores(*iters[0], 0)
                    for rho in range(dil):
                        for kt in range(NQ + 1):
                            v = vt[it % 3]
                            row0 = PADR + dil * (128 * kt - 64) + rho
                            b.dma("sp", v[:], V(Vd, Vd.t[row0:row0 + 127 * dil + 1:dil, g * 260:(g + 1) * 260].rearrange("p (h e) -> p h e", e=65)))
                            sp = S[it % 2]
                            pt = PT[it % 2]
                            q0, q1, nq, m0 = geom(kt)
                            if it + 1 < len(iters):
                                emit_scores(*iters[it + 1], it + 1)
                            b.act(pt[:, :, 0:nq], V(sp, sp.t[:].rearrange("p (h c) -> p h c", c=256)[:, :, 0:nq]), AF.Exp, scale=0.125)
                            if kt > 0:
                                acc = ACC[(kt - 1) % 2]
                                for hh in range(4):
                                    b.mm(acc[:, hh * 65:(hh + 1) * 65], pt[:, hh, 0:128], v[:, hh, :], start=False, stop=True,
                                         skip_group_check=True)
                                o = osb[(kt - 1) % 2]
                                b.copy(o[:], acc[:, 0:260])
                                tok0 = dil * 128 * (kt - 1) + rho
                                b.dma("sp", V(Nd[g], Nd[g].t[tok0:tok0 + 127 * dil + 1:dil, :], (rho, kt - 1)), o[:])
                            if kt < NQ:
                                acc = ACC[kt % 2]
                                c0 = nq - 128
                                for hh in range(4):
                                    b.mm(acc[:, hh * 65:(hh + 1) * 65], pt[:, hh, c0:c0 + 128], v[:, hh, :], start=(hh == 0), stop=False,
                                         skip_group_check=True)
                            it += 1
        with b.phase():
            nt = [b.sb("nt%d" % i, [128, 3, 4, 65]) for i in range(4)]
            zt = b.sb("zt", [128, 32])
            yb = [b.sb("yb%d" % i, [128, 768], BF16) for i in range(4)]
            yT = [b.sb("yT%d" % i, [128, 6, 512], BF16) for i in range(2)]
            PTt = [b.ps("PTt%d" % i, [128, 512]) for i in range(4)]
            for s in range(T // 128):
                n_ = nt[s % 4]
                for g in range(3):
                    b.dma("sp", n_[:, g, :, :], V(Nd[g], Nd[g].t[s * 128:(s + 1) * 128, :].rearrange("p (h e) -> p h e", e=65)))
                zc = _Cols(zt, (s % 4) * 8)
                b.tt(V(zt, zt.t[:, (s % 4) * 8:(s % 4) * 8 + 4], (s % 4) * 8), n_[:, 0, :, 64], n_[:, 1, :, 64], ALU.add)
                b.tt(V(zt, zt.t[:, (s % 4) * 8:(s % 4) * 8 + 4], (s % 4) * 8), zc[:, 0:4], n_[:, 2, :, 64], ALU.add)
                b.recip(zc[:, 4:8], zc[:, 0:4])
                rzb = zt.t[:, (s % 4) * 8 + 4:(s % 4) * 8 + 8].unsqueeze(1).unsqueeze(3).to_broadcast([128, 3, 4, 64])
                b.tt(V(yb[s % 4], yb[s % 4].t[:].rearrange("p (g h e) -> p g h e", g=3, h=4)), n_[:, :, :, 0:64],
                     V(zt, rzb, (s % 4) * 8), ALU.mult)
                pt = PTt[s % 4]
                ptb = pt.t[:, 0:512].bitcast(BF16)
                for c in range(6):
                    b.tr(V(pt, ptb[:, c * 128:(c + 1) * 128]), yb[s % 4][:, c * 128:(c + 1) * 128], ident[:])
                y_ = yT[(s // 4) % 2]
                b.copy(y_[:, :, (s % 4) * 128:(s % 4) * 128 + 128], V(pt, ptb[:, 0:768].rearrange("p (c t) -> p c t", c=6)), eng="act")
                if s % 4 == 3:
                    for c in range(6):
                        b.dma("sp", V(mixT1, mixT1.t[c * 128:(c + 1) * 128, (s - 3) * 128:(s + 1) * 128], (c, s)), y_[:, c, :])


def colf0(t):
    return t + 1 + (2 if t >= 2048 else 0)


XW = T + 4


def l0_norm(k, x_in, xnT, ident, cst, do_norm=True):
    b = k.b
    if do_norm:
        with b.phase():
            norm_to_T(k, x_in, "mix_pre0", xnT, colf0, ident)
    flag = b.sb("flag", [128, 4])
    b.dma("sp", flag[:], cst["flag"])
    b.memset(xnT[:, :, 0:1], 0.0)
    b.memset(xnT[:, :, XW - 1:XW], 0.0)
    b.ts(xnT[:, :, 2049:2050], xnT[:, :, 2051:2052], flag[:, 0:1], ALU.mult)
    b.ts(xnT[:, :, 2050:2051], xnT[:, :, 2048:2049], flag[:, 0:1], ALU.mult)
    return flag


def l0_qkv(k, xnT, cst, QTd, KTd, Vad, x_in=None, ident=None):
    b = k.b
    win_b = k.d["w_in0_b"]
    with b.phase():
        ngen = norm_to_T_gen(k, x_in, "mix_pre0", xnT, colf0, ident, NPS=2) if x_in is not None else None

        def norm_steps(n):
            if ngen is None:
                return
            for _ in range(n):
                try:
                    next(ngen)
                except StopIteration:
                    return
        cosT = b.sb("cosT", [128, T])
        sinT = b.sb("sinT", [128, T])
        b.dma("sp", cosT[:], cst["cos"])
        b.dma("sp", sinT[:], cst["sin"])
        wqk = b.sb("wqk0", [128, 8, 1024], BF16)
        wv = b.sb("wv0", [128, 8, 512], BF16)
        for kc in range(8):
            b.dma("sp", wqk[:, kc, :], V(win_b, win_b.t[kc, :, 0:1024]))
            b.dma("sp", wv[:, kc, :], V(win_b, win_b.t[kc, :, 1024:1536]))
        perm = b.sb("perm0", [128, 128], BF16)
        b.dma("pool", perm[:], cst["perm"])
        qbf = [b.sb("qbf0_%d" % i, [128, 512], BF16) for i in range(2)]
        t1 = [b.sb("t1_%d" % i, [128, 512]) for i in range(2)]
        t2 = [b.sb("t2_%d" % i, [128, 512]) for i in range(2)]
        qst = [b.sb("qst%d" % i, [128, 512], BF16) for i in range(3)]
        va = [b.sb("va0_%d" % i, [128, 4, 129], BF16) for i in range(2)]
        for i in range(2):
            b.memset(va[i][:], 1.0)
        PA = [b.ps("PA%d" % i, [128, 512]) for i in range(2)]
        PB = [b.ps("PB%d" % i, [128, 512]) for i in range(2)]
        PV = [b.ps("PV%d" % i, [128, 512]) for i in range(2)]
        nctr = [0]

        def qkv_tile(j):
            c0 = colf0(j * 512)
            for a in range(2):
                for m in range(4):
                    n = nctr[0]
                    nctr[0] += 1
                    pa, pb = PA[n % 2], PB[n % 2]
                    col = a * 512 + m * 128
                    for kc in range(8):
                        b.mm(pa[:], wqk[:, kc, col:col + 128], xnT.k(j)[:, kc, c0:c0 + 512], start=(kc == 0), stop=(kc == 7))
                    b.copy(qbf[n % 2][:], pa[:], eng="act")
                    b.mm(pb[:], perm[:], qbf[n % 2][:])
                    b.tt(t1[n % 2][:], pa[:], cosT[:, j * 512:(j + 1) * 512], ALU.mult, after=[qbf[n % 2][:]])
                    b.tt(t2[n % 2][:], pb[:], sinT[:, j * 512:(j + 1) * 512], ALU.mult)
                    q = qst[n % 3]
                    b.tt(q[:], t1[n % 2][:], t2[n % 2][:], ALU.add, eng="pool")
                    dst = QTd if a == 0 else KTd
                    b.dma("sp", V(dst, dst.t[m * 128:(m + 1) * 128, j * 512:(j + 1) * 512], (m, j)), q[:])
                    yield
            for s4 in range(4):
                s = j * 4 + s4
                p = PV[s % 2]
                for kc in range(8):
                    b.mm(p[:], xnT.k(j)[:, kc, c0 + s4 * 128:c0 + (s4 + 1) * 128], wv[:, kc, :], start=(kc == 0), stop=(kc == 7))
                b.copy(va[s % 2][:, :, 0:128], V(p, p.t[:].rearrange("p (h e) -> p h e", e=128)), eng="act")
                b.dma("sp", V(Vad, Vad.t[s * 128:(s + 1) * 128, :], s), V(va[s % 2], va[s % 2].t[:].rearrange("p h e -> p (h e)")))
                yield
        norm_steps(4)
        for j in range(T // 512):
            for i_, _ in enumerate(qkv_tile(j)):
                if i_ % 3 == 1:
                    norm_steps(1)
        norm_steps(T // 128)


def l0_diffattn(k, cst, QTd, KTd, Vad, mixT0, ident, flag, co_setup=None):
    b = k.b
    with b.phase():
        QT = b.sb("QT0", [128, 4, T], BF16)
        KT = b.sb("KT0", [128, 4, T], BF16)
        VA = b.sb("VA0", [128, 32, 4 * 129], BF16)
        for m in range(4):
            for hh in range(2):
                b.dma("sp", QT.k(m)[:, m, hh * 2048:(hh + 1) * 2048], V(QTd, QTd.t[m * 128:(m + 1) * 128, hh * 2048:(hh + 1) * 2048]))
                b.dma("sp", KT.k(m)[:, m, hh * 2048:(hh + 1) * 2048], V(KTd, KTd.t[m * 128:(m + 1) * 128, hh * 2048:(hh + 1) * 2048]))
        for s in range(32):
            b.dma("sp", VA.k(s)[:, s, :], V(Vad, Vad.t[s * 128:(s + 1) * 128, :]))
        lv = b.sb("lv", [128, 4, 64])
        for i, nm in enumerate(("lam_q1", "lam_k1", "lam_q2", "lam_k2")):
            b.dma("sp", lv[:, i, :], V(k.d[nm], k.d[nm].t.partition_broadcast(128)))
        ls = b.sb("ls", [128, 8])
        lj = b.sb("lj", [128, 64])
        b.tt(lj[:], lv[:, 0, :], lv[:, 1, :], ALU.mult)
        b.reduce(ls[:, 0:1], lj[:])
        b.tt(lj[:], lv[:, 2, :], lv[:, 3, :], ALU.mult)
        b.reduce(ls[:, 1:2], lj[:])
        b.act(ls[:, 2:4], ls[:, 0:2], AF.Exp)
        b.tt(ls[:, 4:5], ls[:, 2:3], ls[:, 3:4], ALU.subtract)
        b.ts(ls[:, 5:6], ls[:, 4:5], -1.0, ALU.mult, -0.2, ALU.add)
        sw = b.sb("sw", [128, 128])
        b.dma("sp", sw[:], V(k.d["subln_w"], k.d["subln_w"].t.partition_broadcast(128)))
        b.ts(sw[:], sw[:], 0.8, ALU.mult)
        PT = [b.sb("PT0_%d" % i, [128, 1024], BF16) for i in range(3)]
        o1 = [b.sb("o1_%d" % i, [128, 128]) for i in range(2)]
        ob = [b.sb("ob_%d" % i, [128, 128], BF16) for i in range(2)]
        oj = b.sb("oj", [128, 128])
        aT = [b.sb("aT%d" % i, [128, 512], BF16) for i in range(2)]
        accs = [b.sb("accs%d" % i, [128, 3, 512]) for i in range(2)]
        st_t = b.sb("dst", [128, 64])
        S = [b.ps("S0_%d" % i, [128, 1024]) for i in range(2)]
        ACC = [b.ps("AC0_%d" % i, [128, 512]) for i in range(3)]
        TP = b.ps("TP0", [128, 512])
        it = 0
        nsub = 0
        cogens = co_setup(TP) if co_setup is not None else []

        def advance():
            for g_ in list(cogens):
                try:
                    next(g_)
                except StopIteration:
                    cogens.remove(g_)

        def emit_qk(h, qb, kt, it_):
            sp = S[it_ % 2]
            for c in range(2):
                b.mm(sp[:, c * 512:(c + 1) * 512], KT.k(h)[c * 64:(c + 1) * 64, h, kt * 128:(kt + 1) * 128],
                     QT.k(h)[c * 64:(c + 1) * 64, h, qb * 512:(qb + 1) * 512], start=True, stop=True)
        iters = [(h, qb, kt) for h in range(4) for qb in range(8) for kt in range(32)]
        emit_qk(*iters[0], 0)
        for h in range(4):
            for qb in range(8):
                for kt in range(32):
                    sp = S[it % 2]
                    pt = PT[it % 3]
                    if it + 1 < len(iters):
                        emit_qk(*iters[it + 1], it + 1)
                    cross = (kt < 16) != (qb < 4)
                    if cross:
                        b.act(pt[:], sp[:], AF.Exp, scale=0.125, bias=flag[:, 1:2])
                    else:
                        b.act(pt[:], sp[:], AF.Exp, scale=0.125)
                    for c in range(2):
                        for qs in range(4):
                            gi = c * 4 + qs
                            acc = ACC[gi // 3]
                            co = (gi % 3) * 129
                            b.mm(acc[:, co:co + 129], pt[:, c * 512 + qs * 128:c * 512 + (qs + 1) * 128], VA.k(kt)[:, kt, h * 129:(h + 1) * 129],
                                 start=(kt == 0 and gi % 3 == 0), stop=(kt == 31), skip_group_check=True)
                    it += 1
                    if it % CO_EVERY == 0:
                        advance()
                asb = accs[(h * 8 + qb) % 2]
                for i_ in range(3):
                    w_ = 387 if i_ < 2 else 258
                    b.copy(asb[:, i_, 0:w_], ACC[i_][:, 0:w_], eng="dve")
                tp = TP
                tpb = tp.t[:, 0:256].bitcast(BF16)
                for qs in range(4):
                    st = _Cols(st_t, (nsub % 2) * 16)
                    a0 = _Bank(asb, qs // 3)
                    c0 = (qs % 3) * 129
                    a1 = _Bank(asb, (4 + qs) // 3)
                    c1 = ((4 + qs) % 3) * 129
                    b.recip(st[:, 0:1], a0[:, c0 + 128:c0 + 129])
                    b.recip(st[:, 1:2], a1[:, c1 + 128:c1 + 129])
                    b.tt(st[:, 2:3], st[:, 1:2], ls[:, 5:6], ALU.mult)
                    o = o1[nsub % 2]
                    b.ts(o[:], a0[:, c0:c0 + 128], st[:, 0:1], ALU.mult)
                    b.stt(o[:], a1[:, c1:c1 + 128], st[:, 2:3], o[:], ALU.mult, ALU.add)
                    b.act(oj[:], o[:], AF.Square, accum=st[:, 3:4])
                    b.act(st[:, 4:5], st[:, 3:4], AF.Sqrt, bias=1e-5, scale=1.0 / 128)
                    b.recip(st[:, 5:6], st[:, 4:5])
                    obf = ob[nsub % 2]
                    b.stt(obf[:], o[:], st[:, 5:6], sw[:], ALU.mult, ALU.mult)
                    b.tr(V(tp, tpb[:, qs * 128:(qs + 1) * 128]), obf[:], ident[:])
                    nsub += 1
                at = aT[(h * 8 + qb) % 2]
                b.copy(at[:], V(tp, tpb), eng="act")
                b.dma("sp", V(mixT0, mixT0.t[h * 128:(h + 1) * 128, qb * 512:(qb + 1) * 512], (h, qb)), at[:])
        while cogens:
            advance()


CDEC = 0.6065306597126334
NTL = 2
STAGGER = 0
CO_EVERY = 2
RW_BURST = 2


def l0_rwproj(k, xnT, cst, rwd):
    b = k.b
    d = k.d
    with b.phase():
        W1 = b.sb("W1", [128, 8, 1536], BF16)
        W2 = b.sb("W2", [128, 8, 1536], BF16)
        La = b.sb("La", [128, 8, 416], BF16)
        Lh = b.sb("Lh", [128, 8, 416], BF16)
        L2a = b.sb("L2a", [128, 512], BF16)
        L2b = b.sb("L2b", [128, 512], BF16)
        L2g = b.sb("L2g", [128, 512], BF16)
        L2g2 = b.sb("L2g2", [32, 512], BF16)
        b.dma("pool", L2a[0:64, :], d["w2_f"][:])
        b.dma("pool", L2a[64:128, :], d["w2_b"][:])
        b.dma("pool", L2b[0:64, :], d["a2_f"][:])
        b.dma("pool", L2b[64:128, :], d["a2_b"][:])
        b.dma("pool", L2g[:], V(d["g2"], d["g2"].t[0:128, :]))
        b.dma("pool", L2g2[:], V(d["g2"], d["g2"].t[128:160, :]))
        with b.phase():
            mub = b.sb("mub", [128, 1536])
            for i, nm in enumerate(("mu_r", "mu_k", "mu_v")):
                b.dma("sp", mub[:, i * 512:(i + 1) * 512], V(d[nm], d[nm].t.partition_broadcast(128)))
            omm = b.sb("omm", [128, 1536])
            hm = b.sb("hm", [128, 1536])
            b.ts(omm[:], mub[:], -1.0, ALU.mult, 1.0, ALU.add)
            b.ts(hm[:], mub[:], 0.5, ALU.mult)
            mus = b.sb("mus", [128, 3, 8])
            for i, nm in enumerate(("mu_w", "mu_a", "mu_g")):
                dma_nc(b, "sp", mus[:, i, :], V(d[nm], d[nm].t.rearrange("(kc p) -> p kc", p=128)))
            omm3 = b.sb("omm3", [128, 3, 8])
            hm3 = b.sb("hm3", [128, 3, 8])
            b.ts(omm3[:], mus[:], -1.0, ALU.mult, 1.0, ALU.add)
            b.ts(hm3[:], mus[:], 0.5, ALU.mult)
            wf = [b.sb("wf%d" % i, [128, 1536]) for i in range(2)]
            lf = [b.sb("lf%d" % i, [128, 416]) for i in range(2)]
            win = d["w_in0"]
            for kc in range(8):
                w = wf[kc % 2]
                b.dma("sp", w[:], V(win, win.t[kc * 128:(kc + 1) * 128, 1536:3072]))
                b.tt(W1[:, kc, :], w[:], omm[:], ALU.mult)
                b.tt(W2[:, kc, :], w[:], hm[:], ALU.mult, eng="pool")
                l = lf[kc % 2]
                for (nm, c0, cw) in (("w1_f", 0, 64), ("w1_b", 64, 64), ("a1_f", 128, 64), ("a1_b", 192, 64), ("g1", 256, 160)):
                    b.dma("sp", l[:, c0:c0 + cw], V(d[nm], d[nm].t[kc * 128:(kc + 1) * 128, :]))
                for gi, (c0, c1) in enumerate(((0, 128), (128, 256), (256, 416))):
                    b.ts(La[:, kc, c0:c1], l[:, c0:c1], omm3[:, gi, kc:kc + 1], ALU.mult)
                    b.ts(Lh[:, kc, c0:c1], l[:, c0:c1], hm3[:, gi, kc:kc + 1], ALU.mult)
        xsh = [b.sb("xsh%d" % i, [128, 8, 512], BF16) for i in range(2)]
        h1 = [[b.sb("h1_%d_%d" % (i, g), [128, 512], BF16) for g in range(4)] for i in range(2)]
        rw = [b.sb("rw%d" % i, [128, 8, 512]) for i in range(2)]
        PL = [b.ps("PL%d" % i, [128, 512]) for i in range(2)]
        PT_ = [b.ps("PTk%d" % i, [128, 512]) for i in range(4)]
        npl = 0
        npt = 0
        for j in range(T // 512):
            c0 = colf0(j * 512)
            xs = xsh[j % 2]
            b.tt(xs[:], xnT[:, :, c0 - 1:c0 + 511], xnT[:, :, c0 + 1:c0 + 513], ALU.add, eng="pool")
            hh = h1[j % 2]
            for gi, (r0, nr, fn) in enumerate(((0, 128, AF.Tanh), (128, 128, AF.Copy), (256, 128, AF.Sigmoid), (384, 32, AF.Sigmoid))):
                p = PL[npl % 2]
                npl += 1
                for kc in range(8):
                    b.mm(p[0:nr, :], La[:, kc, r0:r0 + nr], xnT[:, kc, c0:c0 + 512], start=(kc == 0), stop=False)
                for kc in range(8):
                    b.mm(p[0:nr, :], Lh[:, kc, r0:r0 + nr], xs[:, kc, :], start=False, stop=(kc == 7))
                if fn == AF.Copy:
                    b.copy(hh[gi][0:nr, :], p[0:nr, :], eng="act")
                else:
                    b.act(hh[gi][0:nr, :], p[0:nr, :], fn)
            for s4 in range(4):
                s = j * 4 + s4
                r = rw[s % 2]
                cs = c0 + s4 * 128
                ts_ = slice(s4 * 128, (s4 + 1) * 128)
                for q in range(8):
                    p = PT_[npt % 4]
                    npt += 1
                    if q < 3:
                        for kc in range(8):
                            b.mm(p[:], xnT[:, kc, cs:cs + 128], W1[:, kc, q * 512:(q + 1) * 512], start=(kc == 0), stop=False)
                        for kc in range(8):
                            b.mm(p[:], xs[:, kc, ts_], W2[:, kc, q * 512:(q + 1) * 512], start=False, stop=(kc == 7))
                    elif q == 3:
                        b.mm(p[:], hh[0][0:64, ts_], L2a[0:64, :])
                    elif q == 4:
                        b.mm(p[:], hh[0][64:128, ts_], L2a[64:128, :])
                    elif q == 5:
                        b.mm(p[:], hh[1][0:64, ts_], L2b[0:64, :])
                    elif q == 6:
                        b.mm(p[:], hh[1][64:128, ts_], L2b[64:128, :])
                    else:
                        b.mm(p[:], hh[2][:, ts_], L2g[:], start=True, stop=False)
                        b.mm(p[:], hh[3][0:32, ts_], L2g2[0:32, :], start=False, stop=True)
                    b.copy(r[:, q, :], p[:], eng=("act" if q % 2 == 0 else "dve"))
                b.dma("sp", V(rwd, rwd.t[s * 128:(s + 1) * 128, :, :], s), r[:])


def l0_rwkv(k, cst, rwd, mixT0, ident, Yd=None, flag=None, stage=9, do_post=True):
    b = k.b
    d = k.d
    if Yd is None:
        Yd = k.dram("Yd", [2, T, 512], F32)
    NT = T // 128
    with b.phase():
        def bc(nm):
            t = b.sb("bc_" + nm, [128, 512])
            src = d[nm].t
            if len(src.shape) == 2:
                src = src.rearrange("h n -> (h n)")
            b.dma("sp", t[:], V(d[nm], src.partition_broadcast(128)))
            return t
        w0 = [bc("w0_f"), bc("w0_b")]
        a0 = [bc("a0_f"), bc("a0_b")]
        kkb = bc("k_k")
        kab = bc("k_a")
        omka = b.sb("omka", [128, 512])
        b.ts(omka[:], kab[:], -1.0, ALU.mult, 1.0, ALU.add)
        tri = b.sb("tri", [128, 6, 128])
        b.dma("sp", tri[:], cst["tri"])
        irep = b.sb("irep", [128, 512])
        b.dma("sp", irep[:], cst["irep"])
        SU, IU, SL, IL = 0, 1, 2, 3
        M4 = []
        MQ = []
        for dr in range(2):
            s_, i_, sp_ = (SU, IU, SL) if dr == 0 else (SL, IL, SU)
            m4 = b.sb("M4_%d" % dr, [128, 4, 128], BF16)
            mq = b.sb("MQ_%d" % dr, [128, 4, 128], BF16)
            for j in range(4):
                b.copy(m4[:, j, :], tri[:, (s_ if j % 2 == 0 else i_), :], eng="pool")
                b.copy(mq[:, j, :], tri[:, sp_, :], eng="pool")
            M4.append(m4)
            MQ.append(mq)
        CS = [(IU, SU, SL), (IL, SL, SU)]
        GS = [[b.ps("Gp%d_%d" % (d_, i), [128, 512]) for i in range(3)] for d_ in range(2)]
        PYS = [b.ps("PY%d" % i, [128, 512]) for i in range(2)]
        gcnt = [0, 0]

        class DirState:
            pass
        DS = []
        for dr in range(2):
            s = DirState()
            s.rwb = [b.sb("rw%d_%d" % (i, dr), [128, 5, 512]) for i in range(2)]
            s.f = [b.sb("f%d_%d" % (i, dr), [128, 512]) for i in range(8)]
            s.st = b.sb("st_%d" % dr, [128, 32])
            s.h16 = {nm: b.sb("%s_%d" % (nm, dr), [128, 512], BF16) for nm in ("Rt", "Kt", "Bt", "Kp", "Kh", "Bh", "v16", "AV")}
            s.U = [b.sb("U%d_%d" % (c, dr), [128, 512], BF16) for c in range(2)]
            s.vz = [b.sb("vz%d_%d" % (c, dr), [128, 512], BF16) for c in range(2)]
            s.RTz = b.sb("RTz_%d" % dr, [128, 8, 128], BF16)
            for c in range(2):
                b.memset(s.U[c][:], 0.0)
                b.memset(s.vz[c][:], 0.0)
            b.memset(s.RTz[:], 0.0)
            s.Dg = [b.sb("Dg%d_%d" % (c, dr), [128, 512]) for c in range(2)]
            s.XT = b.sb("XT_%d" % dr, [128, 4, 4, 128], BF16)
            s.AM = b.sb("AM_%d" % dr, [128, 8, 4, 128], BF16)
            s.P = [b.sb("P%d_%d" % (i, dr), [128, 8, 128], BF16) for i in range(2)]
            s.PT = [b.sb("PT%d_%d" % (i, dr), [128, 8, 128], BF16) for i in range(2)]
            s.S = [b.sb("S%d_%d" % (i, dr), [128, 8, 128], BF16) for i in range(2)]
            s.WT = b.sb("WT_%d" % dr, [128, 8, 128], BF16)
            b.memset(s.WT[:], 0.0)
            s.H32 = [b.sb("H32_%d_%d" % (i, dr), [128, 4, 64]) for i in range(2)]
            s.H16 = [b.sb("H16_%d_%d" % (i, dr), [128, 4, 64], BF16) for i in range(2)]
            s.hi = 0
            s.Y = b.sb("Yt_%d" % dr, [128, 512])
            b.memset(s.H32[0][:], 0.0)
            b.memset(s.H16[0][:], 0.0)
            DS.append(s)

        def v3(view_tile, ap):
            return V(view_tile, ap.rearrange("p (h n) -> p h n", n=64))

        tcount = [0, 0]

        def load_rw(dr, ti, dst):
            rows = slice(ti * 128, (ti + 1) * 128)
            b.dma("sp", dst[:, 0:3, :], V(rwd, rwd.t[rows, 0:3, :]))
            b.dma("sp", dst[:, 3, :], V(rwd, rwd.t[rows, 3 + dr, :]))
            b.dma("sp", dst[:, 4, :], V(rwd, rwd.t[rows, 5 + dr, :]))

        def rw_tile(dr, ti):
            s = DS[dr]
            rw = s.rwb[tcount[dr] % 2]
            f = s.f
            h = s.h16
            st = s.st
            PY = PYS[dr]

            def gp():
                gcnt[dr] += 1
                return GS[dr][gcnt[dr] % 3]
            if tcount[dr] == 0:
                load_rw(dr, ti, rw)
            tn = ti + 1 if dr == 0 else ti - 1
            if 0 <= tn < NT:
                load_rw(dr, tn, s.rwb[(tcount[dr] + 1) % 2])
            tcount[dr] += 1
            yield
            r_, k_, v_ = rw[:, 0, :], rw[:, 1, :], rw[:, 2, :]
            b.tt(f[0][:], rw[:, 3, :], w0[dr][:], ALU.add)
            yield
            b.act(f[0][:], f[0][:], AF.Sigmoid)
            yield
            b.tt(f[1][:], rw[:, 4, :], a0[dr][:], ALU.add, eng="pool")
            yield
            b.act(f[1][:], f[1][:], AF.Sigmoid)
            yield
            b.tt(f[2][:], k_, kkb[:], ALU.mult)
            yield
            b.act(f[3][:], f[2][:], AF.Square)
            yield
            b.reduce(st[:, 0:8], v3(f[3], f[3].t[:]))
            yield
            b.act(st[:, 8:16], st[:, 0:8], AF.Sqrt)
            yield
            b.ts(st[:, 8:16], st[:, 8:16], 1e-12, ALU.max)
            yield
            b.recip(st[:, 16:24], st[:, 8:16])
            yield
            b.tt(v3(f[2], f[2].t[:]), v3(f[2], f[2].t[:]),
                 V(st, st.t[:, 16:24].unsqueeze(2).to_broadcast([128, 8, 64])), ALU.mult)
            yield
            b.tt(f[3][:], f[1][:], kab[:], ALU.mult, eng="pool")
            yield
            b.tt(f[3][:], f[3][:], omka[:], ALU.add, eng="pool")
            yield
            b.tt(f[3][:], f[3][:], k_, ALU.mult, eng="pool")
            yield
            b.stt(f[4][:], f[2][:], -1.0, f[1][:], ALU.mult, ALU.mult)
            yield
            b.copy(h["v16"][:], v_, eng="act")
            yield
            for c in range(2):
                b.copy(s.vz[c][c * 64:(c + 1) * 64, :], rw[c * 64:(c + 1) * 64, 2, :], eng="act")
                yield
            ci, ce, ca = CS[dr]
            pi = gp()
            b.mm(pi[:], tri[:, ci, :], f[0][:])
            yield
            b.act(f[5][:], pi[:], AF.Exp, scale=-CDEC)
            yield
            b.act(f[6][:], pi[:], AF.Exp, scale=CDEC)
            yield
            b.tt(h["Rt"][:], r_, f[5][:], ALU.mult)
            yield
            b.tt(h["Kt"][:], f[3][:], f[6][:], ALU.mult)
            yield
            b.tt(h["Bt"][:], f[4][:], f[6][:], ALU.mult)
            yield
            pe = gp()
            b.mm(pe[:], tri[:, ce, :], f[0][:])
            yield
            b.act(f[5][:], pe[:], AF.Exp, scale=-CDEC)
            yield
            b.tt(h["Kp"][:], f[2][:], f[5][:], ALU.mult)
            yield
            pa = gp()
            b.mm(pa[:], tri[:, ca, :], f[0][:])
            yield
            b.act(f[6][:], pa[:], AF.Exp, scale=-CDEC)
            yield
            b.tt(h["Kh"][:], f[3][:], f[6][:], ALU.mult, eng="pool")
            yield
            b.tt(h["Bh"][:], f[4][:], f[6][:], ALU.mult, eng="pool")
            yield
            p0 = gp()
            b.mm(p0[:], tri[:, 4, :], f[0][:])
            yield
            b.act(f[5][:], p0[:], AF.Exp, scale=-CDEC)
            yield
            b.tt(s.Dg[0][:], f[5][:], irep[:], ALU.mult, eng="pool")
            yield
            p1 = gp()
            b.mm(p1[:], tri[:, 5, :], f[0][:])
            yield
            b.act(f[7][:], p1[:], AF.Exp, scale=-CDEC)
            yield
            b.tt(s.Dg[1][:], f[7][:], irep[:], ALU.mult, eng="pool")
            yield
            for half in range(2):
                pt = gp()
                ptb = pt.t[:, 0:512].bitcast(BF16)
                for qi2 in range(2):
                    qi = half * 2 + qi2
                    src = h[("Kt", "Bt", "Kp", "Rt")[qi]]
                    for hp in range(4):
                        b.tr(V(pt, ptb[:, (qi2 * 4 + hp) * 128:(qi2 * 4 + hp + 1) * 128]), src[:, hp * 128:(hp + 1) * 128], ident[:])
                        yield
                for qi2 in range(2):
                    b.copy(V(s.XT, s.XT.t[:, :, half * 2 + qi2, :]),
                           V(pt, ptb[:, qi2 * 512:(qi2 + 1) * 512].rearrange("p (hp t) -> p hp t", hp=4)), eng=("act" if qi2 == 0 else "dve"))
                    yield
            b.copy(V(s.RTz, s.RTz.t[0:64, 0:8:2, :]), s.XT[0:64, :, 3, :], eng="act")
            yield
            b.copy(V(s.RTz, s.RTz.t[64:128, 1:8:2, :]), s.XT[64:128, :, 3, :], eng="act")
            yield
            if stage < 2:
                return
            pq = None
            for hd in range(8):
                hp, pb = hd // 2, (hd % 2) * 64
                p12 = gp()
                rhs = V(s.XT, s.XT.t[pb:pb + 64, hp, 2:4, :].rearrange("p a t -> p (a t)"))
                b.mm(p12[:, 0:256], s.XT[pb:pb + 64, hp, 0, :], rhs)
                yield
                b.mm(p12[:, 256:512], s.XT[pb:pb + 64, hp, 1, :], rhs)
                yield
                if stage >= 2.2:
                    b.tt(V(s.AM, s.AM.t[:, hd, :, :]), V(p12, p12.t[:].rearrange("p (a t) -> p a t", a=4)), M4[dr][:], ALU.mult)
                    yield
            for par in range(2):
                pq = gp()
                pb = par * 64
                for j in range(4):
                    hd = 2 * j + par
                    b.mm(pq[:, j * 128:(j + 1) * 128], s.XT[pb:pb + 64, j, 2, :], s.XT[pb:pb + 64, j, 1, :])
                    yield
                b.copy(s.f[7][:], pq[:], eng="act")
                yield
                b.tt(V(s.PT[0], s.PT[0].t[:, par:8:2, :]), V(s.f[7], s.f[7].t[:].rearrange("p (a t) -> p a t", a=4)),
                     MQ[dr][:], ALU.mult, eng="pool")
                yield
            if stage < 3:
                return
            b.copy(s.P[0][:], V(s.AM, s.AM.t[:, :, 2, :]), eng="act")
            yield
            b.tt(s.S[0][:], V(s.AM, s.AM.t[:, :, 2, :]), V(ident, ident.t[:].unsqueeze(1).to_broadcast([128, 8, 128])), ALU.add, eng="pool")
            yield
            cur = 0
            for lev in range(1, 6):
                nxt = 1 - cur
                for g4 in range(2):
                    hs = range(g4 * 4, g4 * 4 + 4)
                    if lev < 5:
                        p = gp()
                        for hd in hs:
                            b.mm(p[:, (hd % 4) * 128:(hd % 4 + 1) * 128], s.PT[cur][:, hd, :], s.P[cur][:, hd, :])
                            yield
                        b.copy(V(s.P[nxt], s.P[nxt].t[:, g4 * 4:g4 * 4 + 4, :]), V(p, p.t[:].rearrange("p (a t) -> p a t", a=4)), eng="act")
                        yield
                    p = gp()
                    for hd in hs:
                        b.mm(p[:, (hd % 4) * 128:(hd % 4 + 1) * 128], s.P[cur][:, hd, :], s.PT[cur][:, hd, :])
                        yield
                    b.copy(V(s.PT[nxt], s.PT[nxt].t[:, g4 * 4:g4 * 4 + 4, :]), V(p, p.t[:].rearrange("p (a t) -> p a t", a=4)),
                           eng=("act" if g4 == 0 else "dve"))
                    yield
                for g4 in range(2):
                    hs = range(g4 * 4, g4 * 4 + 4)
                    p = gp()
                    for hd in hs:
                        o = p[:, (hd % 4) * 128:(hd % 4 + 1) * 128]
                        b.mm(o, s.PT[nxt][:, hd, :], s.S[cur][:, hd, :])
                        yield
                    b.tt(V(s.S[nxt], s.S[nxt].t[:, g4 * 4:g4 * 4 + 4, :]), V(p, p.t[:].rearrange("p (a t) -> p a t", a=4)),
                         V(s.S[cur], s.S[cur].t[:, g4 * 4:g4 * 4 + 4, :]), ALU.add)
                    yield
                cur = nxt
            TT = s.S[cur]
            if stage < 4:
                return
            p = gp()
            for hd in range(8):
                b.mm(p[:, hd * 64:(hd + 1) * 64], s.AM[:, hd, 0, :], h["v16"][:, hd * 64:(hd + 1) * 64])
                yield
            b.copy(h["AV"][:], p[:], eng="act")
            yield
            p = gp()
            for hd in range(8):
                hp, pb = hd // 2, (hd % 2) * 64
                b.mm(p[pb:pb + 64, hp * 128:(hp + 1) * 128], h["Kp"][:, hd * 64:(hd + 1) * 64], TT[:, hd, :])
                yield
            b.copy(V(s.WT, s.WT.t[0:64, 0:8:2, :]), V(p, p.t[0:64, :].rearrange("p (a t) -> p a t", a=4)), eng="act")
            yield
            b.copy(V(s.WT, s.WT.t[64:128, 1:8:2, :]), V(p, p.t[64:128, :].rearrange("p (a t) -> p a t", a=4)), eng="act")
            yield
            if stage < 5:
                return
            if (dr == 0 and ti == NT // 2) or (dr == 1 and ti == NT // 2 - 1):
                b.ts(s.H32[s.hi][:], s.H32[s.hi][:], flag[:, 0:1], ALU.mult)
                yield
                b.ts(s.H16[s.hi][:], s.H16[s.hi][:], flag[:, 0:1], ALU.mult)
                yield
            for c in ((0, 1) if dr == 0 else (1, 0)):
                cb = c * 64
                cs = slice(cb, cb + 64)
                Ho32, Ho16 = s.H32[s.hi], s.H16[s.hi]
                Hn32, Hn16 = s.H32[1 - s.hi], s.H16[1 - s.hi]
                s.hi = 1 - s.hi
                PH = gp()
                for hd in range(8):
                    hp, pb = hd // 2, (hd % 2) * 64
                    hc = slice(hd * 64, (hd + 1) * 64)
                    o = PH[pb:pb + 64, hp * 64:(hp + 1) * 64]
                    b.mm(o, s.Dg[c][:, hc], Ho32[:, hp, :], start=(hd < 2), stop=False, skip_group_check=True)
                    yield
                    b.mm(o, h["Kh"][:, hc], s.vz[c][:, hc], start=False, stop=False, skip_group_check=True)
                    yield
                PU = gp()
                for hd in range(8):
                    hp = hd // 2
                    hc = slice(hd * 64, (hd + 1) * 64)
                    o = PU[cs, hc]
                    b.mm(o, TT[:, hd, cs], h["AV"][:, hc], start=True, stop=False, skip_group_check=True)
                    yield
                    b.mm(o, s.WT[:, hd, cs], Ho16[:, hp, :], start=False, stop=True, skip_group_check=True)
                    yield
                b.copy(s.U[c][cs, :], PU[cs, :], eng="act")
                yield
                for hd in range(8):
                    hp, pb = hd // 2, (hd % 2) * 64
                    hc = slice(hd * 64, (hd + 1) * 64)
                    o = PH[pb:pb + 64, hp * 64:(hp + 1) * 64]
                    b.mm(o, h["Bh"][:, hc], s.U[c][:, hc], start=False, stop=True, skip_group_check=True)
                    yield
                b.copy(V(Hn32, Hn32.t[:].rearrange("p a n -> p (a n)")), PH[:, 0:256], eng="act")
                yield
                b.copy(Hn16[:], Hn32[:], eng="pool")
                yield
                for hd in range(8):
                    hp = hd // 2
                    hc = slice(hd * 64, (hd + 1) * 64)
                    o = PY[cs, hc]
                    b.mm(o, s.RTz[:, hd, cs], Ho16[:, hp, :], start=True, stop=False, skip_group_check=True)
                    yield
                    b.mm(o, s.AM[:, hd, 1, cs], h["v16"][:, hc], start=False, stop=False, skip_group_check=True)
                    yield
                    b.mm(o, s.AM[:, hd, 3, cs], s.U[c][:, hc], start=False, stop=True, skip_group_check=True)
                    yield
            b.copy(s.Y[:], PY[:], eng="act")
            yield
            b.dma("sp", V(Yd, Yd.t[dr, ti * 128:(ti + 1) * 128, :], (dr, ti)), s.Y[:])
            yield

        nloop = NT if stage >= 9 else min(NT, NTL)

        def stream(dr):
            for i in range(nloop):
                yield from rw_tile(dr, i if dr == 0 else NT - 1 - i)
        alive = [stream(0), stream(1)]
        for _ in range(STAGGER):
            next(alive[0])
        while alive:
            for g_ in list(alive):
                try:
                    for _ in range(RW_BURST):
                        next(g_)
                except StopIteration:
                    alive.remove(g_)
    if stage < 9:
        return

    if not do_post:
        return
    with b.phase():
        gens = rwkv_post_setup(k, rwd, Yd, mixT0, ident, None, 4, "pool")
        while gens:
            for g_ in list(gens):
                try:
                    next(g_)
                except StopIteration:
                    gens.remove(g_)


def rwkv_post_setup(k, rwd, Yd, mixT0, ident, TP, NS, e2):
    b = k.b
    d = k.d
    NT = T // 128
    if True:
        def bc2(nm):
            t = b.sb("bc_" + nm, [128, 512])
            src = d[nm].t
            if len(src.shape) == 2:
                src = src.rearrange("h n -> (h n)")
            b.dma("sp", t[:], V(d[nm], src.partition_broadcast(128)))
            return t
        a0 = [bc2("a0_f"), bc2("a0_b")]
        kab = bc2("k_a")
        rkb = bc2("r_k")
        lw = bc2("lnx_w")
        lb = bc2("lnx_b")
        omka = b.sb("omka2", [128, 512])
        b.ts(omka[:], kab[:], -1.0, ALU.mult, 1.0, ALU.add)
        b.ts(kab[:], kab[:], 0.5, ALU.mult)
        rws = [b.sb("rwp%d" % i, [128, 8, 512]) for i in range(NS)]
        ys = [b.sb("yp%d" % i, [128, 2, 512]) for i in range(NS)]
        fs = [[b.sb("pf%d_%d" % (i, j), [128, 512]) for i in range(5)] for j in range(NS)]
        st_t = b.sb("pst", [128, 32 * NS])
        ob = [b.sb("pob%d" % i, [128, 512], BF16) for i in range(NS)]
        oT = [b.sb("poT%d" % i, [128, 4, 128], BF16) for i in range(NS)]
        PTt = [TP] * NS if TP is not None else [b.ps("PTp%d" % i, [128, 512]) for i in range(NS)]

        def v3(view_tile, ap):
            return V(view_tile, ap.rearrange("p (h n) -> p h n", n=64))

        def bc8(tile, c0, key):
            return V(tile, tile.t[:, c0:c0 + 8].unsqueeze(2).to_broadcast([128, 8, 64]), key)

        def post_tile(j, ti):
            rw = rws[j]
            yy = ys[j]
            f = fs[j]
            off = j * 32
            st = _Cols(st_t, off)
            b.dma("sp", rw[:, 0:3, :], V(rwd, rwd.t[ti * 128:(ti + 1) * 128, 0:3, :]))
            b.dma("sp", rw[:, 5:8, :], V(rwd, rwd.t[ti * 128:(ti + 1) * 128, 5:8, :]))
            for dr in range(2):
                b.dma("sp", yy[:, dr, :], V(Yd, Yd.t[dr, ti * 128:(ti + 1) * 128, :]))
            yield
            y = f[0]
            b.tt(y[:], yy[:, 0, :], yy[:, 1, :], ALU.add)
            yield
            b.reduce(st[:, 0:8], v3(y, y.t[:]))
            yield
            b.ts(st[:, 0:8], st[:, 0:8], 1.0 / 64, ALU.mult)
            yield
            b.tt(v3(y, y.t[:]), v3(y, y.t[:]), bc8(st_t, off, off), ALU.subtract)
            yield
            b.tt(f[1][:], y[:], y[:], ALU.mult)
            yield
            b.reduce(st[:, 8:16], v3(f[1], f[1].t[:]))
            yield
            b.act(st[:, 16:24], st[:, 8:16], AF.Sqrt, bias=64e-5, scale=1.0 / 64)
            yield
            b.recip(st[:, 24:32], st[:, 16:24])
            yield
            b.tt(v3(y, y.t[:]), v3(y, y.t[:]), bc8(st_t, off + 24, off), ALU.mult)
            yield
            b.tt(y[:], y[:], lw[:], ALU.mult, eng=e2)
            yield
            b.tt(y[:], y[:], lb[:], ALU.add, eng=e2)
            yield
            b.tt(f[2][:], rw[:, 5, :], a0[0][:], ALU.add, eng=e2)
            yield
            b.act(f[2][:], f[2][:], AF.Exp, scale=-1.0)
            yield
            b.ts(f[2][:], f[2][:], 1.0, ALU.add)
            yield
            b.recip(f[2][:], f[2][:])
            yield
            b.tt(f[3][:], rw[:, 6, :], a0[1][:], ALU.add, eng=e2)
            yield
            b.act(f[3][:], f[3][:], AF.Exp, scale=-1.0)
            yield
            b.ts(f[3][:], f[3][:], 1.0, ALU.add)
            yield
            b.recip(f[3][:], f[3][:])
            yield
            b.tt(f[2][:], f[2][:], f[3][:], ALU.add, eng=e2)
            yield
            b.tt(f[2][:], f[2][:], kab[:], ALU.mult, eng=e2)
            yield
            b.tt(f[2][:], f[2][:], omka[:], ALU.add, eng=e2)
            yield
            b.tt(f[2][:], f[2][:], rw[:, 1, :], ALU.mult)
            yield
            b.tt(f[2][:], f[2][:], rw[:, 0, :], ALU.mult)
            yield
            b.tt(f[2][:], f[2][:], rkb[:], ALU.mult)
            yield
            b.reduce(st[:, 8:16], v3(f[2], f[2].t[:]))
            yield
            b.tt(v3(f[4], f[4].t[:]), v3(rw, rw.t[:, 2, :]), bc8(st_t, off + 8, off), ALU.mult)
            yield
            b.tt(y[:], y[:], f[4][:], ALU.add)
            yield
            o = ob[j]
            b.tt(o[:], y[:], rw[:, 7, :], ALU.mult)
            yield
            pt = PTt[j]
            ptb = pt.t[:, 0:256].bitcast(BF16)
            for c in range(4):
                b.tr(V(pt, ptb[:, c * 128:(c + 1) * 128]), o[:, c * 128:(c + 1) * 128], ident[:])
            ot = oT[j]
            b.copy(ot[:], V(pt, ptb.rearrange("p (c t) -> p c t", c=4)), eng="act")
            yield
            for c in range(4):
                b.dma("sp", V(mixT0, mixT0.t[512 + c * 128:512 + (c + 1) * 128, ti * 128:(ti + 1) * 128], ("r", c, ti)), ot[:, c, :])
            yield

        def pstream(j):
            for ti in range(j, NT, NS):
                yield from post_tile(j, ti)
        return [pstream(j) for j in range(NS)]


def make_consts(is_prompt):
    c = {}
    p = np.arange(128)[:, None]
    f = np.arange(128)[None, :]
    c["c_ident"] = np.eye(128, dtype=np.float32)
    partner = (np.arange(128) // 64) * 64 + (np.arange(128) % 64 + 32) % 64
    perm = np.zeros((128, 128), np.float32)
    perm[partner, np.arange(128)] = 1.0
    c["c_perm"] = perm
    NEG = -30000.0
    IU = np.where(p <= f, 0.0, NEG).astype(np.float32)
    IL = np.where(p >= f, 0.0, NEG).astype(np.float32)
    c["c_maskAB"] = np.concatenate([IU, IL], axis=1)
    if is_prompt:
        c["c_maskX"] = c["c_maskAB"].copy()
    else:
        a = IU.copy(); a[64:, :] = NEG
        bb = IL.copy(); bb[:64, :] = NEG
        c["c_maskX"] = np.concatenate([a, bb], axis=1)
    T = 4096
    S = 4096 if is_prompt else 2048
    pos = (np.arange(T) % S).astype(np.float32)
    half = 32
    inv = (10000.0 ** (-np.arange(half, dtype=np.float32) / half)).astype(np.float32)
    ang = pos[None, :] * inv[:, None]
    cos = np.cos(ang).astype(np.float32)
    sin = np.sin(ang).astype(np.float32)
    c["c_cos"] = np.tile(cos, (4, 1))
    c["c_sin"] = np.concatenate([-sin, sin, -sin, sin], axis=0)
    flag = np.zeros((128, 4), np.float32)
    flag[:, 0] = 1.0 if is_prompt else 0.0
    flag[:, 1] = 0.0 if is_prompt else NEG
    c["c_flag"] = flag
    blk = (p // 64) == (f // 64)
    SU = (blk & (p < f)).astype(np.float32)
    IUb = (blk & (p <= f)).astype(np.float32)
    SL = (blk & (p > f)).astype(np.float32)
    ILb = (blk & (p >= f)).astype(np.float32)
    T0 = np.broadcast_to((p < 64), (128, 128)).astype(np.float32)
    T1 = np.broadcast_to((p >= 64), (128, 128)).astype(np.float32)
    c["c_tri"] = np.stack([SU, IUb, SL, ILb, T0, T1], axis=1).reshape(128, 6 * 128).astype(np.float32)
    kk = np.arange(512)[None, :] % 64
    hh = np.arange(512)[None, :] // 64
    c["c_irep"] = (((p % 64) == kk) & ((p // 64) == (hh % 2))).astype(np.float32)
    return c


INPUT_NAMES = None


def build_program(upto="all", skip=()):
    nc = bass.Bass("TRN2", target_bir_lowering=False)
    k = K(nc)
    b = k.b
    shapes = {
        "mix_pre0": [D], "mix_post0": [D], "w_in0": [D, 3072], "lam_q1": [64], "lam_k1": [64], "lam_q2": [64], "lam_k2": [64],
        "subln_w": [128], "mu_r": [512], "mu_k": [512], "mu_v": [512], "mu_w": [D], "mu_a": [D], "mu_g": [D],
        "w0_f": [512], "w1_f": [D, 64], "w2_f": [64, 512], "w0_b": [512], "w1_b": [D, 64], "w2_b": [64, 512],
        "a0_f": [512], "a1_f": [D, 64], "a2_f": [64, 512], "a0_b": [512], "a1_b": [D, 64], "a2_b": [64, 512],
        "g1": [D, 160], "g2": [160, 512], "k_k": [512], "k_a": [512], "r_k": [8, 64], "lnx_w": [512], "lnx_b": [512],
        "w_out0": [D, D], "ffn_pre0": [D], "ffn_post0": [D], "ffn_gate0": [D, FH], "ffn_up0": [D, FH], "ffn_down0": [FH, D],
        "mix_pre1": [D], "mix_post1": [D], "w_in1": [D, 2304], "w_out1": [768, D], "ffn_pre1": [D], "ffn_post1": [D],
        "ffn_gate1": [D, FH], "ffn_up1": [D, FH], "ffn_down1": [FH, D],
        "c_ident": [128, 128], "c_perm": [128, 128], "c_maskAB": [128, 256], "c_maskX": [128, 256], "c_cos": [128, T], "c_sin": [128, T],
        "c_flag": [128, 4], "c_tri": [128, 6, 128], "c_irep": [128, 512],
    }
    for n, sh in shapes.items():
        k.dram(n, sh, F32, kind="ExternalInput")
    x = k.dram("x", [T, D], F32, kind="ExternalInput")
    y = k.dram("y", [T, D], F32, kind="ExternalOutput")
    if upto != "all":
        k.ext_out = {upto}
    cst = {"ident": k.d["c_ident"][:], "perm": k.d["c_perm"][:], "flag": k.d["c_flag"][:], "cos": k.d["c_cos"][:], "sin": k.d["c_sin"][:],
           "maskAB": k.d["c_maskAB"][:], "maskX": k.d["c_maskX"][:], "tri": k.d["c_tri"][:], "irep": k.d["c_irep"][:], "eps": EPS}
    prep_weight_rows(k, "w_in0", 8, 3072)

    def late_casts():
        r = {}
        r["wout0"] = prep_weight_rows(k, "w_out0", 8, D)
        prep_weights_ffn(k, 0)
        prep_weight_rows(k, "w_in1", 8, 2304)
        r["wout1"] = prep_weight_rows(k, "w_out1", 6, D)
        prep_weights_ffn(k, 1)
        return r
    mixT0 = k.dram("mixT0", [1024, T], BF16)
    mixT1 = k.dram("mixT1", [768, T], BF16)
    xmid = k.dram("xmid", [T, D], F32)
    if upto != "all":
        y.t
    QTd = k.dram("QTd", [512, T], BF16)
    KTd = k.dram("KTd", [512, T], BF16)
    Vad = k.dram("Vad", [T, 4 * 129], BF16)
    rwd = k.dram("rwd", [T, 8, 512], F32)
    with b.phase():
        ident = b.sb("ident", [128, 128], BF16)
        b.dma("pool", ident[:], cst["ident"])
        flag = b.sb("flagm", [128, 4])
        b.dma("sp", flag[:], cst["flag"])
        with b.phase():
            xnT = b.sb("xnT0", [128, 8, XW], BF16)
            l0_norm(k, x, xnT, ident, cst)
            if "qkv" not in skip:
                l0_qkv(k, xnT, cst, QTd, KTd, Vad)
            if "rwproj" not in skip:
                l0_rwproj(k, xnT, cst, rwd)
        Yd = k.dram("Yd", [2, T, 512], F32)
        if "rwkv" not in skip:
            l0_rwkv(k, cst, rwd, mixT0, ident, Yd, flag, do_post=False)
        lc = late_casts()
        wout0_b, wout1_b = lc["wout0"], lc["wout1"]
        if "diff" not in skip:
            l0_diffattn(k, cst, QTd, KTd, Vad, mixT0, ident, flag,
                        co_setup=lambda TP: rwkv_post_setup(k, rwd, Yd, mixT0, ident, TP, 2, "dve"))
    if upto == "mixT0":
        return nc, list(shapes.keys())
    phase_post(k, 0, 8, mixT0, wout0_b, x, xmid, cst)
    if upto == "xmid":
        return nc, list(shapes.keys())
    phase_l1(k, xmid, mixT1, cst)
    if upto == "mixT1":
        return nc, list(shapes.keys())
    phase_post(k, 1, 6, mixT1, wout1_b, xmid, y, cst)
    return nc, list(shapes.keys())


def kernel(**inputs):
    n = 8
    xp = np.asarray(inputs["x_prompt"], dtype=np.float32)
    xs = np.asarray(inputs["x_sample"], dtype=np.float32)
    nc, names = build_program()
    cp = make_consts(True)
    cs = make_consts(False)
    in_maps = []
    for c in range(n):
        m = {}
        prompt = c < 4
        cc = cp if prompt else cs
        for nm in names:
            if nm.startswith("c_"):
                a = cc[nm]
                if nm == "c_tri":
                    a = a.reshape(128, 6, 128)
                m[nm] = np.ascontiguousarray(a, dtype=np.float32)
            else:
                m[nm] = np.ascontiguousarray(np.asarray(inputs[nm], dtype=np.float32))
        if prompt:
            m["x"] = np.ascontiguousarray(xp[c])
        else:
            j = 2 * (c - 4)
            m["x"] = np.ascontiguousarray(xs[j:j + 2].reshape(T, D))
        in_maps.append(m)
    res = run_bass_kernel_spmd(nc, in_maps, core_ids=list(range(n)))
    outs = [np.asarray(r["y"], dtype=np.float32) for r in res.results]
    y_prompt = np.stack(outs[0:4], axis=0)
    y_sample = np.concatenate([o.reshape(2, T // 2, D) for o in outs[4:8]], axis=0)
    return (y_prompt, y_sample)
```

```python
import numpy as np
from contextlib import ExitStack
import concourse.bass as bass
import concourse.mybir as mybir

F32 = mybir.dt.float32
BF16 = mybir.dt.bfloat16
ALU = mybir.AluOpType
AF = mybir.ActivationFunctionType
AX = mybir.AxisListType


class Res:
    __slots__ = ("w", "r")

    def __init__(self):
        self.w = None
        self.r = {}


class View:
    __slots__ = ("tile", "ap", "key")

    def __init__(self, tile, ap, key):
        self.tile = tile
        self.ap = ap
        self.key = key


class _Keyed:
    def __init__(self, tile, key):
        self.tile = tile
        self.key = key

    def __getitem__(self, idx):
        return View(self.tile, self.tile.t[idx], self.key)


class Tile:
    def __init__(self, name, t):
        self.name = name
        self.t = t
        self.res = {None: Res()}

    def __getitem__(self, idx):
        return View(self, self.t[idx], None)

    def k(self, key):
        return _Keyed(self, key)

    def conflicts(self, key):
        if key is None:
            return list(self.res.values())
        if key not in self.res:
            self.res[key] = Res()
        return [self.res[None], self.res[key]]

    def get(self, key):
        if key not in self.res:
            self.res[key] = Res()
        return self.res[key]


NDMA = 56
SB_DEBUG = False
NSW = 32


class Builder:
    def __init__(self, nc):
        self.nc = nc
        self.eng = {"pe": nc.tensor, "dve": nc.vector, "act": nc.scalar, "pool": nc.gpsimd, "sp": nc.sync}
        self.sem = {e: nc.alloc_semaphore("sem_" + e) for e in ("pe", "dve", "act", "pool")}
        self.tick = {e: 0 for e in self.sem}
        self.dsem = [nc.alloc_semaphore("dsem%d" % i) for i in range(NDMA)]
        self.ssem = [nc.alloc_semaphore("ssem%d" % i) for i in range(NSW)]
        self.ndma = 0
        self.nsw = 0
        self.waited = {e: {} for e in self.eng}
        self.epoch = 0
        self.barA = nc.alloc_semaphore("barA")
        self.barB = nc.alloc_semaphore("barB")
        self.nwait = 0
        self.ninst = 0
        self.stack = None

    def phase(self):
        return _Phase(self)

    def sb(self, name, shape, dtype=F32):
        self.uid = getattr(self, "uid", 0) + 1
        name = "%s_u%d" % (name, self.uid)
        t = self.stack.enter_context(self.nc.sbuf_tensor(name, list(shape), dtype))
        self.sb_hi = max(getattr(self, "sb_hi", 0), self.nc.sbuf_base)
        if SB_DEBUG:
            print("SB", name, shape, "end", self.nc.sbuf_base)
        return Tile(name, t)

    def ps(self, name, shape, dtype=F32):
        self.uid = getattr(self, "uid", 0) + 1
        name = "%s_u%d" % (name, self.uid)
        t = self.stack.enter_context(self.nc.psum_tensor(name, list(shape), dtype))
        return Tile(name, t)

    def dram(self, name, shape, dtype, kind="Internal"):
        t = self.nc.dram_tensor(name, list(shape), dtype, kind=kind)
        return Tile(name, t.ap())

    def _wait(self, eng, tok):
        sem, val, owner = tok[0], tok[1], tok[2]
        if len(tok) > 3 and tok[3] < self.epoch:
            return
        if owner == eng and eng == "pe":
            return
        key = id(sem)
        w = self.waited[eng]
        if w.get(key, 0) >= val:
            return
        self.eng[eng].wait_ge(sem, val)
        w[key] = val
        self.nwait += 1

    def _deps(self, eng, reads, writes):
        for v in reads:
            for r in v.tile.conflicts(v.key):
                if r.w is not None:
                    self._wait(eng, r.w)
        for v in writes:
            for r in v.tile.conflicts(v.key):
                if r.w is not None:
                    self._wait(eng, r.w)
                for (sem, owner, ep), val in list(r.r.items()):
                    self._wait(eng, (sem, val, owner, ep))

    def _commit(self, tok, reads, writes):
        sem, val, owner = tok[0], tok[1], tok[2]
        for v in writes:
            if v.key is None:
                t = v.tile
                t.res = {None: t.res[None]}
            r = v.tile.get(v.key)
            r.w = tok
            r.r = {}
        for v in reads:
            r = v.tile.get(v.key)
            k = (sem, owner, tok[3])
            if r.r.get(k, 0) < val:
                r.r[k] = val

    def op(self, eng, fn, reads, writes):
        self._deps(eng, reads, writes)
        ins = fn()
        self.tick[eng] += 1
        ins.then_inc(self.sem[eng], 1)
        tok = (self.sem[eng], self.tick[eng], eng, self.epoch)
        self._commit(tok, reads, writes)
        self.ninst += 1
        return tok

    def _dslot(self, q):
        if q == "pool":
            i = self.nsw
            self.nsw += 1
            return self.ssem[i % NSW], i // NSW, "sdma%d" % (i % NSW)
        i = self.ndma
        self.ndma += 1
        return self.dsem[i % NDMA], i // NDMA, "dma%d" % (i % NDMA)

    def dma(self, q, out, in_, **kw):
        sem, gen, owner = self._dslot(q)
        if gen > 0:
            self._wait(q, (sem, 16 * gen, "dma"))
        self._deps(q, [in_], [out])
        ins = self.eng[q].dma_start(out=out.ap, in_=in_.ap, **kw)
        ins.then_inc(sem, 16)
        tok = (sem, 16 * (gen + 1), owner, self.epoch)
        self._commit(tok, [in_], [out])
        self.ninst += 1
        return tok

    def barrier(self):
        toks = [(self.sem[e], self.tick[e], e) for e in self.sem if self.tick[e] > 0]
        for (n, N, sems) in ((self.ndma, NDMA, self.dsem), (self.nsw, NSW, self.ssem)):
            for slot in range(min(n, N)):
                cnt = (n - 1 - slot) // N + 1
                toks.append((sems[slot], 16 * cnt, "dma"))
        for e in self.eng:
            for tok in toks:
                self._wait(e, tok)
        self.epoch += 1
        ep = self.epoch
        for e in self.eng:
            self.eng[e].sem_inc(self.barA, 1)
        for e in self.sem:
            self.eng[e].wait_ge(self.barA, 5 * ep)
            self.eng[e].sem_clear(self.sem[e])
            self.eng[e].sem_inc(self.barB, 1)
        for e in self.eng:
            self.eng[e].wait_ge(self.barB, 4 * ep)
        for e in self.sem:
            self.tick[e] = 0
        for e in self.eng:
            for s_ in self.sem.values():
                self.waited[e].pop(id(s_), None)

    def mm(self, out, lhsT, rhs, start=True, stop=True, **kw):
        return self.op("pe", lambda: self.nc.tensor.matmul(out.ap, lhsT.ap, rhs.ap, start=start, stop=stop, **kw),
                       [lhsT, rhs] + ([] if start else [out]), [out])

    def tr(self, out, in_, ident):
        return self.op("pe", lambda: self.nc.tensor.transpose(out.ap, in_.ap, ident.ap), [in_, ident], [out])

    def act(self, out, in_, func, bias=0.0, scale=1.0, accum=None):
        reads = [in_]
        kw = {}
        if isinstance(bias, View):
            reads.append(bias)
            kw["bias"] = bias.ap
        else:
            kw["bias"] = float(bias)
        if isinstance(scale, View):
            reads.append(scale)
            kw["scale"] = scale.ap
        else:
            kw["scale"] = float(scale)
        writes = [out]
        if accum is not None:
            writes.append(accum)
            kw["accum_out"] = accum.ap
        return self.op("act", lambda: self.nc.scalar.activation(out.ap, in_.ap, func, **kw), reads, writes)

    def tt(self, out, a, b, op, eng="dve", after=()):
        e = self.eng[eng]
        return self.op(eng, lambda: e.tensor_tensor(out.ap, a.ap, b.ap, op), [a, b] + list(after), [out])

    def ts(self, out, a, s1, op0, s2=None, op1=None, eng="dve", accum=None):
        e = self.eng[eng]
        reads = [a]
        x1 = s1.ap if isinstance(s1, View) else float(s1)
        if isinstance(s1, View):
            reads.append(s1)
        x2 = None
        if s2 is not None:
            x2 = s2.ap if isinstance(s2, View) else float(s2)
            if isinstance(s2, View):
                reads.append(s2)
        writes = [out]
        kw = {}
        if accum is not None:
            writes.append(accum)
            kw["accum_out"] = accum.ap
        if op1 is None:
            return self.op(eng, lambda: e.tensor_scalar(out.ap, a.ap, x1, None, op0, **kw), reads, writes)
        return self.op(eng, lambda: e.tensor_scalar(out.ap, a.ap, x1, x2, op0, op1, **kw), reads, writes)

    def stt(self, out, a, s, b, op0, op1, accum=None):
        reads = [a, b]
        x = s.ap if isinstance(s, View) else float(s)
        if isinstance(s, View):
            reads.append(s)
        writes = [out]
        kw = {}
        if accum is not None:
            writes.append(accum)
            kw["accum_out"] = accum.ap
        return self.op("dve", lambda: self.nc.vector.scalar_tensor_tensor(out.ap, a.ap, x, b.ap, op0, op1, **kw),
                       reads, writes)

    def copy(self, out, in_, eng="dve"):
        if eng == "act":
            return self.op("act", lambda: self.nc.scalar.copy(out.ap, in_.ap), [in_], [out])
        e = self.eng[eng]
        return self.op(eng, lambda: e.tensor_copy(out.ap, in_.ap), [in_], [out])

    def memset(self, out, val, eng="pool"):
        e = self.eng[eng]
        return self.op(eng, lambda: e.memset(out.ap, val), [], [out])

    def recip(self, out, in_):
        return self.op("dve", lambda: self.nc.vector.reciprocal(out.ap, in_.ap), [in_], [out])

    def reduce(self, out, in_, op=ALU.add, axis=AX.X):
        return self.op("dve", lambda: self.nc.vector.tensor_reduce(out.ap, in_.ap, axis, op), [in_], [out])


class _Phase:
    def __init__(self, b):
        self.b = b

    def __enter__(self):
        self.prev = self.b.stack
        self.st = ExitStack()
        self.st.__enter__()
        self.b.stack = self.st
        return self

    def __exit__(self, *a):
        self.b.barrier()
        self.b.stack = self.prev
        return self.st.__exit__(*a)
from concourse.bass_utils import run_bass_kernel_spmd
import math
T = 4096
D = 1024
FH = 2816
NHC = FH // 128
EPS = 1e-6


class K:
    def __init__(self, nc, ext_in=(), ext_out=()):
        self.nc = nc
        self.b = Builder(nc)
        self.ext_in = set(ext_in)
        self.ext_out = set(ext_out)
        self.d = {}

    def dram(self, name, shape, dtype, kind=None):
        if kind is None:
            kind = "ExternalInput" if name in self.ext_in else ("ExternalOutput" if name in self.ext_out else "Internal")
        t = self.b.dram(name, shape, dtype, kind=kind)
        self.d[name] = t
        return t


class _Bank:
    def __init__(self, tile, i):
        self.tile = tile
        self.i = i

    def __getitem__(self, idx):
        rows, cols = idx
        return View(self.tile, self.tile.t[rows, self.i, cols], None)


class _Cols:
    def __init__(self, tile, off):
        self.tile = tile
        self.off = off

    def __getitem__(self, idx):
        rows, cols = idx
        return View(self.tile, self.tile.t[rows, cols.start + self.off:cols.stop + self.off], self.off)


def V(tile, ap, key=None):
    return View(tile, ap, key)


def dma_nc(b, q, out, in_):
    return b.dma(q, out, in_, allow_slow_non_contiguous=True)


def prep_weights_ffn(k, L):
    b = k.b
    wg, wu, wd = k.d["ffn_gate%d" % L], k.d["ffn_up%d" % L], k.d["ffn_down%d" % L]
    wgu_b = k.dram("wgu_b%d" % L, [NHC, 128, 2, 8, 128], BF16)
    wd_b = k.dram("wd_b%d" % L, [NHC, 128, D], BF16)
    for hc in range(NHC):
        for j, w in enumerate((wg, wu)):
            src = V(w, w.t[:, hc * 128:(hc + 1) * 128].rearrange("(kc p) c -> p kc c", p=128))
            b.dma("pool", V(wgu_b, wgu_b.t[hc, :, j, :, :], hc), src)
    for hc in range(0, NHC, 2):
        src = V(wd, wd.t[hc * 128:(hc + 2) * 128, :].rearrange("(h p) c -> h p c", p=128))
        b.dma("pool", V(wd_b, wd_b.t[hc:hc + 2], hc), src)


def prep_weight_rows(k, name, nchunk, ncol):
    b = k.b
    w = k.d[name]
    wb = k.dram(name + "_b", [nchunk, 128, ncol], BF16)
    step = 2
    for c in range(0, nchunk, step):
        n = min(step, nchunk - c)
        src = V(w, w.t[c * 128:(c + n) * 128, :].rearrange("(h p) c -> h p c", p=128))
        b.dma("pool", V(wb, wb.t[c:c + n], c), src)
    return wb


def phase_post(k, L, C, mixT, wout_b, x_in, x_out, cst):
    b = k.b
    nc = k.nc
    wgu_b, wd_b = k.d["wgu_b%d" % L], k.d["wd_b%d" % L]
    x1d = k.dram("x1d%d" % L, [T, D], F32)
    with b.phase():
        ident = b.sb("ident", [128, 128], BF16)
        b.dma("pool", ident[:], cst["ident"])
        g_post = b.sb("g_post", [128, D])
        g_fpost = b.sb("g_fpost", [128, D])
        g_fpre = b.sb("g_fpre", [128, 8])
        b.dma("sp", g_post[:], V(k.d["mix_post%d" % L], k.d["mix_post%d" % L].t.partition_broadcast(128)))
        b.dma("sp", g_fpost[:], V(k.d["ffn_post%d" % L], k.d["ffn_post%d" % L].t.partition_broadcast(128)))
        dma_nc(b, "sp", g_fpre[:], V(k.d["ffn_pre%d" % L], k.d["ffn_pre%d" % L].t.rearrange("(kc p) -> p kc", p=128)))
        wout = b.sb("wout_sb%d" % L, [128, C, D], BF16)
        for c in range(C):
            b.dma("sp", wout[:, c, :], wout_b[c])
        wd = b.sb("wd", [128, NHC, D], BF16)
        for hc in range(NHC):
            b.dma("sp", wd.k(hc)[:, hc, :], wd_b.k(hc - hc % 2)[hc])
        hT = b.sb("hT", [128, NHC, 1024], BF16)
        xn2T = b.sb("xn2T", [128, 8, 1024], BF16)
        mixh = b.sb("mixh", [128, C, 512], BF16)
        wgu = [b.sb("wgu%d" % i, [128, 2, 8, 128], BF16) for i in range(2)]
        xin = [b.sb("xin%d" % i, [128, D]) for i in range(2)]
        x1r = [b.sb("x1r%d" % i, [128, D]) for i in range(2)]
        tmp = b.sb("tmp", [128, D])
        junk = b.sb("junk", [128, D], BF16)
        xs = b.sb("xs", [128, D], BF16)
        sg = [b.sb("sg%d" % i, [128, 512], BF16) for i in range(2)]
        st_t = b.sb("st", [128, 64])
        A = [b.ps("A%d" % i, [128, 1024]) for i in range(2)]
        G = [b.ps("G%d" % i, [128, 1024]) for i in range(2)]
        na = 0
        ng = 0
        nw = 0
        for stile in range(T // 1024):
            t0 = stile * 1024
            na0 = na

            def emit_outproj(s):
                if s % 4 == 0:
                    for c in range(C):
                        b.dma("sp", mixh[:, c, :], V(mixT, mixT.t[c * 128:(c + 1) * 128, t0 + (s // 4) * 512: t0 + (s // 4) * 512 + 512]))
                r0 = t0 + s * 128
                xi = xin[s % 2]
                b.dma("sp", xi[:], V(x_in, x_in.t[r0:r0 + 128, :]))
                acc = A[(na0 + s) % 2]
                for half in range(2):
                    for c in range(C):
                        b.mm(acc[:, half * 512:(half + 1) * 512], mixh[:, c, (s % 4) * 128:(s % 4) * 128 + 128],
                             wout[:, c, half * 512:(half + 1) * 512], start=(c == 0), stop=(c == C - 1))
            emit_outproj(0)
            for s in range(8):
                r0 = t0 + s * 128
                st = _Cols(st_t, (s % 2) * 16)
                xi = xin[s % 2]
                acc = A[(na0 + s) % 2]
                if s + 1 < 8 and (s + 1) % 4 != 0:
                    emit_outproj(s + 1)
                b.act(junk[:], acc[:], AF.Square, accum=st[:, 0:1])
                b.act(st[:, 1:2], st[:, 0:1], AF.Sqrt, bias=cst["eps"], scale=1.0 / D)
                b.recip(st[:, 2:3], st[:, 1:2])
                b.stt(tmp[:], acc[:], st[:, 2:3], g_post[:], ALU.mult, ALU.mult)
                b.tt(xi[:], tmp[:], xi[:], ALU.add)
                b.dma("act", V(x1d, x1d.t[r0:r0 + 128, :], r0), xi[:])
                b.act(junk[:], xi[:], AF.Square, accum=st[:, 3:4])
                b.act(st[:, 4:5], st[:, 3:4], AF.Sqrt, bias=cst["eps"], scale=1.0 / D)
                b.recip(st[:, 5:6], st[:, 4:5])
                b.ts(xs[:], xi[:], st[:, 5:6], ALU.mult)
                gt = G[ng % 2]
                ng += 1
                gtb = gt.t[:, 0:512].bitcast(BF16)
                for kc in range(8):
                    b.tr(V(gt, gtb[:, kc * 128:(kc + 1) * 128]), xs[:, kc * 128:(kc + 1) * 128], ident[:])
                b.tt(xn2T[:, :, s * 128:(s + 1) * 128], V(gt, gtb.rearrange("p (kc t) -> p kc t", kc=8)),
                     V(g_fpre, g_fpre.t[:, :].unsqueeze(2).to_broadcast([128, 8, 128])), ALU.mult)
                if s + 1 < 8 and (s + 1) % 4 == 0:
                    emit_outproj(s + 1)
            na += 8
            for hc in range(NHC):
                w = wgu[nw % 2]
                nw += 1
                b.dma("sp", w[:], wgu_b.k(hc)[hc])
                for th in range(2):
                    gt = G[ng % 2]
                    ng += 1
                    for j in range(2):
                        for kc in range(8):
                            b.mm(gt[:, j * 512:(j + 1) * 512], w[:, j, kc, :], xn2T[:, kc, th * 512:(th + 1) * 512],
                                 start=(kc == 0), stop=(kc == 7))
                    sgt = sg[(ng) % 2]
                    b.act(sgt[:], gt[:, 0:512], AF.Silu)
                    b.tt(hT[:, hc, th * 512:(th + 1) * 512], gt[:, 512:1024], sgt[:], ALU.mult)
            for s in range(8):
                r0 = t0 + s * 128
                st = _Cols(st_t, 32 + (s % 2) * 16)
                xr = x1r[s % 2]
                b.dma("sp", xr[:], V(x1d, x1d.t[r0:r0 + 128, :], r0))
                acc = A[na % 2]
                na += 1
                for half in range(2):
                    for hc in range(NHC):
                        b.mm(acc[:, half * 512:(half + 1) * 512], hT[:, hc, s * 128:(s + 1) * 128],
                             wd.k(hc)[:, hc, half * 512:(half + 1) * 512], start=(hc == 0), stop=(hc == NHC - 1))
                b.act(junk[:], acc[:], AF.Square, accum=st[:, 6:7])
                b.act(st[:, 7:8], st[:, 6:7], AF.Sqrt, bias=cst["eps"], scale=1.0 / D)
                b.recip(st[:, 8:9], st[:, 7:8])
                b.stt(tmp[:], acc[:], st[:, 8:9], g_fpost[:], ALU.mult, ALU.mult)
                b.tt(xr[:], tmp[:], xr[:], ALU.add)
                b.dma("act", V(x_out, x_out.t[r0:r0 + 128, :], r0), xr[:])


NEG = -30000.0


def norm_to_T(k, x_in, gain_name, xnT, colf, ident, tagp=""):
    for _ in norm_to_T_gen(k, x_in, gain_name, xnT, colf, ident):
        pass


def norm_to_T_gen(k, x_in, gain_name, xnT, colf, ident, NPS=4):
    b = k.b
    g = b.sb("gpre", [128, 8])
    dma_nc(b, "sp", g[:], V(k.d[gain_name], k.d[gain_name].t.rearrange("(kc p) -> p kc", p=128)))
    NB = 4
    xin = [b.sb("nx%d" % i, [128, D]) for i in range(NB)]
    xs = [b.sb("nxs%d" % i, [128, D], BF16) for i in range(NB)]
    junks = [b.sb("njunk%d" % i, [128, D], BF16) for i in range(2)]
    st_t = b.sb("nst", [128, 16 * NB])
    P = [b.ps("nP%d" % i, [128, 512]) for i in range(NPS)]
    for s in range(T // 128):
        r0 = s * 128
        st = _Cols(st_t, (s % NB) * 16)
        xi = xin[s % NB]
        junk = junks[s % 2]
        b.dma("sp", xi[:], V(x_in, x_in.t[r0:r0 + 128, :]))
        b.act(junk[:], xi[:], AF.Square, accum=st[:, 0:1])
        b.act(st[:, 1:2], st[:, 0:1], AF.Sqrt, bias=EPS, scale=1.0 / D)
        b.recip(st[:, 2:3], st[:, 1:2])
        b.ts(xs[s % NB][:], xi[:], st[:, 2:3], ALU.mult)
        pt = P[s % NPS]
        ptb = pt.t[:, 0:512].bitcast(BF16)
        for kc in range(8):
            b.tr(V(pt, ptb[:, kc * 128:(kc + 1) * 128]), xs[s % NB][:, kc * 128:(kc + 1) * 128], ident[:])
        c0 = colf(r0)
        b.tt(xnT.k(s // 4)[:, :, c0:c0 + 128], V(pt, ptb.rearrange("p (kc t) -> p kc t", kc=8)),
             V(g, g.t[:, :].unsqueeze(2).to_broadcast([128, 8, 128])), ALU.mult)
        yield


def phase_l1(k, x_in, mixT1, cst):
    b = k.b
    nc = k.nc
    win_b = k.d["w_in1_b"]
    PADR = 1024
    Vd = k.dram("Vd", [PADR + T + PADR, 12 * 65], BF16)
    Nd = [k.dram("Nd%d" % g, [T, 260], F32) for g in range(3)]
    DIL = (1, 4, 16)
    with b.phase():
        ident = b.sb("ident", [128, 128], BF16)
        b.dma("pool", ident[:], cst["ident"])
        perm = b.sb("perm", [128, 128], BF16)
        b.dma("pool", perm[:], cst["perm"])
        xnT = b.sb("xnT1", [128, 8, T], BF16)
        with b.phase():
            norm_to_T(k, x_in, "mix_pre1", xnT, lambda t: t, ident)
        cosT = b.sb("cosT", [128, T])
        sinT = b.sb("sinT", [128, T])
        b.dma("sp", cosT[:], cst["cos"])
        b.dma("sp", sinT[:], cst["sin"])
        with b.phase():
            wv = b.sb("wv", [128, 8, 768], BF16)
            for kc in range(8):
                b.dma("sp", wv[:, kc, :], V(win_b, win_b.t[kc, :, 1536:2304]))
            z = b.sb("zpad", [128, 12 * 65], BF16)
            b.memset(z[:], 0.0)
            for i in range(PADR // 128):
                b.dma("sp", V(Vd, Vd.t[i * 128:(i + 1) * 128, :], "p%d" % i), z[:])
                b.dma("sp", V(Vd, Vd.t[PADR + T + i * 128:PADR + T + (i + 1) * 128, :], "q%d" % i), z[:])
            va = [b.sb("va%d" % i, [128, 12, 65], BF16) for i in range(2)]
            for i in range(2):
                b.memset(va[i][:], 1.0)
            Pv = [b.ps("Pv%d" % i, [128, 1024]) for i in range(2)]
            for s in range(T // 128):
                p = Pv[s % 2]
                for (c0, c1) in ((0, 512), (512, 768)):
                    for kc in range(8):
                        b.mm(p[:, c0:c1], xnT[:, kc, s * 128:(s + 1) * 128], wv[:, kc, c0:c1], start=(kc == 0), stop=(kc == 7))
                b.copy(va[s % 2][:, :, 0:64], V(p, p.t[:, 0:768].rearrange("p (h e) -> p h e", e=64)), eng="act")
                b.dma("sp", V(Vd, Vd.t[PADR + s * 128:PADR + (s + 1) * 128, :], s),
                      V(va[s % 2], va[s % 2].t[:].rearrange("p h e -> p (h e)")))
        for g in range(3):
            dil = DIL[g]
            L = T // dil
            NQ = L // 128
            with b.phase():
                wqk = b.sb("wqk", [128, 8, 2, 256], BF16)
                for kc in range(8):
                    b.dma("sp", wqk[:, kc, 0, :], V(win_b, win_b.t[kc, :, g * 256:(g + 1) * 256]))
                    b.dma("sp", wqk[:, kc, 1, :], V(win_b, win_b.t[kc, :, 768 + g * 256:768 + (g + 1) * 256]))
                QT = b.sb("QT", [128, 2, dil, L], BF16)
                KT = b.sb("KT", [128, 2, dil, L + 128], BF16)
                b.memset(KT[:], 0.0)
                t1 = [b.sb("t1_%d" % i, [128, 512]) for i in range(2)]
                t2 = [b.sb("t2_%d" % i, [128, 512]) for i in range(2)]
                qbf = [b.sb("qbf_%d" % i, [128, 512], BF16) for i in range(2)]
                with b.phase():
                    PA = [b.ps("PA%d" % i, [128, 512]) for i in range(2)]
                    PB = [b.ps("PB%d" % i, [128, 512]) for i in range(2)]
                    n = 0
                    for j in range(T // 512):
                        for a in range(2):
                            for mm in range(2):
                                pa, pb = PA[n % 2], PB[n % 2]
                                for kc in range(8):
                                    b.mm(pa[:], wqk[:, kc, a, mm * 128:(mm + 1) * 128], xnT[:, kc, j * 512:(j + 1) * 512],
                                         start=(kc == 0), stop=(kc == 7))
                                b.copy(qbf[n % 2][:], pa[:], eng="act")
                                b.mm(pb[:], perm[:], qbf[n % 2][:])
                                b.tt(t1[n % 2][:], pa[:], cosT[:, j * 512:(j + 1) * 512], ALU.mult, after=[qbf[n % 2][:]])
                                b.tt(t2[n % 2][:], pb[:], sinT[:, j * 512:(j + 1) * 512], ALU.mult)
                                w = 512 // dil
                                if a == 0:
                                    dst = V(QT, QT.t[:, mm, :, j * w:(j + 1) * w])
                                else:
                                    dst = V(KT, KT.t[:, mm, :, 64 + j * w:64 + (j + 1) * w])
                                b.tt(dst, V(t1[n % 2], t1[n % 2].t[:].rearrange("p (jl r) -> p r jl", r=dil)),
                                     V(t2[n % 2], t2[n % 2].t[:].rearrange("p (jl r) -> p r jl", r=dil)), ALU.add, eng="pool")
                                n += 1
                with b.phase():
                    mk = b.sb("mk", [128, 2, 256], BF16)
                    mkx = b.sb("mkx", [128, 2, 256], BF16)
                    for i in range(2):
                        b.dma("pool", mk[:, i, :], cst["maskAB"])
                        b.dma("pool", mkx[:, i, :], cst["maskX"])
                    vt = [b.sb("vt%d" % i, [128, 4, 65], BF16) for i in range(3)]
                    PT = [b.sb("PT%d" % i, [128, 4, 256], BF16) for i in range(2)]
                    osb = [b.sb("osb%d" % i, [128, 260]) for i in range(2)]
                    S = [b.ps("S%d" % i, [128, 1024]) for i in range(2)]
                    ACC = [b.ps("ACC%d" % i, [128, 512]) for i in range(2)]
                    it = 0

                    def geom(kt):
                        q0 = max(kt - 1, 0) * 128
                        q1 = min(kt + 1, NQ) * 128
                        return q0, q1, q1 - q0, (0 if kt > 0 else 128)

                    def emit_scores(rho, kt, it_):
                        sp = S[it_ % 2]
                        q0, q1, nq, m0 = geom(kt)
                        msk = mkx if kt == NQ // 2 else mk
                        for hh in range(4):
                            mm, pb = hh // 2, (hh % 2) * 64
                            b.mm(sp[:, hh * 256:hh * 256 + nq], ident[:], msk[:, 0, m0:m0 + nq], start=True, stop=False,
                                 skip_group_check=True)
                            b.mm(sp[:, hh * 256:hh * 256 + nq], KT[pb:pb + 64, mm, rho, kt * 128:(kt + 1) * 128],
                                 QT[pb:pb + 64, mm, rho, q0:q1], start=False, stop=True, skip_group_check=True)
                    iters = [(rho, kt) for rho in range(dil) for kt in range(NQ + 1)]
                    emit_scores(*iters[0], 0)
                    for rho in range(dil):
                        for kt in range(NQ + 1):
                            v = vt[it % 3]
                            row0 = PADR + dil * (128 * kt - 64) + rho
                            b.dma("sp", v[:], V(Vd, Vd.t[row0:row0 + 127 * dil + 1:dil, g * 260:(g + 1) * 260].rearrange("p (h e) -> p h e", e=65)))
                            sp = S[it % 2]
                            pt = PT[it % 2]
                            q0, q1, nq, m0 = geom(kt)
                            if it + 1 < len(iters):
                                emit_scores(*iters[it + 1], it + 1)
                            b.act(pt[:, :, 0:nq], V(sp, sp.t[:].rearrange("p (h c) -> p h c", c=256)[:, :, 0:nq]), AF.Exp, scale=0.125)
                            if kt > 0:
                                acc = ACC[(kt - 1) % 2]
                                for hh in range(4):
                                    b.mm(acc[:, hh * 65:(hh + 1) * 65], pt[:, hh, 0:128], v[:, hh, :], start=False, stop=True,
                                         skip_group_check=True)
                                o = osb[(kt - 1) % 2]
                                b.copy(o[:], acc[:, 0:260])
                                tok0 = dil * 128 * (kt - 1) + rho
                                b.dma("sp", V(Nd[g], Nd[g].t[tok0:tok0 + 127 * dil + 1:dil, :], (rho, kt - 1)), o[:])
                            if kt < NQ:
                                acc = ACC[kt % 2]
                                c0 = nq - 128
                                for hh in range(4):
                                    b.mm(acc[:, hh * 65:(hh + 1) * 65], pt[:, hh, c0:c0 + 128], v[:, hh, :], start=(hh == 0), stop=False,
                                         skip_group_check=True)
                            it += 1
        with b.phase():
            nt = [b.sb("nt%d" % i, [128, 3, 4, 65]) for i in range(4)]
            zt = b.sb("zt", [128, 32])
            yb = [b.sb("yb%d" % i, [128, 768], BF16) for i in range(4)]
            yT = [b.sb("yT%d" % i, [128, 6, 512], BF16) for i in range(2)]
            PTt = [b.ps("PTt%d" % i, [128, 512]) for i in range(4)]
            for s in range(T // 128):
                n_ = nt[s % 4]
                for g in range(3):
                    b.dma("sp", n_[:, g, :, :], V(Nd[g], Nd[g].t[s * 128:(s + 1) * 128, :].rearrange("p (h e) -> p h e", e=65)))
                zc = _Cols(zt, (s % 4) * 8)
                b.tt(V(zt, zt.t[:, (s % 4) * 8:(s % 4) * 8 + 4], (s % 4) * 8), n_[:, 0, :, 64], n_[:, 1, :, 64], ALU.add)
                b.tt(V(zt, zt.t[:, (s % 4) * 8:(s % 4) * 8 + 4], (s % 4) * 8), zc[:, 0:4], n_[:, 2, :, 64], ALU.add)
                b.recip(zc[:, 4:8], zc[:, 0:4])
                rzb = zt.t[:, (s % 4) * 8 + 4:(s % 4) * 8 + 8].unsqueeze(1).unsqueeze(3).to_broadcast([128, 3, 4, 64])
                b.tt(V(yb[s % 4], yb[s % 4].t[:].rearrange("p (g h e) -> p g h e", g=3, h=4)), n_[:, :, :, 0:64],
                     V(zt, rzb, (s % 4) * 8), ALU.mult)
                pt = PTt[s % 4]
                ptb = pt.t[:, 0:512].bitcast(BF16)
                for c in range(6):
                    b.tr(V(pt, ptb[:, c * 128:(c + 1) * 128]), yb[s % 4][:, c * 128:(c + 1) * 128], ident[:])
                y_ = yT[(s // 4) % 2]
                b.copy(y_[:, :, (s % 4) * 128:(s % 4) * 128 + 128], V(pt, ptb[:, 0:768].rearrange("p (c t) -> p c t", c=6)), eng="act")
                if s % 4 == 3:
                    for c in range(6):
                        b.dma("sp", V(mixT1, mixT1.t[c * 128:(c + 1) * 128, (s - 3) * 128:(s + 1) * 128], (c, s)), y_[:, c, :])


def colf0(t):
    return t + 1 + (2 if t >= 2048 else 0)


XW = T + 4


def l0_norm(k, x_in, xnT, ident, cst, do_norm=True):
    b = k.b
    if do_norm:
        with b.phase():
            norm_to_T(k, x_in, "mix_pre0", xnT, colf0, ident)
    flag = b.sb("flag", [128, 4])
    b.dma("sp", flag[:], cst["flag"])
    b.memset(xnT[:, :, 0:1], 0.0)
    b.memset(xnT[:, :, XW - 1:XW], 0.0)
    b.ts(xnT[:, :, 2049:2050], xnT[:, :, 2051:2052], flag[:, 0:1], ALU.mult)
    b.ts(xnT[:, :, 2050:2051], xnT[:, :, 2048:2049], flag[:, 0:1], ALU.mult)
    return flag


def l0_qkv(k, xnT, cst, QTd, KTd, Vad, x_in=None, ident=None):
    b = k.b
    win_b = k.d["w_in0_b"]
    with b.phase():
        ngen = norm_to_T_gen(k, x_in, "mix_pre0", xnT, colf0, ident, NPS=2) if x_in is not None else None

        def norm_steps(n):
            if ngen is None:
                return
            for _ in range(n):
                try:
                    next(ngen)
                except StopIteration:
                    return
        cosT = b.sb("cosT", [128, T])
        sinT = b.sb("sinT", [128, T])
        b.dma("sp", cosT[:], cst["cos"])
        b.dma("sp", sinT[:], cst["sin"])
        wqk = b.sb("wqk0", [128, 8, 1024], BF16)
        wv = b.sb("wv0", [128, 8, 512], BF16)
        for kc in range(8):
            b.dma("sp", wqk[:, kc, :], V(win_b, win_b.t[kc, :, 0:1024]))
            b.dma("sp", wv[:, kc, :], V(win_b, win_b.t[kc, :, 1024:1536]))
        perm = b.sb("perm0", [128, 128], BF16)
        b.dma("pool", perm[:], cst["perm"])
        qbf = [b.sb("qbf0_%d" % i, [128, 512], BF16) for i in range(2)]
        t1 = [b.sb("t1_%d" % i, [128, 512]) for i in range(2)]
        t2 = [b.sb("t2_%d" % i, [128, 512]) for i in range(2)]
        qst = [b.sb("qst%d" % i, [128, 512], BF16) for i in range(3)]
        va = [b.sb("va0_%d" % i, [128, 4, 129], BF16) for i in range(2)]
        for i in range(2):
            b.memset(va[i][:], 1.0)
        PA = [b.ps("PA%d" % i, [128, 512]) for i in range(2)]
        PB = [b.ps("PB%d" % i, [128, 512]) for i in range(2)]
        PV = [b.ps("PV%d" % i, [128, 512]) for i in range(2)]
        nctr = [0]

        def qkv_tile(j):
            c0 = colf0(j * 512)
            for a in range(2):
                for m in range(4):
                    n = nctr[0]
                    nctr[0] += 1
                    pa, pb = PA[n % 2], PB[n % 2]
                    col = a * 512 + m * 128
                    for kc in range(8):
                        b.mm(pa[:], wqk[:, kc, col:col + 128], xnT.k(j)[:, kc, c0:c0 + 512], start=(kc == 0), stop=(kc == 7))
                    b.copy(qbf[n % 2][:], pa[:], eng="act")
                    b.mm(pb[:], perm[:], qbf[n % 2][:])
                    b.tt(t1[n % 2][:], pa[:], cosT[:, j * 512:(j + 1) * 512], ALU.mult, after=[qbf[n % 2][:]])
                    b.tt(t2[n % 2][:], pb[:], sinT[:, j * 512:(j + 1) * 512], ALU.mult)
                    q = qst[n % 3]
                    b.tt(q[:], t1[n % 2][:], t2[n % 2][:], ALU.add, eng="pool")
                    dst = QTd if a == 0 else KTd
                    b.dma("sp", V(dst, dst.t[m * 128:(m + 1) * 128, j * 512:(j + 1) * 512], (m, j)), q[:])
                    yield
            for s4 in range(4):
                s = j * 4 + s4
                p = PV[s % 2]
                for kc in range(8):
                    b.mm(p[:], xnT.k(j)[:, kc, c0 + s4 * 128:c0 + (s4 + 1) * 128], wv[:, kc, :], start=(kc == 0), stop=(kc == 7))
                b.copy(va[s % 2][:, :, 0:128], V(p, p.t[:].rearrange("p (h e) -> p h e", e=128)), eng="act")
                b.dma("sp", V(Vad, Vad.t[s * 128:(s + 1) * 128, :], s), V(va[s % 2], va[s % 2].t[:].rearrange("p h e -> p (h e)")))
                yield
        norm_steps(4)
        for j in range(T // 512):
            for i_, _ in enumerate(qkv_tile(j)):
                if i_ % 3 == 1:
                    norm_steps(1)
        norm_steps(T // 128)


def l0_diffattn(k, cst, QTd, KTd, Vad, mixT0, ident, flag, co_setup=None):
    b = k.b
    with b.phase():
        QT = b.sb("QT0", [128, 4, T], BF16)
        KT = b.sb("KT0", [128, 4, T], BF16)
        VA = b.sb("VA0", [128, 32, 4 * 129], BF16)
        for m in range(4):
            for hh in range(2):
                b.dma("sp", QT.k(m)[:, m, hh * 2048:(hh + 1) * 2048], V(QTd, QTd.t[m * 128:(m + 1) * 128, hh * 2048:(hh + 1) * 2048]))
                b.dma("sp", KT.k(m)[:, m, hh * 2048:(hh + 1) * 2048], V(KTd, KTd.t[m * 128:(m + 1) * 128, hh * 2048:(hh + 1) * 2048]))
        for s in range(32):
            b.dma("sp", VA.k(s)[:, s, :], V(Vad, Vad.t[s * 128:(s + 1) * 128, :]))
        lv = b.sb("lv", [128, 4, 64])
        for i, nm in enumerate(("lam_q1", "lam_k1", "lam_q2", "lam_k2")):
            b.dma("sp", lv[:, i, :], V(k.d[nm], k.d[nm].t.partition_broadcast(128)))
        ls = b.sb("ls", [128, 8])
        lj = b.sb("lj", [128, 64])
        b.tt(lj[:], lv[:, 0, :], lv[:, 1, :], ALU.mult)
        b.reduce(ls[:, 0:1], lj[:])
        b.tt(lj[:], lv[:, 2, :], lv[:, 3, :], ALU.mult)
        b.reduce(ls[:, 1:2], lj[:])
        b.act(ls[:, 2:4], ls[:, 0:2], AF.Exp)
        b.tt(ls[:, 4:5], ls[:, 2:3], ls[:, 3:4], ALU.subtract)
        b.ts(ls[:, 5:6], ls[:, 4:5], -1.0, ALU.mult, -0.2, ALU.add)
        sw = b.sb("sw", [128, 128])
        b.dma("sp", sw[:], V(k.d["subln_w"], k.d["subln_w"].t.partition_broadcast(128)))
        b.ts(sw[:], sw[:], 0.8, ALU.mult)
        PT = [b.sb("PT0_%d" % i, [128, 1024], BF16) for i in range(3)]
        o1 = [b.sb("o1_%d" % i, [128, 128]) for i in range(2)]
        ob = [b.sb("ob_%d" % i, [128, 128], BF16) for i in range(2)]
        oj = b.sb("oj", [128, 128])
        aT = [b.sb("aT%d" % i, [128, 512], BF16) for i in range(2)]
        accs = [b.sb("accs%d" % i, [128, 3, 512]) for i in range(2)]
        st_t = b.sb("dst", [128, 64])
        S = [b.ps("S0_%d" % i, [128, 1024]) for i in range(2)]
        ACC = [b.ps("AC0_%d" % i, [128, 512]) for i in range(3)]
        TP = b.ps("TP0", [128, 512])
        it = 0
        nsub = 0
        cogens = co_setup(TP) if co_setup is not None else []

        def advance():
            for g_ in list(cogens):
                try:
                    next(g_)
                except StopIteration:
                    cogens.remove(g_)

        def emit_qk(h, qb, kt, it_):
            sp = S[it_ % 2]
            for c in range(2):
                b.mm(sp[:, c * 512:(c + 1) * 512], KT.k(h)[c * 64:(c + 1) * 64, h, kt * 128:(kt + 1) * 128],
                     QT.k(h)[c * 64:(c + 1) * 64, h, qb * 512:(qb + 1) * 512], start=True, stop=True)
        iters = [(h, qb, kt) for h in range(4) for qb in range(8) for kt in range(32)]
        emit_qk(*iters[0], 0)
        for h in range(4):
            for qb in range(8):
                for kt in range(32):
                    sp = S[it % 2]
                    pt = PT[it % 3]
                    if it + 1 < len(iters):
                        emit_qk(*iters[it + 1], it + 1)
                    cross = (kt < 16) != (qb < 4)
                    if cross:
                        b.act(pt[:], sp[:], AF.Exp, scale=0.125, bias=flag[:, 1:2])
                    else:
                        b.act(pt[:], sp[:], AF.Exp, scale=0.125)
                    for c in range(2):
                        for qs in range(4):
                            gi = c * 4 + qs
                            acc = ACC[gi // 3]
                            co = (gi % 3) * 129
                            b.mm(acc[:, co:co + 129], pt[:, c * 512 + qs * 128:c * 512 + (qs + 1) * 128], VA.k(kt)[:, kt, h * 129:(h + 1) * 129],
                                 start=(kt == 0 and gi % 3 == 0), stop=(kt == 31), skip_group_check=True)
                    it += 1
                    if it % CO_EVERY == 0:
                        advance()
                asb = accs[(h * 8 + qb) % 2]
                for i_ in range(3):
                    w_ = 387 if i_ < 2 else 258
                    b.copy(asb[:, i_, 0:w_], ACC[i_][:, 0:w_], eng="dve")
                tp = TP
                tpb = tp.t[:, 0:256].bitcast(BF16)
                for qs in range(4):
                    st = _Cols(st_t, (nsub % 2) * 16)
                    a0 = _Bank(asb, qs // 3)
                    c0 = (qs % 3) * 129
                    a1 = _Bank(asb, (4 + qs) // 3)
                    c1 = ((4 + qs) % 3) * 129
                    b.recip(st[:, 0:1], a0[:, c0 + 128:c0 + 129])
                    b.recip(st[:, 1:2], a1[:, c1 + 128:c1 + 129])
                    b.tt(st[:, 2:3], st[:, 1:2], ls[:, 5:6], ALU.mult)
                    o = o1[nsub % 2]
                    b.ts(o[:], a0[:, c0:c0 + 128], st[:, 0:1], ALU.mult)
                    b.stt(o[:], a1[:, c1:c1 + 128], st[:, 2:3], o[:], ALU.mult, ALU.add)
                    b.act(oj[:], o[:], AF.Square, accum=st[:, 3:4])
                    b.act(st[:, 4:5], st[:, 3:4], AF.Sqrt, bias=1e-5, scale=1.0 / 128)
                    b.recip(st[:, 5:6], st[:, 4:5])
                    obf = ob[nsub % 2]
                    b.stt(obf[:], o[:], st[:, 5:6], sw[:], ALU.mult, ALU.mult)
                    b.tr(V(tp, tpb[:, qs * 128:(qs + 1) * 128]), obf[:], ident[:])
                    nsub += 1
                at = aT[(h * 8 + qb) % 2]
                b.copy(at[:], V(tp, tpb), eng="act")
                b.dma("sp", V(mixT0, mixT0.t[h * 128:(h + 1) * 128, qb * 512:(qb + 1) * 512], (h, qb)), at[:])
        while cogens:
            advance()


CDEC = 0.6065306597126334
NTL = 2
STAGGER = 0
CO_EVERY = 2
RW_BURST = 1


def l0_rwproj(k, xnT, cst, rwd):
    b = k.b
    d = k.d
    with b.phase():
        W1 = b.sb("W1", [128, 8, 1536], BF16)
        W2 = b.sb("W2", [128, 8, 1536], BF16)
        La = b.sb("La", [128, 8, 416], BF16)
        Lh = b.sb("Lh", [128, 8, 416], BF16)
        L2a = b.sb("L2a", [128, 512], BF16)
        L2b = b.sb("L2b", [128, 512], BF16)
        L2g = b.sb("L2g", [128, 512], BF16)
        L2g2 = b.sb("L2g2", [32, 512], BF16)
        b.dma("pool", L2a[0:64, :], d["w2_f"][:])
        b.dma("pool", L2a[64:128, :], d["w2_b"][:])
        b.dma("pool", L2b[0:64, :], d["a2_f"][:])
        b.dma("pool", L2b[64:128, :], d["a2_b"][:])
        b.dma("pool", L2g[:], V(d["g2"], d["g2"].t[0:128, :]))
        b.dma("pool", L2g2[:], V(d["g2"], d["g2"].t[128:160, :]))
        with b.phase():
            mub = b.sb("mub", [128, 1536])
            for i, nm in enumerate(("mu_r", "mu_k", "mu_v")):
                b.dma("sp", mub[:, i * 512:(i + 1) * 512], V(d[nm], d[nm].t.partition_broadcast(128)))
            omm = b.sb("omm", [128, 1536])
            hm = b.sb("hm", [128, 1536])
            b.ts(omm[:], mub[:], -1.0, ALU.mult, 1.0, ALU.add)
            b.ts(hm[:], mub[:], 0.5, ALU.mult)
            mus = b.sb("mus", [128, 3, 8])
            for i, nm in enumerate(("mu_w", "mu_a", "mu_g")):
                dma_nc(b, "sp", mus[:, i, :], V(d[nm], d[nm].t.rearrange("(kc p) -> p kc", p=128)))
            omm3 = b.sb("omm3", [128, 3, 8])
            hm3 = b.sb("hm3", [128, 3, 8])
            b.ts(omm3[:], mus[:], -1.0, ALU.mult, 1.0, ALU.add)
            b.ts(hm3[:], mus[:], 0.5, ALU.mult)
            wf = [b.sb("wf%d" % i, [128, 1536]) for i in range(2)]
            lf = [b.sb("lf%d" % i, [128, 416]) for i in range(2)]
            win = d["w_in0"]
            for kc in range(8):
                w = wf[kc % 2]
                b.dma("sp", w[:], V(win, win.t[kc * 128:(kc + 1) * 128, 1536:3072]))
                b.tt(W1[:, kc, :], w[:], omm[:], ALU.mult)
                b.tt(W2[:, kc, :], w[:], hm[:], ALU.mult, eng="pool")
                l = lf[kc % 2]
                for (nm, c0, cw) in (("w1_f", 0, 64), ("w1_b", 64, 64), ("a1_f", 128, 64), ("a1_b", 192, 64), ("g1", 256, 160)):
                    b.dma("sp", l[:, c0:c0 + cw], V(d[nm], d[nm].t[kc * 128:(kc + 1) * 128, :]))
                for gi, (c0, c1) in enumerate(((0, 128), (128, 256), (256, 416))):
                    b.ts(La[:, kc, c0:c1], l[:, c0:c1], omm3[:, gi, kc:kc + 1], ALU.mult)
                    b.ts(Lh[:, kc, c0:c1], l[:, c0:c1], hm3[:, gi, kc:kc + 1], ALU.mult)
        xsh = [b.sb("xsh%d" % i, [128, 8, 512], BF16) for i in range(2)]
        h1 = [[b.sb("h1_%d_%d" % (i, g), [128, 512], BF16) for g in range(4)] for i in range(2)]
        rw = [b.sb("rw%d" % i, [128, 8, 512]) for i in range(2)]
        PL = [b.ps("PL%d" % i, [128, 512]) for i in range(2)]
        PT_ = [b.ps("PTk%d" % i, [128, 512]) for i in range(4)]
        npl = 0
        npt = 0
        for j in range(T // 512):
            c0 = colf0(j * 512)
            xs = xsh[j % 2]
            b.tt(xs[:], xnT[:, :, c0 - 1:c0 + 511], xnT[:, :, c0 + 1:c0 + 513], ALU.add, eng="pool")
            hh = h1[j % 2]
            for gi, (r0, nr, fn) in enumerate(((0, 128, AF.Tanh), (128, 128, AF.Copy), (256, 128, AF.Sigmoid), (384, 32, AF.Sigmoid))):
                p = PL[npl % 2]
                npl += 1
                for kc in range(8):
                    b.mm(p[0:nr, :], La[:, kc, r0:r0 + nr], xnT[:, kc, c0:c0 + 512], start=(kc == 0), stop=False)
                for kc in range(8):
                    b.mm(p[0:nr, :], Lh[:, kc, r0:r0 + nr], xs[:, kc, :], start=False, stop=(kc == 7))
                if fn == AF.Copy:
                    b.copy(hh[gi][0:nr, :], p[0:nr, :], eng="act")
                else:
                    b.act(hh[gi][0:nr, :], p[0:nr, :], fn)
            for s4 in range(4):
                s = j * 4 + s4
                r = rw[s % 2]
                cs = c0 + s4 * 128
                ts_ = slice(s4 * 128, (s4 + 1) * 128)
                for q in range(8):
                    p = PT_[npt % 4]
                    npt += 1
                    if q < 3:
                        for kc in range(8):
                            b.mm(p[:], xnT[:, kc, cs:cs + 128], W1[:, kc, q * 512:(q + 1) * 512], start=(kc == 0), stop=False)
                        for kc in range(8):
                            b.mm(p[:], xs[:, kc, ts_], W2[:, kc, q * 512:(q + 1) * 512], start=False, stop=(kc == 7))
                    elif q == 3:
                        b.mm(p[:], hh[0][0:64, ts_], L2a[0:64, :])
                    elif q == 4:
                        b.mm(p[:], hh[0][64:128, ts_], L2a[64:128, :])
                    elif q == 5:
                        b.mm(p[:], hh[1][0:64, ts_], L2b[0:64, :])
                    elif q == 6:
                        b.mm(p[:], hh[1][64:128, ts_], L2b[64:128, :])
                    else:
                        b.mm(p[:], hh[2][:, ts_], L2g[:], start=True, stop=False)
                        b.mm(p[:], hh[3][0:32, ts_], L2g2[0:32, :], start=False, stop=True)
                    b.copy(r[:, q, :], p[:], eng=("act" if q % 2 == 0 else "dve"))
                b.dma("sp", V(rwd, rwd.t[s * 128:(s + 1) * 128, :, :], s), r[:])


def l0_rwkv(k, cst, rwd, mixT0, ident, Yd=None, flag=None, stage=9, do_post=True):
    b = k.b
    d = k.d
    if Yd is None:
        Yd = k.dram("Yd", [2, T, 512], F32)
    NT = T // 128
    with b.phase():
        def bc(nm):
            t = b.sb("bc_" + nm, [128, 512])
            src = d[nm].t
            if len(src.shape) == 2:
                src = src.rearrange("h n -> (h n)")
            b.dma("sp", t[:], V(d[nm], src.partition_broadcast(128)))
            return t
        w0 = [bc("w0_f"), bc("w0_b")]
        a0 = [bc("a0_f"), bc("a0_b")]
        kkb = bc("k_k")
        kab = bc("k_a")
        omka = b.sb("omka", [128, 512])
        b.ts(omka[:], kab[:], -1.0, ALU.mult, 1.0, ALU.add)
        tri = b.sb("tri", [128, 6, 128])
        b.dma("sp", tri[:], cst["tri"])
        irep = b.sb("irep", [128, 512])
        b.dma("sp", irep[:], cst["irep"])
        SU, IU, SL, IL = 0, 1, 2, 3
        M4 = []
        MQ = []
        for dr in range(2):
            s_, i_, sp_ = (SU, IU, SL) if dr == 0 else (SL, IL, SU)
            m4 = b.sb("M4_%d" % dr, [128, 4, 128], BF16)
            mq = b.sb("MQ_%d" % dr, [128, 4, 128], BF16)
            for j in range(4):
                b.copy(m4[:, j, :], tri[:, (s_ if j % 2 == 0 else i_), :], eng="pool")
                b.copy(mq[:, j, :], tri[:, sp_, :], eng="pool")
            M4.append(m4)
            MQ.append(mq)
        CS = [(IU, SU, SL), (IL, SL, SU)]
        GS = [[b.ps("Gp%d_%d" % (d_, i), [128, 512]) for i in range(3)] for d_ in range(2)]
        PYS = [b.ps("PY%d" % i, [128, 512]) for i in range(2)]
        gcnt = [0, 0]

        class DirState:
            pass
        DS = []
        for dr in range(2):
            s = DirState()
            s.rwb = [b.sb("rw%d_%d" % (i, dr), [128, 5, 512]) for i in range(2)]
            s.f = [b.sb("f%d_%d" % (i, dr), [128, 512]) for i in range(8)]
            s.st = b.sb("st_%d" % dr, [128, 32])
            s.h16 = {nm: b.sb("%s_%d" % (nm, dr), [128, 512], BF16) for nm in ("Rt", "Kt", "Bt", "Kp", "Kh", "Bh", "v16", "AV")}
            s.U = [b.sb("U%d_%d" % (c, dr), [128, 512], BF16) for c in range(2)]
            s.vz = [b.sb("vz%d_%d" % (c, dr), [128, 512], BF16) for c in range(2)]
            s.RTz = b.sb("RTz_%d" % dr, [128, 8, 128], BF16)
            for c in range(2):
                b.memset(s.U[c][:], 0.0)
                b.memset(s.vz[c][:], 0.0)
            b.memset(s.RTz[:], 0.0)
            s.Dg = [b.sb("Dg%d_%d" % (c, dr), [128, 512]) for c in range(2)]
            s.XT = b.sb("XT_%d" % dr, [128, 4, 4, 128], BF16)
            s.AM = b.sb("AM_%d" % dr, [128, 8, 4, 128], BF16)
            s.P = [b.sb("P%d_%d" % (i, dr), [128, 8, 128], BF16) for i in range(2)]
            s.PT = [b.sb("PT%d_%d" % (i, dr), [128, 8, 128], BF16) for i in range(2)]
            s.S = [b.sb("S%d_%d" % (i, dr), [128, 8, 128], BF16) for i in range(2)]
            s.WT = b.sb("WT_%d" % dr, [128, 8, 128], BF16)
            b.memset(s.WT[:], 0.0)
            s.H32 = [b.sb("H32_%d_%d" % (i, dr), [128, 4, 64]) for i in range(2)]
            s.H16 = [b.sb("H16_%d_%d" % (i, dr), [128, 4, 64], BF16) for i in range(2)]
            s.hi = 0
            s.Y = b.sb("Yt_%d" % dr, [128, 512])
            b.memset(s.H32[0][:], 0.0)
            b.memset(s.H16[0][:], 0.0)
            DS.append(s)

        def v3(view_tile, ap):
            return V(view_tile, ap.rearrange("p (h n) -> p h n", n=64))

        tcount = [0, 0]

        def load_rw(dr, ti, dst):
            rows = slice(ti * 128, (ti + 1) * 128)
            b.dma("sp", dst[:, 0:3, :], V(rwd, rwd.t[rows, 0:3, :]))
            b.dma("sp", dst[:, 3, :], V(rwd, rwd.t[rows, 3 + dr, :]))
            b.dma("sp", dst[:, 4, :], V(rwd, rwd.t[rows, 5 + dr, :]))

        def rw_tile(dr, ti):
            s = DS[dr]
            rw = s.rwb[tcount[dr] % 2]
            f = s.f
            h = s.h16
            st = s.st
            PY = PYS[dr]

            def gp():
                gcnt[dr] += 1
                return GS[dr][gcnt[dr] % 3]
            if tcount[dr] == 0:
                load_rw(dr, ti, rw)
            tn = ti + 1 if dr == 0 else ti - 1
            if 0 <= tn < NT:
                load_rw(dr, tn, s.rwb[(tcount[dr] + 1) % 2])
            tcount[dr] += 1
            yield
            r_, k_, v_ = rw[:, 0, :], rw[:, 1, :], rw[:, 2, :]
            b.tt(f[0][:], rw[:, 3, :], w0[dr][:], ALU.add)
            yield
            b.act(f[0][:], f[0][:], AF.Sigmoid)
            yield
            b.tt(f[1][:], rw[:, 4, :], a0[dr][:], ALU.add, eng="pool")
            yield
            b.act(f[1][:], f[1][:], AF.Sigmoid)
            yield
            b.tt(f[2][:], k_, kkb[:], ALU.mult)
            yield
            b.act(f[3][:], f[2][:], AF.Square)
            yield
            b.reduce(st[:, 0:8], v3(f[3], f[3].t[:]))
            yield
            b.act(st[:, 8:16], st[:, 0:8], AF.Sqrt)
            yield
            b.ts(st[:, 8:16], st[:, 8:16], 1e-12, ALU.max)
            yield
            b.recip(st[:, 16:24], st[:, 8:16])
            yield
            b.tt(v3(f[2], f[2].t[:]), v3(f[2], f[2].t[:]),
                 V(st, st.t[:, 16:24].unsqueeze(2).to_broadcast([128, 8, 64])), ALU.mult)
            yield
            b.tt(f[3][:], f[1][:], kab[:], ALU.mult, eng="pool")
            yield
            b.tt(f[3][:], f[3][:], omka[:], ALU.add, eng="pool")
            yield
            b.tt(f[3][:], f[3][:], k_, ALU.mult, eng="pool")
            yield
            b.stt(f[4][:], f[2][:], -1.0, f[1][:], ALU.mult, ALU.mult)
            yield
            b.copy(h["v16"][:], v_, eng="act")
            yield
            for c in range(2):
                b.copy(s.vz[c][c * 64:(c + 1) * 64, :], rw[c * 64:(c + 1) * 64, 2, :], eng="act")
                yield
            ci, ce, ca = CS[dr]
            pi = gp()
            b.mm(pi[:], tri[:, ci, :], f[0][:])
            yield
            b.act(f[5][:], pi[:], AF.Exp, scale=-CDEC)
            yield
            b.act(f[6][:], pi[:], AF.Exp, scale=CDEC)
            yield
            b.tt(h["Rt"][:], r_, f[5][:], ALU.mult)
            yield
            b.tt(h["Kt"][:], f[3][:], f[6][:], ALU.mult)
            yield
            b.tt(h["Bt"][:], f[4][:], f[6][:], ALU.mult)
            yield
            pe = gp()
            b.mm(pe[:], tri[:, ce, :], f[0][:])
            yield
            b.act(f[5][:], pe[:], AF.Exp, scale=-CDEC)
            yield
            b.tt(h["Kp"][:], f[2][:], f[5][:], ALU.mult)
            yield
            pa = gp()
            b.mm(pa[:], tri[:, ca, :], f[0][:])
            yield
            b.act(f[6][:], pa[:], AF.Exp, scale=-CDEC)
            yield
            b.tt(h["Kh"][:], f[3][:], f[6][:], ALU.mult, eng="pool")
            yield
            b.tt(h["Bh"][:], f[4][:], f[6][:], ALU.mult, eng="pool")
            yield
            p0 = gp()
            b.mm(p0[:], tri[:, 4, :], f[0][:])
            yield
            b.act(f[5][:], p0[:], AF.Exp, scale=-CDEC)
            yield
            b.tt(s.Dg[0][:], f[5][:], irep[:], ALU.mult, eng="pool")
            yield
            p1 = gp()
            b.mm(p1[:], tri[:, 5, :], f[0][:])
            yield
            b.act(f[7][:], p1[:], AF.Exp, scale=-CDEC)
            yield
            b.tt(s.Dg[1][:], f[7][:], irep[:], ALU.mult, eng="pool")
            yield
            for half in range(2):
                pt = gp()
                ptb = pt.t[:, 0:512].bitcast(BF16)
                for qi2 in range(2):
                    qi = half * 2 + qi2
                    src = h[("Kt", "Bt", "Kp", "Rt")[qi]]
                    for hp in range(4):
                        b.tr(V(pt, ptb[:, (qi2 * 4 + hp) * 128:(qi2 * 4 + hp + 1) * 128]), src[:, hp * 128:(hp + 1) * 128], ident[:])
                        yield
                for qi2 in range(2):
                    b.copy(V(s.XT, s.XT.t[:, :, half * 2 + qi2, :]),
                           V(pt, ptb[:, qi2 * 512:(qi2 + 1) * 512].rearrange("p (hp t) -> p hp t", hp=4)), eng=("act" if qi2 == 0 else "dve"))
                    yield
            b.copy(V(s.RTz, s.RTz.t[0:64, 0:8:2, :]), s.XT[0:64, :, 3, :], eng="act")
            yield
            b.copy(V(s.RTz, s.RTz.t[64:128, 1:8:2, :]), s.XT[64:128, :, 3, :], eng="act")
            yield
            if stage < 2:
                return
            pq = None
            for hd in range(8):
                hp, pb = hd // 2, (hd % 2) * 64
                p12 = gp()
                rhs = V(s.XT, s.XT.t[pb:pb + 64, hp, 2:4, :].rearrange("p a t -> p (a t)"))
                b.mm(p12[:, 0:256], s.XT[pb:pb + 64, hp, 0, :], rhs)
                yield
                b.mm(p12[:, 256:512], s.XT[pb:pb + 64, hp, 1, :], rhs)
                yield
                if stage >= 2.2:
                    b.tt(V(s.AM, s.AM.t[:, hd, :, :]), V(p12, p12.t[:].rearrange("p (a t) -> p a t", a=4)), M4[dr][:], ALU.mult)
                    yield
            for par in range(2):
                pq = gp()
                pb = par * 64
                for j in range(4):
                    hd = 2 * j + par
                    b.mm(pq[:, j * 128:(j + 1) * 128], s.XT[pb:pb + 64, j, 2, :], s.XT[pb:pb + 64, j, 1, :])
                    yield
                b.copy(s.f[7][:], pq[:], eng="act")
                yield
                b.tt(V(s.PT[0], s.PT[0].t[:, par:8:2, :]), V(s.f[7], s.f[7].t[:].rearrange("p (a t) -> p a t", a=4)),
                     MQ[dr][:], ALU.mult, eng="pool")
                yield
            if stage < 3:
                return
            b.copy(s.P[0][:], V(s.AM, s.AM.t[:, :, 2, :]), eng="act")
            yield
            b.tt(s.S[0][:], V(s.AM, s.AM.t[:, :, 2, :]), V(ident, ident.t[:].unsqueeze(1).to_broadcast([128, 8, 128])), ALU.add, eng="pool")
            yield
            cur = 0
            for lev in range(1, 6):
                nxt = 1 - cur
                for g4 in range(2):
                    hs = range(g4 * 4, g4 * 4 + 4)
                    if lev < 5:
                        p = gp()
                        for hd in hs:
                            b.mm(p[:, (hd % 4) * 128:(hd % 4 + 1) * 128], s.PT[cur][:, hd, :], s.P[cur][:, hd, :])
                            yield
                        b.copy(V(s.P[nxt], s.P[nxt].t[:, g4 * 4:g4 * 4 + 4, :]), V(p, p.t[:].rearrange("p (a t) -> p a t", a=4)), eng="act")
                        yield
                    p = gp()
                    for hd in hs:
                        b.mm(p[:, (hd % 4) * 128:(hd % 4 + 1) * 128], s.P[cur][:, hd, :], s.PT[cur][:, hd, :])
                        yield
                    b.copy(V(s.PT[nxt], s.PT[nxt].t[:, g4 * 4:g4 * 4 + 4, :]), V(p, p.t[:].rearrange("p (a t) -> p a t", a=4)),
                           eng=("act" if g4 == 0 else "dve"))
                    yield
                for g4 in range(2):
                    hs = range(g4 * 4, g4 * 4 + 4)
                    p = gp()
                    for hd in hs:
                        o = p[:, (hd % 4) * 128:(hd % 4 + 1) * 128]
                        b.mm(o, s.PT[nxt][:, hd, :], s.S[cur][:, hd, :])
                        yield
                    b.tt(V(s.S[nxt], s.S[nxt].t[:, g4 * 4:g4 * 4 + 4, :]), V(p, p.t[:].rearrange("p (a t) -> p a t", a=4)),
                         V(s.S[cur], s.S[cur].t[:, g4 * 4:g4 * 4 + 4, :]), ALU.add)
                    yield
                cur = nxt
            TT = s.S[cur]
            if stage < 4:
                return
            p = gp()
            for hd in range(8):
                b.mm(p[:, hd * 64:(hd + 1) * 64], s.AM[:, hd, 0, :], h["v16"][:, hd * 64:(hd + 1) * 64])
                yield
            b.copy(h["AV"][:], p[:], eng="act")
            yield
            p = gp()
            for hd in range(8):
                hp, pb = hd // 2, (hd % 2) * 64
                b.mm(p[pb:pb + 64, hp * 128:(hp + 1) * 128], h["Kp"][:, hd * 64:(hd + 1) * 64], TT[:, hd, :])
                yield
            b.copy(V(s.WT, s.WT.t[0:64, 0:8:2, :]), V(p, p.t[0:64, :].rearrange("p (a t) -> p a t", a=4)), eng="act")
            yield
            b.copy(V(s.WT, s.WT.t[64:128, 1:8:2, :]), V(p, p.t[64:128, :].rearrange("p (a t) -> p a t", a=4)), eng="act")
            yield
            if stage < 5:
                return
            if (dr == 0 and ti == NT // 2) or (dr == 1 and ti == NT // 2 - 1):
                b.ts(s.H32[s.hi][:], s.H32[s.hi][:], flag[:, 0:1], ALU.mult)
                yield
                b.ts(s.H16[s.hi][:], s.H16[s.hi][:], flag[:, 0:1], ALU.mult)
                yield
            for c in ((0, 1) if dr == 0 else (1, 0)):
                cb = c * 64
                cs = slice(cb, cb + 64)
                Ho32, Ho16 = s.H32[s.hi], s.H16[s.hi]
                Hn32, Hn16 = s.H32[1 - s.hi], s.H16[1 - s.hi]
                s.hi = 1 - s.hi
                PH = gp()
                for hd in range(8):
                    hp, pb = hd // 2, (hd % 2) * 64
                    hc = slice(hd * 64, (hd + 1) * 64)
                    o = PH[pb:pb + 64, hp * 64:(hp + 1) * 64]
                    b.mm(o, s.Dg[c][:, hc], Ho32[:, hp, :], start=(hd < 2), stop=False, skip_group_check=True)
                    yield
                    b.mm(o, h["Kh"][:, hc], s.vz[c][:, hc], start=False, stop=False, skip_group_check=True)
                    yield
                PU = gp()
                for hd in range(8):
                    hp = hd // 2
                    hc = slice(hd * 64, (hd + 1) * 64)
                    o = PU[cs, hc]
                    b.mm(o, TT[:, hd, cs], h["AV"][:, hc], start=True, stop=False, skip_group_check=True)
                    yield
                    b.mm(o, s.WT[:, hd, cs], Ho16[:, hp, :], start=False, stop=True, skip_group_check=True)
                    yield
                b.copy(s.U[c][cs, :], PU[cs, :], eng="act")
                yield
                for hd in range(8):
                    hp, pb = hd // 2, (hd % 2) * 64
                    hc = slice(hd * 64, (hd + 1) * 64)
                    o = PH[pb:pb + 64, hp * 64:(hp + 1) * 64]
                    b.mm(o, h["Bh"][:, hc], s.U[c][:, hc], start=False, stop=True, skip_group_check=True)
                    yield
                b.copy(V(Hn32, Hn32.t[:].rearrange("p a n -> p (a n)")), PH[:, 0:256], eng="act")
                yield
                b.copy(Hn16[:], Hn32[:], eng="pool")
                yield
                for hd in range(8):
                    hp = hd // 2
                    hc = slice(hd * 64, (hd + 1) * 64)
                    o = PY[cs, hc]
                    b.mm(o, s.RTz[:, hd, cs], Ho16[:, hp, :], start=True, stop=False, skip_group_check=True)
                    yield
                    b.mm(o, s.AM[:, hd, 1, cs], h["v16"][:, hc], start=False, stop=False, skip_group_check=True)
                    yield
                    b.mm(o, s.AM[:, hd, 3, cs], s.U[c][:, hc], start=False, stop=True, skip_group_check=True)
                    yield
            b.copy(s.Y[:], PY[:], eng="act")
            yield
            b.dma("sp", V(Yd, Yd.t[dr, ti * 128:(ti + 1) * 128, :], (dr, ti)), s.Y[:])
            yield

        nloop = NT if stage >= 9 else min(NT, NTL)

        def stream(dr):
            for i in range(nloop):
                yield from rw_tile(dr, i if dr == 0 else NT - 1 - i)
        alive = [stream(0), stream(1)]
        for _ in range(STAGGER):
            next(alive[0])
        while alive:
            for g_ in list(alive):
                try:
                    for _ in range(RW_BURST):
                        next(g_)
                except StopIteration:
                    alive.remove(g_)
    if stage < 9:
        return

    if not do_post:
        return
    with b.phase():
        gens = rwkv_post_setup(k, rwd, Yd, mixT0, ident, None, 4, "pool")
        while gens:
            for g_ in list(gens):
                try:
                    next(g_)
                except StopIteration:
                    gens.remove(g_)


def rwkv_post_setup(k, rwd, Yd, mixT0, ident, TP, NS, e2):
    b = k.b
    d = k.d
    NT = T // 128
    if True:
        def bc2(nm):
            t = b.sb("bc_" + nm, [128, 512])
            src = d[nm].t
            if len(src.shape) == 2:
                src = src.rearrange("h n -> (h n)")
            b.dma("sp", t[:], V(d[nm], src.partition_broadcast(128)))
            return t
        a0 = [bc2("a0_f"), bc2("a0_b")]
        kab = bc2("k_a")
        rkb = bc2("r_k")
        lw = bc2("lnx_w")
        lb = bc2("lnx_b")
        omka = b.sb("omka2", [128, 512])
        b.ts(omka[:], kab[:], -1.0, ALU.mult, 1.0, ALU.add)
        b.ts(kab[:], kab[:], 0.5, ALU.mult)
        rws = [b.sb("rwp%d" % i, [128, 8, 512]) for i in range(NS)]
        ys = [b.sb("yp%d" % i, [128, 2, 512]) for i in range(NS)]
        fs = [[b.sb("pf%d_%d" % (i, j), [128, 512]) for i in range(5)] for j in range(NS)]
        st_t = b.sb("pst", [128, 32 * NS])
        ob = [b.sb("pob%d" % i, [128, 512], BF16) for i in range(NS)]
        oT = [b.sb("poT%d" % i, [128, 4, 128], BF16) for i in range(NS)]
        PTt = [TP] * NS if TP is not None else [b.ps("PTp%d" % i, [128, 512]) for i in range(NS)]

        def v3(view_tile, ap):
            return V(view_tile, ap.rearrange("p (h n) -> p h n", n=64))

        def bc8(tile, c0, key):
            return V(tile, tile.t[:, c0:c0 + 8].unsqueeze(2).to_broadcast([128, 8, 64]), key)

        def post_tile(j, ti):
            rw = rws[j]
            yy = ys[j]
            f = fs[j]
            off = j * 32
            st = _Cols(st_t, off)
            b.dma("sp", rw[:, 0:3, :], V(rwd, rwd.t[ti * 128:(ti + 1) * 128, 0:3, :]))
            b.dma("sp", rw[:, 5:8, :], V(rwd, rwd.t[ti * 128:(ti + 1) * 128, 5:8, :]))
            for dr in range(2):
                b.dma("sp", yy[:, dr, :], V(Yd, Yd.t[dr, ti * 128:(ti + 1) * 128, :]))
            yield
            y = f[0]
            b.tt(y[:], yy[:, 0, :], yy[:, 1, :], ALU.add)
            yield
            b.reduce(st[:, 0:8], v3(y, y.t[:]))
            yield
            b.ts(st[:, 0:8], st[:, 0:8], 1.0 / 64, ALU.mult)
            yield
            b.tt(v3(y, y.t[:]), v3(y, y.t[:]), bc8(st_t, off, off), ALU.subtract)
            yield
            b.tt(f[1][:], y[:], y[:], ALU.mult)
            yield
            b.reduce(st[:, 8:16], v3(f[1], f[1].t[:]))
            yield
            b.act(st[:, 16:24], st[:, 8:16], AF.Sqrt, bias=64e-5, scale=1.0 / 64)
            yield
            b.recip(st[:, 24:32], st[:, 16:24])
            yield
            b.tt(v3(y, y.t[:]), v3(y, y.t[:]), bc8(st_t, off + 24, off), ALU.mult)
            yield
            b.tt(y[:], y[:], lw[:], ALU.mult, eng=e2)
            yield
            b.tt(y[:], y[:], lb[:], ALU.add, eng=e2)
            yield
            b.tt(f[2][:], rw[:, 5, :], a0[0][:], ALU.add, eng=e2)
            yield
            b.act(f[2][:], f[2][:], AF.Exp, scale=-1.0)
            yield
            b.ts(f[2][:], f[2][:], 1.0, ALU.add)
            yield
            b.recip(f[2][:], f[2][:])
            yield
            b.tt(f[3][:], rw[:, 6, :], a0[1][:], ALU.add, eng=e2)
            yield
            b.act(f[3][:], f[3][:], AF.Exp, scale=-1.0)
            yield
            b.ts(f[3][:], f[3][:], 1.0, ALU.add)
            yield
            b.recip(f[3][:], f[3][:])
            yield
            b.tt(f[2][:], f[2][:], f[3][:], ALU.add, eng=e2)
            yield
            b.tt(f[2][:], f[2][:], kab[:], ALU.mult, eng=e2)
            yield
            b.tt(f[2][:], f[2][:], omka[:], ALU.add, eng=e2)
            yield
            b.tt(f[2][:], f[2][:], rw[:, 1, :], ALU.mult)
            yield
            b.tt(f[2][:], f[2][:], rw[:, 0, :], ALU.mult)
            yield
            b.tt(f[2][:], f[2][:], rkb[:], ALU.mult)
            yield
            b.reduce(st[:, 8:16], v3(f[2], f[2].t[:]))
            yield
            b.tt(v3(f[4], f[4].t[:]), v3(rw, rw.t[:, 2, :]), bc8(st_t, off + 8, off), ALU.mult)
            yield
            b.tt(y[:], y[:], f[4][:], ALU.add)
            yield
            o = ob[j]
            b.tt(o[:], y[:], rw[:, 7, :], ALU.mult)
            yield
            pt = PTt[j]
            ptb = pt.t[:, 0:256].bitcast(BF16)
            for c in range(4):
                b.tr(V(pt, ptb[:, c * 128:(c + 1) * 128]), o[:, c * 128:(c + 1) * 128], ident[:])
            ot = oT[j]
            b.copy(ot[:], V(pt, ptb.rearrange("p (c t) -> p c t", c=4)), eng="act")
            yield
            for c in range(4):
                b.dma("sp", V(mixT0, mixT0.t[512 + c * 128:512 + (c + 1) * 128, ti * 128:(ti + 1) * 128], ("r", c, ti)), ot[:, c, :])
            yield

        def pstream(j):
            for ti in range(j, NT, NS):
                yield from post_tile(j, ti)
        return [pstream(j) for j in range(NS)]


def make_consts(is_prompt):
    c = {}
    p = np.arange(128)[:, None]
    f = np.arange(128)[None, :]
    c["c_ident"] = np.eye(128, dtype=np.float32)
    partner = (np.arange(128) // 64) * 64 + (np.arange(128) % 64 + 32) % 64
    perm = np.zeros((128, 128), np.float32)
    perm[partner, np.arange(128)] = 1.0
    c["c_perm"] = perm
    NEG = -30000.0
    IU = np.where(p <= f, 0.0, NEG).astype(np.float32)
    IL = np.where(p >= f, 0.0, NEG).astype(np.float32)
    c["c_maskAB"] = np.concatenate([IU, IL], axis=1)
    if is_prompt:
        c["c_maskX"] = c["c_maskAB"].copy()
    else:
        a = IU.copy(); a[64:, :] = NEG
        bb = IL.copy(); bb[:64, :] = NEG
        c["c_maskX"] = np.concatenate([a, bb], axis=1)
    T = 4096
    S = 4096 if is_prompt else 2048
    pos = (np.arange(T) % S).astype(np.float32)
    half = 32
    inv = (10000.0 ** (-np.arange(half, dtype=np.float32) / half)).astype(np.float32)
    ang = pos[None, :] * inv[:, None]
    cos = np.cos(ang).astype(np.float32)
    sin = np.sin(ang).astype(np.float32)
    c["c_cos"] = np.tile(cos, (4, 1))
    c["c_sin"] = np.concatenate([-sin, sin, -sin, sin], axis=0)
    flag = np.zeros((128, 4), np.float32)
    flag[:, 0] = 1.0 if is_prompt else 0.0
    flag[:, 1] = 0.0 if is_prompt else NEG
    c["c_flag"] = flag
    blk = (p // 64) == (f // 64)
    SU = (blk & (p < f)).astype(np.float32)
    IUb = (blk & (p <= f)).astype(np.float32)
    SL = (blk & (p > f)).astype(np.float32)
    ILb = (blk & (p >= f)).astype(np.float32)
    T0 = np.broadcast_to((p < 64), (128, 128)).astype(np.float32)
    T1 = np.broadcast_to((p >= 64), (128, 128)).astype(np.float32)
    c["c_tri"] = np.stack([SU, IUb, SL, ILb, T0, T1], axis=1).reshape(128, 6 * 128).astype(np.float32)
    kk = np.arange(512)[None, :] % 64
    hh = np.arange(512)[None, :] // 64
    c["c_irep"] = (((p % 64) == kk) & ((p // 64) == (hh % 2))).astype(np.float32)
    return c


INPUT_NAMES = None


def build_program(upto="all", skip=()):
    nc = bass.Bass("TRN2", target_bir_lowering=False)
    k = K(nc)
    b = k.b
    shapes = {
        "mix_pre0": [D], "mix_post0": [D], "w_in0": [D, 3072], "lam_q1": [64], "lam_k1": [64], "lam_q2": [64], "lam_k2": [64],
        "subln_w": [128], "mu_r": [512], "mu_k": [512], "mu_v": [512], "mu_w": [D], "mu_a": [D], "mu_g": [D],
        "w0_f": [512], "w1_f": [D, 64], "w2_f": [64, 512], "w0_b": [512], "w1_b": [D, 64], "w2_b": [64, 512],
        "a0_f": [512], "a1_f": [D, 64], "a2_f": [64, 512], "a0_b": [512], "a1_b": [D, 64], "a2_b": [64, 512],
        "g1": [D, 160], "g2": [160, 512], "k_k": [512], "k_a": [512], "r_k": [8, 64], "lnx_w": [512], "lnx_b": [512],
        "w_out0": [D, D], "ffn_pre0": [D], "ffn_post0": [D], "ffn_gate0": [D, FH], "ffn_up0": [D, FH], "ffn_down0": [FH, D],
        "mix_pre1": [D], "mix_post1": [D], "w_in1": [D, 2304], "w_out1": [768, D], "ffn_pre1": [D], "ffn_post1": [D],
        "ffn_gate1": [D, FH], "ffn_up1": [D, FH], "ffn_down1": [FH, D],
        "c_ident": [128, 128], "c_perm": [128, 128], "c_maskAB": [128, 256], "c_maskX": [128, 256], "c_cos": [128, T], "c_sin": [128, T],
        "c_flag": [128, 4], "c_tri": [128, 6, 128], "c_irep": [128, 512],
    }
    for n, sh in shapes.items():
        k.dram(n, sh, F32, kind="ExternalInput")
    x = k.dram("x", [T, D], F32, kind="ExternalInput")
    y = k.dram("y", [T, D], F32, kind="ExternalOutput")
    if upto != "all":
        k.ext_out = {upto}
    cst = {"ident": k.d["c_ident"][:], "perm": k.d["c_perm"][:], "flag": k.d["c_flag"][:], "cos": k.d["c_cos"][:], "sin": k.d["c_sin"][:],
           "maskAB": k.d["c_maskAB"][:], "maskX": k.d["c_maskX"][:], "tri": k.d["c_tri"][:], "irep": k.d["c_irep"][:], "eps": EPS}
    prep_weight_rows(k, "w_in0", 8, 3072)

    def late_casts():
        r = {}
        r["wout0"] = prep_weight_rows(k, "w_out0", 8, D)
        prep_weights_ffn(k, 0)
        prep_weight_rows(k, "w_in1", 8, 2304)
        r["wout1"] = prep_weight_rows(k, "w_out1", 6, D)
        prep_weights_ffn(k, 1)
        return r
    mixT0 = k.dram("mixT0", [1024, T], BF16)
    mixT1 = k.dram("mixT1", [768, T], BF16)
    xmid = k.dram("xmid", [T, D], F32)
    if upto != "all":
        y.t
    QTd = k.dram("QTd", [512, T], BF16)
    KTd = k.dram("KTd", [512, T], BF16)
    Vad = k.dram("Vad", [T, 4 * 129], BF16)
    rwd = k.dram("rwd", [T, 8, 512], F32)
    with b.phase():
        ident = b.sb("ident", [128, 128], BF16)
        b.dma("pool", ident[:], cst["ident"])
        flag = b.sb("flagm", [128, 4])
        b.dma("sp", flag[:], cst["flag"])
        with b.phase():
            xnT = b.sb("xnT0", [128, 8, XW], BF16)
            l0_norm(k, x, xnT, ident, cst)
            if "qkv" not in skip:
                l0_qkv(k, xnT, cst, QTd, KTd, Vad)
            if "rwproj" not in skip:
                l0_rwproj(k, xnT, cst, rwd)
        Yd = k.dram("Yd", [2, T, 512], F32)
        if "rwkv" not in skip:
            l0_rwkv(k, cst, rwd, mixT0, ident, Yd, flag, do_post=False)
        lc = late_casts()
        wout0_b, wout1_b = lc["wout0"], lc["wout1"]
        if "diff" not in skip:
            l0_diffattn(k, cst, QTd, KTd, Vad, mixT0, ident, flag,
                        co_setup=lambda TP: rwkv_post_setup(k, rwd, Yd, mixT0, ident, TP, 2, "dve"))
    if upto == "mixT0":
        return nc, list(shapes.keys())
    phase_post(k, 0, 8, mixT0, wout0_b, x, xmid, cst)
    if upto == "xmid":
        return nc, list(shapes.keys())
    phase_l1(k, xmid, mixT1, cst)
    if upto == "mixT1":
        return nc, list(shapes.keys())
    phase_post(k, 1, 6, mixT1, wout1_b, xmid, y, cst)
    return nc, list(shapes.keys())


def kernel(**inputs):
    n = 8
    xp = np.asarray(inputs["x_prompt"], dtype=np.float32)
    xs = np.asarray(inputs["x_sample"], dtype=np.float32)
    nc, names = build_program()
    cp = make_consts(True)
    cs = make_consts(False)
    in_maps = []
    for c in range(n):
        m = {}
        prompt = c < 4
        cc = cp if prompt else cs
        for nm in names:
            if nm.startswith("c_"):
                a = cc[nm]
                if nm == "c_tri":
                    a = a.reshape(128, 6, 128)
                m[nm] = np.ascontiguousarray(a, dtype=np.float32)
            else:
                m[nm] = np.ascontiguousarray(np.asarray(inputs[nm], dtype=np.float32))
        if prompt:
            m["x"] = np.ascontiguousarray(xp[c])
        else:
            j = 2 * (c - 4)
            m["x"] = np.ascontiguousarray(xs[j:j + 2].reshape(T, D))
        in_maps.append(m)
    res = run_bass_kernel_spmd(nc, in_maps, core_ids=list(range(n)))
    outs = [np.asarray(r["y"], dtype=np.float32) for r in res.results]
    y_prompt = np.stack(outs[0:4], axis=0)
    y_sample = np.concatenate([o.reshape(2, T // 2, D) for o in outs[4:8]], axis=0)
    return (y_prompt, y_sample)
```
